# Optimizing a Trainium2 kernel written in Bass

```python
import math
import jax, jax.numpy as jnp
from jax import lax
import numpy as np

D_MODEL = 1024
BATCH = 4
SEQ = 4096
DEPTH = 4
DEC_BATCH = 32
DEC_SEQ = 4
PAST_LEN = 8192
PAGE_SIZE = 128

N_HEADS_RET = 4
HD_RET = 128
D_RET = N_HEADS_RET * HD_RET
N_HEADS_ATT = 8
HD_ATT = 64
D_ATT = N_HEADS_ATT * HD_ATT
D_MIX = D_RET + D_ATT
D_IN = 4 * D_RET + 3 * D_ATT
SPLITS = (D_RET, 2 * D_RET, 3 * D_RET, 4 * D_RET, 4 * D_RET + D_ATT, 4 * D_RET + 2 * D_ATT)
DIL_PATTERNS = ((128, 1), (512, 4), (2048, 16))
WIN_MAX = 2048
RET_CHUNK = 128
D_FF = 2816
CONV_W = 3
ROPE_THETA = 10000.0
EPS = 1e-6

kernel_name = "hybrid_retention_dilated_attn_convffn_step"


def rms_norm(x, w):
    xf = x.astype(jnp.float32)
    y = xf * lax.rsqrt(jnp.mean(xf * xf, axis=-1, keepdims=True) + EPS)
    return (y * w.astype(jnp.float32)).astype(x.dtype)


def rotary(x, pos):
    half = x.shape[-1] // 2
    inv = ROPE_THETA ** (-jnp.arange(half, dtype=jnp.float32) / half)
    ang = pos.astype(jnp.float32)[:, None] * inv[None, :]
    cos = jnp.cos(ang)[None, :, None, :]
    sin = jnp.sin(ang)[None, :, None, :]
    xf = x.astype(jnp.float32)
    x1, x2 = xf[..., :half], xf[..., half:]
    return jnp.concatenate([x1 * cos - x2 * sin, x2 * cos + x1 * sin], axis=-1).astype(x.dtype)


def ret_log_decay():
    return jnp.log1p(-jnp.exp2(-5.0 - jnp.arange(N_HEADS_RET, dtype=jnp.float32)))


def retention(q, k, v, s0):
    C = q.shape[2]
    lg = ret_log_decay()
    idx = jnp.arange(C, dtype=jnp.float32)
    diff = idx[:, None] - idx[None, :]
    causal = diff >= 0
    dmask = jnp.where(causal, jnp.exp(jnp.where(causal, diff, 0.0)[None] * lg[:, None, None]), 0.0)
    k = k * HD_RET ** -0.5
    scores = jnp.einsum('bnihd,bnjhd->bnhij', q, k) * dmask
    inner = jnp.einsum('bnhij,bnjhe->bnihe', scores, v)
    kdec = k * jnp.exp((C - 1.0 - idx)[:, None] * lg[None, :])[None, None, :, :, None]
    upd = jnp.einsum('bnjhd,bnjhe->nbhde', kdec, v)
    chunk_decay = jnp.exp(C * lg)[None, :, None, None]

    def step(s, u):
        return chunk_decay * s + u, s

    s_final, s_before = lax.scan(step, s0, upd)
    qdec = q * jnp.exp((idx + 1.0)[:, None] * lg[None, :])[None, None, :, :, None]
    cross = jnp.einsum('bnihd,nbhde->bnihe', qdec, s_before)
    return inner + cross, s_final


def combine_by_denominator(outs, lses):
    w = jax.nn.softmax(jnp.stack(lses, axis=0), axis=0)
    return jnp.sum(w[..., None] * jnp.stack(outs, axis=0), axis=0)


def dilated_attn_prompt(q, k, v):
    B, S, H, hd = q.shape
    outs, lses = [], []
    for window, dil in DIL_PATTERNS:
        nb = window // dil
        span = dil * nb
        Sp = -(-S // span) * span
        nblk = Sp // span
        pad = ((0, 0), (0, Sp - S), (0, 0), (0, 0))

        def strided(t):
            t = jnp.pad(t.astype(jnp.float32), pad).reshape(B, Sp // dil, dil, H, hd)
            return t.transpose(0, 2, 1, 3, 4).reshape(B, dil, nblk, nb, H, hd)

        def with_prev(t):
            prev = jnp.pad(t, ((0, 0), (0, 0), (1, 0), (0, 0), (0, 0), (0, 0)))[:, :, :-1]
            return jnp.concatenate([prev, t], axis=3)

        qs = strided(q)
        kb, vb = with_prev(strided(k)), with_prev(strided(v))
        s = jnp.einsum('brnqhd,brnkhd->brnhqk', qs, kb) * hd ** -0.5
        qi = jnp.arange(nb)[:, None]
        ki = jnp.arange(2 * nb)[None, :] - nb
        dist = qi - ki
        blk = jnp.arange(nblk)[:, None, None]
        valid = (dist >= 0) & (dist <= nb) & (blk * nb + ki >= 0)
        s = jnp.where(valid[None, None, :, None], s, -jnp.inf)
        m = jnp.max(s, axis=-1, keepdims=True)
        p = jnp.exp(s - m)
        den = jnp.sum(p, axis=-1, keepdims=True)
        o = jnp.einsum('brnhqk,brnkhd->brnqhd', p / den, vb)
        lse = (m + jnp.log(den))[..., 0]
        o = o.reshape(B, dil, Sp // dil, H, hd).transpose(0, 2, 1, 3, 4).reshape(B, Sp, H, hd)[:, :S]
        lse = lse.transpose(0, 1, 2, 4, 3).reshape(B, dil, Sp // dil, H).transpose(0, 2, 1, 3).reshape(B, Sp, H)[:, :S]
        outs.append(o)
        lses.append(lse)
    return combine_by_denominator(outs, lses)


def dilated_attn_sample(q, k_all, v_all, n_ctx):
    B, T, H, hd = q.shape
    qf = q.astype(jnp.float32)
    kf, vf = k_all.astype(jnp.float32), v_all.astype(jnp.float32)
    outs, lses = [], []
    for window, dil in DIL_PATTERNS:
        n = window // dil + 1
        idx = n_ctx + jnp.arange(T)[:, None] - dil * jnp.arange(n)[None, :]
        valid = idx >= 0
        idx_c = jnp.clip(idx, 0, None)
        kg = jnp.take(kf, idx_c, axis=1)
        vg = jnp.take(vf, idx_c, axis=1)
        s = jnp.einsum('bthd,btnhd->btnh', qf, kg) * hd ** -0.5
        s = jnp.where(valid[None, :, :, None], s, -jnp.inf)
        m = jnp.max(s, axis=2, keepdims=True)
        p = jnp.exp(s - m)
        den = jnp.sum(p, axis=2, keepdims=True)
        outs.append(jnp.einsum('btnh,btnhd->bthd', p / den, vg))
        lses.append((m + jnp.log(den))[:, :, 0, :])
    return combine_by_denominator(outs, lses)


def project(h, w_in_l, pos):
    B, T, _ = h.shape
    z = h @ w_in_l
    qr, kr, vr, gr, qa, ka, va = jnp.split(z, SPLITS, axis=-1)
    hr = lambda t: t.reshape(B, T, N_HEADS_RET, HD_RET)
    ha = lambda t: t.reshape(B, T, N_HEADS_ATT, HD_ATT)
    qr, kr = rotary(hr(qr), pos), rotary(hr(kr), pos)
    qa, ka = rotary(ha(qa), pos), rotary(ha(ka), pos)
    return qr, kr, hr(vr), gr, qa, ka, ha(va)


def merge(o_ret, g_ret, o_att, gn_w_l, w_out_l, dtype):
    B, T = o_ret.shape[:2]
    mu = jnp.mean(o_ret, axis=-1, keepdims=True)
    var = jnp.mean(jnp.square(o_ret - mu), axis=-1, keepdims=True)
    y = ((o_ret - mu) * lax.rsqrt(var + EPS)).reshape(B, T, D_RET) * gn_w_l.astype(jnp.float32)
    y = jax.nn.silu(g_ret.astype(jnp.float32)) * y
    cat = jnp.concatenate([y.astype(dtype), o_att.reshape(B, T, D_ATT).astype(dtype)], axis=-1)
    return cat @ w_out_l


def token_mixer_prompt(h, w_in_l, gn_w_l, w_out_l):
    B, S, _ = h.shape
    pos = jnp.arange(S, dtype=jnp.int32)
    qr, kr, vr, gr, qa, ka, va = project(h, w_in_l, pos)
    C = RET_CHUNK
    chunk = lambda t: t.astype(jnp.float32).reshape(B, S // C, C, N_HEADS_RET, HD_RET)
    s0 = jnp.zeros((B, N_HEADS_RET, HD_RET, HD_RET), jnp.float32)
    o_r, s_fin = retention(chunk(qr), chunk(kr), chunk(vr), s0)
    o_r = o_r.reshape(B, S, N_HEADS_RET, HD_RET)
    o_a = dilated_attn_prompt(qa, ka, va)
    out = merge(o_r, gr, o_a, gn_w_l, w_out_l, h.dtype)
    w_keep = min(WIN_MAX, S)
    return out, ka[:, S - w_keep:], va[:, S - w_keep:], s_fin


def token_mixer_sample(h, ck, cv, s0, w_in_l, gn_w_l, w_out_l):
    B, T, _ = h.shape
    pos = PAST_LEN + jnp.arange(T, dtype=jnp.int32)
    qr, kr, vr, gr, qa, ka, va = project(h, w_in_l, pos)
    one = lambda t: t.astype(jnp.float32).reshape(B, 1, T, N_HEADS_RET, HD_RET)
    o_r, s_new = retention(one(qr), one(kr), one(vr), s0.astype(jnp.float32))
    o_r = o_r.reshape(B, T, N_HEADS_RET, HD_RET)
    k_all = jnp.concatenate([ck.astype(ka.dtype), ka], axis=1)
    v_all = jnp.concatenate([cv.astype(va.dtype), va], axis=1)
    o_a = dilated_attn_sample(qa, k_all, v_all, ck.shape[1])
    out = merge(o_r, gr, o_a, gn_w_l, w_out_l, h.dtype)
    return out, ka, va, s_new


def conv_ffn(h, ctx, w_up_l, conv_w_l, conv_b_l, w_down_l):
    u = h @ w_up_l
    T = u.shape[1]
    ext = jnp.concatenate([ctx.astype(u.dtype), u], axis=1)
    c = conv_b_l
    for j in range(CONV_W):
        c = c + ext[:, j:j + T] * conv_w_l[j]
    g, val = c[..., :D_FF], c[..., D_FF:]
    y = (jax.nn.silu(g) * val) @ w_down_l
    return y, ext[:, ext.shape[1] - (CONV_W - 1):]


def setup_inputs(seed: int = 0) -> dict:
    key = jax.random.key(seed)
    ks = jax.random.split(key, 20)
    f32 = jnp.float32
    nrm = lambda k, shape, scale: scale * jax.random.normal(k, shape, f32)
    w_ctx = min(WIN_MAX, PAST_LEN)
    return {
        "x_prompt": nrm(ks[0], (BATCH, SEQ, D_MODEL), 1.0),
        "x_sample": nrm(ks[1], (DEC_BATCH, DEC_SEQ, D_MODEL), 1.0),
        "cache_win_k": nrm(ks[2], (DEPTH, DEC_BATCH, w_ctx, N_HEADS_ATT, HD_ATT), 1.0),
        "cache_win_v": nrm(ks[3], (DEPTH, DEC_BATCH, w_ctx, N_HEADS_ATT, HD_ATT), 1.0),
        "state_ret": nrm(ks[4], (DEPTH, DEC_BATCH, N_HEADS_RET, HD_RET, HD_RET), 0.5),
        "state_conv": nrm(ks[5], (DEPTH, DEC_BATCH, CONV_W - 1, 2 * D_FF), 0.5),
        "norm1_w": 1.0 + nrm(ks[6], (DEPTH, D_MODEL), 0.02),
        "w_in": nrm(ks[7], (DEPTH, D_MODEL, D_IN), D_MODEL ** -0.5),
        "ret_gn_w": 1.0 + nrm(ks[8], (DEPTH, D_RET), 0.02),
        "w_out": nrm(ks[9], (DEPTH, D_MIX, D_MODEL), D_MIX ** -0.5),
        "norm2_w": 1.0 + nrm(ks[10], (DEPTH, D_MODEL), 0.02),
        "w_up": nrm(ks[11], (DEPTH, D_MODEL, 2 * D_FF), D_MODEL ** -0.5),
        "conv_w": nrm(ks[12], (DEPTH, CONV_W, 2 * D_FF), CONV_W ** -0.5),
        "conv_b": nrm(ks[13], (DEPTH, 2 * D_FF), 0.02),
        "w_down": nrm(ks[14], (DEPTH, D_FF, D_MODEL), D_FF ** -0.5),
        "final_norm_w": 1.0 + nrm(ks[15], (D_MODEL,), 0.02),
    }


def reference(x_prompt, x_sample, cache_win_k, cache_win_v, state_ret, state_conv,
              norm1_w, w_in, ret_gn_w, w_out, norm2_w, w_up, conv_w, conv_b, w_down, final_norm_w):
    xp, xs = x_prompt, x_sample
    B = xp.shape[0]
    pk, pv, pr, pc, sk, sv, sr, sc = [], [], [], [], [], [], [], []
    for l in range(DEPTH):
        mp, kp_, vp_, rp_ = token_mixer_prompt(rms_norm(xp, norm1_w[l]), w_in[l], ret_gn_w[l], w_out[l])
        xp = xp + mp
        ms, ks_, vs_, rs_ = token_mixer_sample(rms_norm(xs, norm1_w[l]), cache_win_k[l], cache_win_v[l],
                                              state_ret[l], w_in[l], ret_gn_w[l], w_out[l])
        xs = xs + ms
        zero_ctx = jnp.zeros((B, CONV_W - 1, 2 * D_FF), xp.dtype)
        fp, cp_ = conv_ffn(rms_norm(xp, norm2_w[l]), zero_ctx, w_up[l], conv_w[l], conv_b[l], w_down[l])
        xp = xp + fp
        fs, cs_ = conv_ffn(rms_norm(xs, norm2_w[l]), state_conv[l], w_up[l], conv_w[l], conv_b[l], w_down[l])
        xs = xs + fs
        pk.append(kp_); pv.append(vp_); pr.append(rp_); pc.append(cp_)
        sk.append(ks_); sv.append(vs_); sr.append(rs_); sc.append(cs_)
    y_prompt = rms_norm(xp, final_norm_w)
    y_sample = rms_norm(xs, final_norm_w)
    p_win_k, p_win_v = jnp.stack(pk), jnp.stack(pv)
    p_ret, p_conv = jnp.stack(pr), jnp.stack(pc)
    s_win_k, s_win_v = jnp.stack(sk), jnp.stack(sv)
    s_ret, s_conv = jnp.stack(sr), jnp.stack(sc)
    return (y_prompt, y_sample, p_win_k, p_win_v, p_ret, p_conv, s_win_k, s_win_v, s_ret, s_conv)
```

```python
import math
import numpy as np
import concourse.bass as bass
import concourse.mybir as mybir
from concourse.bass_utils import run_bass_kernel_spmd

F32 = mybir.dt.float32
BF16 = mybir.dt.bfloat16
AF = mybir.ActivationFunctionType
ALU = mybir.AluOpType

D = 1024
DIN = 3584
DFF = 2816
NL = 4
NT = 17
NPT = 16
TOK = NT * 128
EPS = 1e-6
GAM = [1.0 - 2.0 ** (-5 - h) for h in range(4)]
PAST = 8192
GR = 4480
VOFF = 2176
SOFF = 4352
FFN_GROUPS = [(0, 6), (6, 11), (11, 16)]


def srow(b, t):
    return 6 * b + 2 + t


class U:
    __slots__ = ("w", "r", "tag", "const")

    def __init__(self, const=False):
        self.w = []
        self.r = []
        self.tag = None
        self.const = const


class KB:
    def __init__(self, nc):
        self.nc = nc
        self.eng = {"pe": nc.tensor, "act": nc.scalar, "dve": nc.vector, "pool": nc.gpsimd, "sp": nc.sync}
        self.csem = {}
        self.ccnt = {}
        for e in ("pe", "act", "dve", "pool"):
            self.csem[e] = nc.alloc_semaphore(name="c_" + e)
            self.ccnt[e] = 0
        self.dsem = {"sp": [nc.alloc_semaphore(name="dsp%d" % i) for i in range(24)],
                     "pool": [nc.alloc_semaphore(name="dpl%d" % i) for i in range(16)]}
        self.dval = {"sp": [0] * 24, "pool": [0] * 16}
        self.dnext = {"sp": 0, "pool": 0}
        self.seen = {e: {} for e in self.eng}
        self.semobj = {}

    def _wait(self, e, evs):
        need = {}
        for (s, v) in evs:
            k = id(s)
            self.semobj[k] = s
            if need.get(k, 0) < v:
                need[k] = v
        sn = self.seen[e]
        for k, v in need.items():
            if sn.get(k, 0) < v:
                self.eng[e].wait_ge(self.semobj[k], v)
                sn[k] = v

    def _collect(self, r, w, wa, tag):
        evs = []
        for u in r:
            evs += u.w
        for u in w:
            evs += u.w
            evs += u.r
        for u in wa:
            evs += u.r
            if u.tag != tag:
                evs += u.w
        return evs

    def _commit(self, ev, r, w, wa, tag):
        for u in r:
            if not u.const:
                u.r.append(ev)
        for u in w:
            u.w = [ev]
            u.r = []
            u.tag = None
        for u in wa:
            if u.tag != tag:
                u.w = []
                u.tag = tag
            u.w.append(ev)
            u.r = []

    def op(self, e, fn, r=(), w=(), wa=(), tag=None):
        self._wait(e, self._collect(r, w, wa, tag))
        ins = fn(self.eng[e])
        self.ccnt[e] += 1
        ins.then_inc(self.csem[e], 1)
        self._commit((self.csem[e], self.ccnt[e]), r, w, wa, tag)

    def dma(self, e, out, in_, r=(), w=(), wa=(), tag=None, **kw):
        evs = self._collect(r, w, wa, tag)
        k = self.dnext[e]
        self.dnext[e] = (k + 1) % len(self.dsem[e])
        sem = self.dsem[e][k]
        if self.dval[e][k] > 0:
            evs.append((sem, self.dval[e][k]))
        self._wait(e, evs)
        self.dval[e][k] += 16
        self.eng[e].dma_start(out=out, in_=in_, **kw).then_inc(sem, 16)
        self._commit((sem, self.dval[e][k]), r, w, wa, tag)

    def finish(self):
        evs = []
        for e in ("sp", "pool"):
            for s, v in zip(self.dsem[e], self.dval[e]):
                if v > 0:
                    evs.append((s, v))
        for e in ("pe", "act", "dve", "pool"):
            if self.ccnt[e] > 0:
                evs.append((self.csem[e], self.ccnt[e]))
        for e in ("sp", "act", "dve", "pe", "pool"):
            self._wait(e, evs)


class Rot:
    def __init__(self, tensors):
        self.t = tensors
        self.u = [U() for _ in tensors]
        self.i = 0

    def next(self):
        k = self.i
        self.i = (k + 1) % len(self.t)
        return self.t[k], self.u[k]


def build(nl=NL, stop=99, dbg=False):
    nc = bass.Bass("TRN2", target_bir_lowering=False)
    kb = KB(nc)

    def din(name, shape, dt=F32):
        return nc.dram_tensor(name, list(shape), dt, kind="ExternalInput").ap()

    def dout(name, shape, dt=F32):
        return nc.dram_tensor(name, list(shape), dt, kind="ExternalOutput").ap()

    def dscr(name, shape, dt):
        return nc.dram_tensor(name, list(shape), dt)

    xin = din("xin", [TOK, D])
    w_in = din("w_in", [NL, D, DIN])
    w_out = din("w_out", [NL, D, D])
    w_up = din("w_up", [NL, D, 2 * DFF])
    w_down = din("w_down", [NL, DFF, D])
    n1b = din("n1b", [NL, 128, D])
    n2b = din("n2b", [NL, 128, D])
    fnb = din("fnb", [128, D])
    gnb = din("gnb", [NL, 128, 512])
    convT = din("convT", [NL, 128, 44, 4])
    sconv_in = din("sconv_in", [NL, 128, 44, 4, 2])
    sret_in = din("sret_in", [NL, 4, 4, 128, 128])
    ck = din("ck", [NL, 4, 2048, 512])
    cv = din("cv", [NL, 4, 2048, 512])
    rope_d = din("rope", [NT, 128, 192])
    decq_d = din("decq", [128, NT, 4])
    deck_d = din("deck", [128, NT, 4])
    decf_d = din("decf", [128, NPT, 4])
    mretp_d = din("mretp", [128, 4, 128])
    mrets_d = din("mrets", [128, 4, 128])
    rowmask_d = din("rowmask", [128, 4])
    amask_d = din("amask", [128, 3, 128])
    smask_d = din("smask", [128, 9, 4])
    flag_d = din("flag", [128, 1])

    y_o = dout("y", [TOK, D])
    pk_o = dout("pk", [NL, 2048, 512])
    pv_o = dout("pv", [NL, 2048, 512])
    sk_o = dout("sk", [NL, 128, 512])
    sv_o = dout("sv", [NL, 128, 512])
    pret_o = dout("pret", [NL, 128, 4, 128])
    pconv_o = dout("pconvT", [NL, 128, 44, 2])
    sret_o = dout("sret", [NL, 4, 128, 4, 128])
    sconv_o = dout("sconvT", [NL, 128, 44, 4, 2])
    if dbg:
        dbg_cat = dout("dbg_cat", [TOK, D], BF16)
        dbg_x = dout("dbg_x", [TOK, D])
        dbg_acc = dout("dbg_acc", [TOK, 520])

    zs = [dscr("zs%d" % g, [TOK, 512], BF16).ap() for g in range(5)]
    zs_u = [U() for _ in range(5)]
    gK = dscr("gK", [2176, 512], BF16)
    gKo = [dscr("gKo%d" % i, [2176, 512], BF16) for i in range(2)]
    rK = dscr("rK", [2176, 512], BF16)
    gV = dscr("gV", [2176, 520], BF16)
    gVo = [dscr("gVo%d" % i, [2176, 520], BF16) for i in range(2)]
    rV = dscr("rV", [2176, 520], BF16)
    rkv_u = U()
    gS = dscr("gS", [128, 512], BF16)
    gSo = dscr("gSo", [256, 512], BF16)
    gin_u = U()
    gout_u = U()
    gin2 = dscr("gin2", [2, D], F32)
    gout2 = dscr("gout2", [4, D], F32)
    gin2_u = U()
    gout2_u = U()
    acc = [dscr("acc%d" % i, [TOK, 520], F32).ap() for i in range(3)]
    acc_u = [U() for _ in range(3)]
    ccsem = nc.alloc_semaphore(name="ccsem")
    cc_cnt = [0]

    def sb(name, shape, dt):
        return nc.alloc_sbuf_tensor("sb_" + name, list(shape), dt)

    x = sb("x", [128, NT, D], F32)
    x_u = [U() for _ in range(NT)]
    BIGN = max(8 * TOK, 22 * 768)
    big = sb("big", [128, BIGN], BF16)
    hT = big[:, 0:8 * TOK].rearrange("p (k t) -> p k t", k=8)
    hT_u = [U() for _ in range(NT)]
    aT = big[:, 0:22 * 768].rearrange("p (c t) -> p c t", c=22)
    aT_u = U()
    xh = big[:, 0:2 * D].bitcast(F32)
    xh_u = U()
    arena = sb("arena", [128, 22 * D], BF16)
    wd = arena[:, :].rearrange("p (c n) -> p c n", c=22)
    wd_u = U()
    a_off = [0]
    arena_units = []

    def carve(shape, dt):
        n = 1
        for s_ in shape[1:]:
            n *= s_
        nb = n * (2 if dt == BF16 else 4) // 2
        nb = (nb + 15) // 16 * 16
        o = a_off[0]
        a_off[0] += nb
        assert a_off[0] <= 22 * D, a_off[0]
        ap = arena[:, o:o + nb]
        if dt == F32:
            ap = ap.bitcast(F32)
        ap = ap[:, 0:n]
        if len(shape) == 3:
            ap = ap.rearrange("p (a b) -> p a b", a=shape[1])
        elif len(shape) == 4:
            ap = ap.rearrange("p (a b c) -> p a b c", a=shape[1], b=shape[2])
        return ap

    def arot(n, shape, dt):
        r_ = Rot([carve(shape, dt) for _ in range(n)])
        arena_units.extend(r_.u)
        return r_

    def aunit():
        u = U()
        arena_units.append(u)
        return u

    ident = sb("ident", [128, 128], BF16)
    ident_u = U(const=True)
    nwb = [sb("nwb0", [128, D], F32)]
    nwb.append(nwb[0])
    nwb_u = [U()]
    nwb_u.append(nwb_u[0])
    cwt = sb("cwt", [128, 44, 4], F32)
    cwt_u = U()
    decq = sb("decq", [128, NT, 4], F32)
    deck = sb("deck", [128, NT, 4], F32)
    decf = sb("decf", [128, NPT, 4], F32)
    mretp = sb("mretp", [128, 4, 128], BF16)
    mrets = sb("mrets", [128, 4, 128], BF16)
    rowmask = sb("rowmask", [128, 4], F32)
    amask = sb("amask", [128, 3, 128], BF16)
    smask = sb("smask", [128, 9, 4], BF16)
    flag = sb("flag", [128, 1], F32)
    epsb = sb("epsb", [128, 1], F32)
    tab_u = U(const=True)
    carry = sb("carry", [128, 44, 2], F32)
    carry_u = U()
    uout_s = sb("uout_s", [128, 44, 4, 2], F32)
    uouts_u = U()
    ctx8 = sb("ctx8", [128, 44, 4, 2], F32)
    ctx8_u = U()
    aTs = sb("aTs", [128, 22, 24], BF16)
    aTs_u = U()
    hTh = sb("hTh", [128, 8, 2], BF16)
    hTh_u = U()
    h2Ts = sb("h2Ts", [128, 8, 24], BF16)
    h2Ts_u = U()

    def rot(name, n, shape, dt):
        return Rot([sb("%s%d" % (name, i), shape, dt) for i in range(n)])

    wbuf = rot("wbuf", 2, [128, 8, 512], BF16)
    wbufF = Rot([wbuf.t[0][:, :, 0:256], wbuf.t[0][:, :, 256:512], wbuf.t[1][:, :, 0:256], wbuf.t[1][:, :, 256:512]])
    t32 = rot("t32", 4, [128, 512], F32)
    t32b = t32
    tb = rot("tb", 3, [128, 512], BF16)
    hb = rot("hb", 2, [128, D], BF16)
    junk = sb("junk", [128, D], BF16)
    junk_u = U()
    sm = rot("sm", 6, [128, 32], F32)
    pes = rot("pes", 2, [128, 32], BF16)
    h2T = sb("h2T", [128, 8, 768], BF16)
    h2T_u = U()

    gnw = carve([128, 512], F32)
    gnw_u = aunit()
    Sst = carve([128, 4, 128], F32)
    Sst_u = aunit()
    Sb = carve([128, 4, 128], BF16)
    Sb_u = aunit()
    s0bp = arot(2, [128, 4, 128], BF16)
    qz = carve([128, 4, 4, 24], BF16)
    qz_u = aunit()
    tb2 = arot(4, [128, 512], BF16)
    ropet = arot(2, [128, 192], F32)
    v65 = arot(5, [128, 8, 65], BF16)
    qkT = arot(2, [128, 8, 128], BF16)
    catT = qkT
    catb = hb
    kTs = arot(3, [128, 4, 128], BF16)
    qTp = arot(2, [128, 4, 128], BF16)
    pm = arot(2, [128, 2, 1024], BF16)
    ob = arot(2, [128, 520], F32)
    accl = ob
    kz = arot(2, [128, 512], BF16)
    _pm0 = pm.t[0].rearrange("p t c -> p (t c)")
    _pm1 = pm.t[1].rearrange("p t c -> p (t c)")
    tb2x = Rot([_pm0[:, i * 512:(i + 1) * 512] for i in range(4)])
    catT2 = Rot([_pm1[:, i * 1024:(i + 1) * 1024].rearrange("p (k t) -> p k t", k=8) for i in range(2)])
    arena_units.extend(tb2x.u + catT2.u)
    sacc = carve([128, 520], F32)
    sacc_u = aunit()

    PA = nc.alloc_psum_tensor("PA", [128, 2, 512], F32)
    PB = nc.alloc_psum_tensor("PB", [128, 2, 512], F32)
    PC = nc.alloc_psum_tensor("PC", [128, 2, 512], F32)
    PT = [nc.alloc_psum_tensor("PT%d" % i, [128, 1024], BF16) for i in range(2)]
    PA_u = [U(), U()]
    PB_u = [U(), U()]
    PC_u = [U(), U()]
    PT_u = [U(), U()]
    pt_i = [0]
    pab = Rot([PA[:, 0, :], PA[:, 1, :], PB[:, 0, :], PB[:, 1, :]])
    pab.u = [PA_u[0], PA_u[1], PB_u[0], PB_u[1]]

    def next_pt():
        k = pt_i[0]
        pt_i[0] = 1 - k
        return PT[k], PT_u[k]

    def alias(frm, to):
        evr = []
        evw = []
        for u in frm:
            evr += u.r
            evw += u.w
        for u in to:
            u.r = u.r + evr
            u.w = u.w + evw

    kb.op("pool", lambda e: e.memset(ident[:], 1.0), w=[ident_u])
    kb.op("pool", lambda e: e.affine_select(out=ident[:], in_=ident[:], pattern=[[-1, 128]],
                                            compare_op=ALU.is_equal, fill=0.0, base=0, channel_multiplier=1),
          w=[ident_u])
    for t_, d_ in ((decq, decq_d), (deck, deck_d), (decf, decf_d), (rowmask, rowmask_d), (flag, flag_d)):
        kb.dma("sp", t_[:], d_, w=[tab_u])
    for t_, d_ in ((amask, amask_d), (smask, smask_d), (mretp, mretp_d), (mrets, mrets_d)):
        kb.dma("pool", t_[:], d_, w=[tab_u])
    for j in range(NT):
        kb.dma("sp", x[:, j, :], xin[j * 128:(j + 1) * 128, :], w=[x_u[j]])
    kb.op("dve", lambda e: e.memset(aTs[:], 0.0), w=[aTs_u])
    kb.op("dve", lambda e: e.memset(epsb[:], EPS), w=[tab_u])
    kb.op("dve", lambda e: e.memset(h2Ts[:], 0.0), w=[h2Ts_u])
    ones_t, ones_u = t32.next()
    kb.op("dve", lambda e: e.memset(ones_t[:], 1.0), w=[ones_u])
    kb.dma("sp", acc[0][2048:2176, 0:512], ones_t[:], r=[ones_u], wa=[acc_u[0]], tag="init")
    kb.dma("sp", acc[0][2048:2176, 8:520], ones_t[:], r=[ones_u], wa=[acc_u[0]], tag="init")

    def rmsnorm(xt_ap, xu, wt, wu, out_ap, out_kw, np_=128):
        st, su = sm.next()
        kb.op("act", lambda e: e.activation(out=junk[0:np_, :], in_=xt_ap, func=AF.Square,
                                            accum_out=st[0:np_, 0:1]), r=[xu], w=[junk_u, su])
        kb.op("act", lambda e: e.activation(out=st[0:np_, 1:2], in_=st[0:np_, 0:1], func=AF.Sqrt, scale=1.0 / D,
                                            bias=epsb[0:np_, 0:1]), r=[su, tab_u], w=[su])
        kb.op("dve", lambda e: e.reciprocal(out=st[0:np_, 2:3], in_=st[0:np_, 1:2]), r=[su], w=[su])
        kb.op("dve", lambda e: e.scalar_tensor_tensor(out=out_ap, in0=xt_ap, scalar=st[0:np_, 2:3],
                                                      in1=wt[0:np_, :], op0=ALU.mult, op1=ALU.mult),
              r=[xu, su, wu], **out_kw)

    def transpose8(src, src_u, dst_ap, dst_kw, ncols=128, eng="act"):
        pt, ptu = next_pt()

        def f(e):
            ins = None
            for k in range(8):
                ins = e.transpose(out=pt[:, k * 128:(k + 1) * 128], in_=src[:, k * 128:(k + 1) * 128],
                                  identity=ident[:])
            return ins
        kb.op("pe", f, r=[src_u, ident_u], w=[ptu])
        pv_ = pt[:, :].rearrange("p (k t) -> p k t", k=8)[:, :, 0:ncols]
        if eng == "act":
            kb.op("act", lambda e: e.copy(out=dst_ap, in_=pv_), r=[ptu], **dst_kw)
        else:
            kb.op("dve", lambda e: e.tensor_copy(out=dst_ap, in_=pv_), r=[ptu], **dst_kw)

    def rope_apply(zt, zu, j, half, H, out_ap, out_u):
        rt, ru = ropet.next()
        kb.dma("sp", rt[:], rope_d[j], w=[ru])
        c0 = 0 if half == 64 else 128
        cos = rt[:, c0:c0 + half]
        sin = rt[:, c0 + half:c0 + 2 * half]
        zv = zt.rearrange("p (h two d) -> p h two d", h=H, two=2)
        ov = out_ap.rearrange("p (h two d) -> p h two d", h=H, two=2)
        t1, t1u = t32.next()
        t1v = t1[:, :].rearrange("p (h two d) -> p h two d", h=H, two=2)
        m, mu = t32.next()
        mv = m[:, :].rearrange("p (two h d) -> p two h d", two=2, h=H)
        cosb = cos.unsqueeze(1).unsqueeze(1).broadcast_to([128, H, 2, half])
        sinb = sin.unsqueeze(1).broadcast_to([128, H, half])
        kb.op("dve", lambda e: e.tensor_tensor(out=t1v, in0=zv, in1=cosb, op=ALU.mult), r=[zu, ru], w=[t1u])
        kb.op("dve", lambda e: e.tensor_tensor(out=mv[:, 0], in0=zv[:, :, 1, :], in1=sinb, op=ALU.mult),
              r=[zu, ru], wa=[mu], tag="m")
        kb.op("dve", lambda e: e.tensor_tensor(out=mv[:, 1], in0=zv[:, :, 0, :], in1=sinb, op=ALU.mult),
              r=[zu, ru], wa=[mu], tag="m")
        kb.op("dve", lambda e: e.tensor_tensor(out=ov[:, :, 0, :], in0=t1v[:, :, 0, :], in1=mv[:, 0],
                                               op=ALU.subtract), r=[t1u, mu], wa=[out_u], tag="o")
        kb.op("dve", lambda e: e.tensor_tensor(out=ov[:, :, 1, :], in0=t1v[:, :, 1, :], in1=mv[:, 1],
                                               op=ALU.add), r=[t1u, mu], wa=[out_u], tag="o")

    def allgather(src, src_u, dst, dst_u):
        evs = kb._collect([src_u], [dst_u], (), None)
        kb._wait("pool", evs)
        cc_cnt[0] += 1
        nc.gpsimd.collective_compute("AllGather", ALU.bypass, replica_groups=[[0, 1], [2, 3], [4, 5], [6, 7]],
                                     ins=[src], outs=[dst]).then_inc(ccsem)
        kb._commit((ccsem, cc_cnt[0]), [src_u], [dst_u], (), None)

    ginK = gK.ap()
    ginV = gV.ap()
    ginS = gS.ap()
    goutK = rK.ap()
    goutV = rV.ap()
    goutS = gSo.ap()[0:128, :]

    for l in range(nl):
        alias([wd_u], arena_units)
        alias([aT_u, xh_u], hT_u)
        kb.dma("sp", nwb[0][:], n1b[l], w=[nwb_u[0]])
        kb.dma("sp", gnw, gnb[l], w=[gnw_u])
        kb.dma("sp", cwt[:], convT[l], w=[cwt_u])
        kb.dma("sp", ctx8[:], sconv_in[l], w=[ctx8_u])
        kb.op("dve", lambda e: e.memset(qz, 0.0), w=[qz_u])
        for i in range(len(v65.t)):
            kb.op("pool", lambda e, i=i: e.memset(v65.t[i], 1.0), w=[v65.u[i]])

        if stop == 0:
            kb.finish()
            return nc
        for j in range(NT):
            h_t, h_u = hb.next()
            rmsnorm(x[:, j, :], x_u[j], nwb[0], nwb_u[0], h_t[:], dict(w=[h_u]))
            transpose8(h_t, h_u, hT[:, :, j * 128:(j + 1) * 128], dict(w=[hT_u[j]]))

        if stop == 1:
            kb.finish()
            return nc
        w_in_v = w_in[l].rearrange("(k p) n -> p k n", p=128)
        def load_win(g):
            wt, wu = wbuf.next()
            for k2 in range(4):
                kb.dma("pool", wt[:, 2 * k2:2 * k2 + 2, :], w_in_v[:, 2 * k2:2 * k2 + 2, g * 512:(g + 1) * 512],
                       wa=[wu], tag="ld%d_%d" % (l, g))
            return wt, wu
        win_next = load_win(0)
        for g in range(7):
            wt, wu = win_next
            if g + 1 < 7:
                win_next = load_win(g + 1)
            for j in range(NT):
                ps, psu = pab.next()

                def f(e, ps=ps, wt=wt, j=j):
                    ins = None
                    for k in range(8):
                        ins = e.matmul(ps, lhsT=hT[:, k, j * 128:(j + 1) * 128], rhs=wt[:, k, :],
                                       start=(k == 0), stop=(k == 7))
                    return ins
                kb.op("pe", f, r=[hT_u[j], wu], w=[psu])
                rows = slice(j * 128, (j + 1) * 128)
                if g in (0, 1, 4):
                    o, ou = tb.next()
                    rope_apply(ps, psu, j, 64 if g < 4 else 32, 4 if g < 4 else 8, o[:, :], ou)
                    kb.dma("pool", zs[g][rows, :], o[:], r=[ou], wa=[zs_u[g]], tag="z%d" % l)
                elif g in (2, 3):
                    o, ou = tb.next()
                    kb.op("act", lambda e, o=o, ps=ps: e.copy(out=o[:], in_=ps), r=[psu], w=[ou])
                    kb.dma("pool", zs[g][rows, :], o[:], r=[ou], wa=[zs_u[g]], tag="z%d" % l)
                elif g == 5:
                    o, ou = t32.next()
                    rope_apply(ps, psu, j, 32, 8, o[:, :], ou)
                    if j < NPT:
                        kb.dma("pool", pk_o[l, rows, :], o[:], r=[ou])
                    else:
                        kb.dma("pool", sk_o[l], o[:], r=[ou])
                    ob_, obu = tb.next()
                    kb.op("act", lambda e, ob_=ob_, o=o: e.copy(out=ob_[:], in_=o[:]), r=[ou], w=[obu])
                    kb.dma("pool", ginK[rows, :], ob_[:], r=[obu], wa=[gin_u], tag="g%d" % l)
                else:
                    z, zu = t32.next()
                    kb.op("act", lambda e, z=z, ps=ps: e.copy(out=z[:], in_=ps), r=[psu], w=[zu])
                    if j < NPT:
                        kb.dma("pool", pv_o[l, rows, :], z[:], r=[zu])
                    else:
                        kb.dma("pool", sv_o[l], z[:], r=[zu])
                    vt, vu = v65.next()
                    kb.op("dve", lambda e, vt=vt, z=z: e.tensor_copy(
                        out=vt[:, :, 0:64], in_=z[:, :].rearrange("p (h d) -> p h d", h=8)), r=[zu], w=[vu])
                    kb.dma("pool", ginV[rows, :], vt.rearrange("p h d -> p (h d)"), r=[vu],
                           wa=[gin_u], tag="g%d" % l)

        if stop == 2:
            kb.finish()
            return nc
        for j in range(NPT):
            rows = slice(j * 128, (j + 1) * 128)
            kt, ku = tb.next()
            vt_, vu_ = tb2.next()
            kb.dma("sp", kt[:], zs[1][rows, :], r=[zs_u[1]], w=[ku])
            kb.dma("sp", vt_, zs[2][rows, :], r=[zs_u[2]], w=[vu_])
            kd, kdu = kz.next()
            kb.op("dve", lambda e, kd=kd, kt=kt, j=j: e.tensor_tensor(
                out=kd.rearrange("p (h d) -> p h d", h=4), in0=kt[:, :].rearrange("p (h d) -> p h d", h=4),
                in1=decf[:, j, :].unsqueeze(2).broadcast_to([128, 4, 128]), op=ALU.mult), r=[ku, tab_u], w=[kdu])

            def f(e, kd=kd, vt_=vt_, j=j):
                ins = None
                for h in range(4):
                    ins = e.matmul(PC[:, 0, h * 128:(h + 1) * 128], lhsT=kd[:, h * 128:(h + 1) * 128],
                                   rhs=vt_[:, h * 128:(h + 1) * 128], start=(j == 0), stop=(j == NPT - 1),
                                   skip_group_check=True)
                return ins
            if j == 0:
                kb.op("pe", f, r=[kdu, vu_], w=[PC_u[0]])
            else:
                kb.op("pe", f, r=[kdu, vu_], wa=[PC_u[0]], tag=None)
        sl, slu = tb.next()
        kb.op("act", lambda e: e.copy(out=sl[:], in_=PC[:, 0, :]), r=[PC_u[0]], w=[slu])
        kb.dma("pool", ginS, sl[:], r=[slu], wa=[gin_u], tag="g%d" % l)

        for hh in range(2):
            allgather(gK.ap()[hh * 1088:(hh + 1) * 1088, :], gin_u, gKo[hh].ap(), gout_u)
            allgather(gV.ap()[hh * 1088:(hh + 1) * 1088, :], gin_u, gVo[hh].ap(), gout_u)
        allgather(gS.ap(), gin_u, gSo.ap(), gout_u)
        for hh in range(2):
            kb.dma("pool", rK.ap()[hh * 1088:(hh + 1) * 1088, :], gKo[hh].ap()[0:1088, :], r=[gout_u],
                   wa=[rkv_u], tag="rkv%d" % l)
            kb.dma("pool", rV.ap()[hh * 1088:(hh + 1) * 1088, :], gVo[hh].ap()[0:1088, :], r=[gout_u],
                   wa=[rkv_u], tag="rkv%d" % l)

        if stop == 3:
            kb.finish()
            return nc
        wo = []
        w_out_v = w_out[l].rearrange("(k p) n -> p k n", p=128)
        for hf in range(2):
            wt, wu = wbuf.next()
            for k2 in range(4):
                kb.dma("pool", wt[:, 2 * k2:2 * k2 + 2, :], w_out_v[:, 2 * k2:2 * k2 + 2, hf * 512:(hf + 1) * 512],
                       wa=[wu], tag="wo%d" % l)
            wo.append((wt, wu))

        def load_kv(rows_ap_k, rows_ap_v, srcs_u, cast=False):
            kt, ku = tb2.next()
            q = "pool" if cast else "sp"
            kb.dma(q, kt, rows_ap_k, r=srcs_u, w=[ku])
            vt, vu = v65.next()
            if cast:
                kb.dma(q, vt[:, :, 0:64], rows_ap_v.rearrange("p (h d) -> p h d", h=8), r=srcs_u, w=[vu])
            else:
                kb.dma(q, vt.rearrange("p h d -> p (h d)"), rows_ap_v, r=srcs_u, w=[vu])
            pt, ptu = next_pt()

            def f(e):
                ins = None
                for k in range(4):
                    ins = e.transpose(out=pt[:, k * 128:(k + 1) * 128], in_=kt[:, k * 128:(k + 1) * 128],
                                      identity=ident[:])
                return ins
            kb.op("pe", f, r=[ku, ident_u], w=[ptu])
            kT, kTu = kTs.next()
            kb.op("act", lambda e: e.copy(out=kT, in_=pt[:, 0:512].rearrange("p (k t) -> p k t", k=4)),
                  r=[ptu], w=[kTu])
            return (kT, kTu, vt, vu)

        def load_q(rows_ap):
            qt, qu = tb2.next()
            kb.dma("sp", qt, rows_ap, r=[zs_u[4]], w=[qu])
            pt, ptu = next_pt()

            def f(e):
                ins = None
                for k in range(4):
                    ins = e.transpose(out=pt[:, k * 128:(k + 1) * 128], in_=qt[:, k * 128:(k + 1) * 128],
                                      identity=ident[:])
                return ins
            kb.op("pe", f, r=[qu, ident_u], w=[ptu])
            qT, qTu = qTp.next()
            kb.op("dve", lambda e: e.tensor_copy(out=qT, in_=pt[:, 0:512].rearrange("p (k t) -> p k t", k=4)),
                  r=[ptu], w=[qTu])
            return qT, qTu

        def scores_exp(kT, kTu, qT, qTu, ncol, P, P_u, mask_ap, pm_ap, pm_kw, q0=0, meng="pool"):
            def f(e):
                ins = None
                for hp in range(4):
                    for ee in range(2):
                        ins = e.matmul(P[:, ee, hp * ncol:(hp + 1) * ncol],
                                       lhsT=kT[64 * ee:64 * ee + 64, hp, :],
                                       rhs=qT[64 * ee:64 * ee + 64, hp, q0:q0 + ncol], start=True, stop=True,
                                       skip_group_check=True)
                return ins
            kb.op("pe", f, r=[kTu, qTu], w=P_u)
            pev = pm_ap.rearrange("p (e c) -> p e c", e=2)
            for ee in range(2):
                kb.op("act", lambda e, ee=ee: e.activation(out=pev[:, ee, :], in_=P[:, ee, 0:4 * ncol],
                                                           func=AF.Exp, scale=0.125),
                      r=[P_u[ee]], **pm_kw)
            pg = pm_ap.rearrange("p (g c) -> p g c", g=8)
            kb.op(meng, lambda e: e.tensor_tensor(
                out=pg, in0=pg, in1=mask_ap.unsqueeze(1).broadcast_to([128, 8, ncol]), op=ALU.mult),
                r=[tab_u], w=[pm_kw["wa"][0]])

        alias(tb2x.u + catT2.u, pm.u)
        blocks = []
        for pi, dil in enumerate((1, 4, 16)):
            span = 128 * dil
            nblk = 2048 // span
            for r_ in range(dil):
                for n in range(nblk):
                    blocks.append((pi, dil, span, nblk, r_, n))
        st_prev = [None]

        def stage_a(blk):
            pi, dil, span, nblk, r_, n = blk
            r0 = n * span + r_
            rs = slice(r0, r0 + 128 * dil, dil) if dil > 1 else slice(r0, r0 + 128)
            if n == 0:
                p0 = (nblk - 1) * span + r_
                ps_ = slice(p0, p0 + 128 * dil, dil) if dil > 1 else slice(p0, p0 + 128)
                prev = load_kv(goutK[ps_, :], goutV[ps_, :], [rkv_u])
            else:
                prev = st_prev[0]
            cur = load_kv(ginK[rs, :], ginV[rs, :], [gin_u])
            qT, qTu = load_q(zs[4][rs, :])
            pmt, pmu = pm.next()
            tg = "pm%d_%d_%d_%d" % (l, pi, r_, n)
            scores_exp(prev[0], prev[1], qT, qTu, 128, PA, PA_u, amask[:, 2 if n == 0 else 1, :],
                       pmt[:, 0, :], dict(wa=[pmu], tag=tg), meng="dve")
            scores_exp(cur[0], cur[1], qT, qTu, 128, PB, PB_u, amask[:, 0, :],
                       pmt[:, 1, :], dict(wa=[pmu], tag=tg), meng="pool")
            st_prev[0] = cur
            return (pi, rs, prev, cur, pmt, pmu)

        def stage_b(st):
            pi, rs, prev, cur, pmt, pmu = st
            pmv = pmt.rearrange("p t (g c) -> p t g c", g=8)

            def f(e):
                ins = None
                for hp in range(4):
                    for ee in range(2):
                        h = 2 * hp + ee
                        g_ = ee * 4 + hp
                        o = PC[:, ee, hp * 65:(hp + 1) * 65]
                        e.matmul(o, lhsT=pmv[:, 0, g_, :], rhs=prev[2][:, h, :], start=True, stop=False)
                        ins = e.matmul(o, lhsT=pmv[:, 1, g_, :], rhs=cur[2][:, h, :], start=False, stop=True)
                return ins
            kb.op("pe", f, r=[pmu, prev[3], cur[3]], w=PC_u)
            o_, ou_ = ob.next()
            kb.op("act", lambda e: e.copy(out=o_.rearrange("p (e c) -> p e c", e=2), in_=PC[:, :, 0:260]),
                  r=PC_u, w=[ou_])
            kb.dma("pool", acc[pi][rs, :], o_, r=[ou_], wa=[acc_u[pi]], tag="acc%d" % l)

        pend = stage_a(blocks[0])
        for bi in range(len(blocks)):
            nxt = stage_a(blocks[bi + 1]) if bi + 1 < len(blocks) else None
            stage_b(pend)
            pend = nxt

        if stop == 4:
            kb.finish()
            return nc
        sq_rows = slice(2048, 2176)
        qTs, qTsu = load_q(zs[4][sq_rows, :])
        for b in range(4):
            specs = [(slice(1920, 2048), 0)]
            specs += [(slice(1536 + r_, 2048, 4), 1 + r_) for r_ in range(4)]
            specs += [(slice(r_, 2048, 16), 1 + r_) for r_ in range(4)]
            specs += [(None, 5 + b)]
            for ti, (rsl, mi) in enumerate(specs):
                if rsl is None:
                    kv = load_kv(ginK[sq_rows, :], ginV[sq_rows, :], [gin_u])
                else:
                    kv = load_kv(ck[l, b, rsl, :], cv[l, b, rsl, :], [], cast=True)
                pe_, peu = pes.next()
                scores_exp(kv[0], kv[1], qTs, qTsu, 4, PA, PA_u, smask[:, mi, :],
                           pe_[:, :], dict(wa=[peu], tag="s%d_%d_%d" % (l, b, ti)), q0=srow(b, 0), meng="dve")

                def f(e, kv=kv, pe_=pe_):
                    ins = None
                    for hp in range(4):
                        for ee in range(2):
                            h = 2 * hp + ee
                            g_ = ee * 4 + hp
                            ins = e.matmul(PC[0:4, ee, hp * 65:(hp + 1) * 65], lhsT=pe_[:, g_ * 4:(g_ + 1) * 4],
                                           rhs=kv[2][:, h, :], start=True, stop=True, skip_group_check=True)
                    return ins
                kb.op("pe", f, r=[peu, kv[3]], w=PC_u)
                sv_ = sacc[0:4, :].rearrange("p (e c) -> p e c", e=2)
                if ti == 0:
                    kb.op("dve", lambda e, sv_=sv_: e.tensor_copy(out=sv_, in_=PC[0:4, :, 0:260]), r=PC_u, w=[sacc_u])
                else:
                    kb.op("dve", lambda e, sv_=sv_: e.tensor_tensor(out=sv_, in0=PC[0:4, :, 0:260], in1=sv_, op=ALU.add),
                          r=PC_u, w=[sacc_u])
            kb.dma("pool", acc[0][2048 + srow(b, 0):2048 + srow(b, 0) + 4, :], sacc[0:4, :], r=[sacc_u],
                   wa=[acc_u[0]], tag="acc%d" % l)

        if stop == 5:
            kb.finish()
            return nc
        s0t, s0u = tb.next()
        kb.dma("sp", s0t[:], goutS, r=[gout_u], w=[s0u])
        kb.op("dve", lambda e: e.tensor_scalar(out=Sst.rearrange("p h d -> p (h d)"), in0=s0t[:],
                                               scalar1=flag[:, 0:1], scalar2=None, op0=ALU.mult),
              r=[s0u, tab_u], w=[Sst_u])
        kb.op("act", lambda e: e.copy(out=Sb, in_=Sst), r=[Sst_u], w=[Sb_u])

        alias(pm.u, tb2x.u + catT2.u)

        def r2_a(j, tset):
            samp = (j == NPT)
            rows = slice(j * 128, (j + 1) * 128)
            qt, qu = tset.next()
            kt, ku = tset.next()
            vt_, vu_ = tset.next()
            gt, gu = tset.next()
            kb.dma("sp", qt, zs[0][rows, :], r=[zs_u[0]], w=[qu])
            kb.dma("sp", kt, zs[1][rows, :], r=[zs_u[1]], w=[ku])
            kb.dma("sp", vt_, zs[2][rows, :], r=[zs_u[2]], w=[vu_])
            kb.dma("sp", gt, zs[3][rows, :], r=[zs_u[3]], w=[gu])
            qk, qku = hb.next()
            kb.op("dve", lambda e, qk=qk, qt=qt, j=j: e.tensor_tensor(
                out=qk[:, 0:512].rearrange("p (h d) -> p h d", h=4), in0=qt.rearrange("p (h d) -> p h d", h=4),
                in1=decq[:, j, :].unsqueeze(2).broadcast_to([128, 4, 128]), op=ALU.mult),
                r=[qu, tab_u], wa=[qku], tag="qk%d_%d" % (l, j))
            kb.op("pool", lambda e, qk=qk, kt=kt: e.tensor_copy(out=qk[:, 512:1024], in_=kt), r=[ku],
                  wa=[qku], tag="qk%d_%d" % (l, j))
            kd, kdu = kz.next()
            kb.op("pool", lambda e, kd=kd, kt=kt, j=j: e.tensor_tensor(
                out=kd.rearrange("p (h d) -> p h d", h=4), in0=kt.rearrange("p (h d) -> p h d", h=4),
                in1=deck[:, j, :].unsqueeze(2).broadcast_to([128, 4, 128]), op=ALU.mult), r=[ku, tab_u], w=[kdu])
            qT, qTu = qkT.next()
            transpose8(qk, qku, qT, dict(w=[qTu]), eng="act")
            ps, psu = pab.next()

            def f(e, ps=ps, qT=qT):
                ins = None
                for h in range(4):
                    ins = e.matmul(ps[:, h * 128:(h + 1) * 128], lhsT=qT[:, 4 + h, :], rhs=qT[:, h, :],
                                   start=True, stop=True, skip_group_check=True)
                return ins
            kb.op("pe", f, r=[qTu], w=[psu])
            pr, pru = tb.next()
            mk = mrets if samp else mretp
            kb.op("dve", lambda e, pr=pr, ps=ps, mk=mk: e.tensor_tensor(
                out=pr[:, :], in0=ps, in1=mk[:, :, :].rearrange("p h i -> p (h i)"), op=ALU.mult),
                r=[psu, tab_u], w=[pru])
            if samp:
                for b in range(4):
                    kb.op("dve", lambda e, b=b, qT=qT: e.tensor_copy(
                        out=qz[:, b, :, srow(b, 0):srow(b, 0) + 4], in_=qT[:, 0:4, srow(b, 0):srow(b, 0) + 4]),
                        r=[qTu], w=[qz_u])
            return dict(j=j, samp=samp, rows=rows, vt_=vt_, vu_=vu_, gt=gt, gu=gu, kd=kd, kdu=kdu, qT=qT, qTu=qTu,
                        pr=pr, pru=pru)

        def r2_b(c):
            j, samp, rows, vt_, vu_, gt, gu = c["j"], c["samp"], c["rows"], c["vt_"], c["vu_"], c["gt"], c["gu"]
            kd, kdu, qT, qTu, pr, pru = c["kd"], c["kdu"], c["qT"], c["qTu"], c["pr"], c["pru"]
            def state_update(kd=kd, kdu=kdu, vt_=vt_, vu_=vu_, j=j, samp=samp):
                if not samp:
                    def f(e, kd=kd, vt_=vt_):
                        ins = None
                        for h in range(4):
                            ins = e.matmul(PC[:, 0, h * 128:(h + 1) * 128], lhsT=kd[:, h * 128:(h + 1) * 128],
                                           rhs=vt_[:, h * 128:(h + 1) * 128], start=True, stop=True, skip_group_check=True)
                        return ins
                    kb.op("pe", f, r=[kdu, vu_], w=[PC_u[0]])
                    for h in range(4):
                        kb.op("dve", lambda e, h=h: e.scalar_tensor_tensor(
                            out=Sst[:, h, :], in0=Sst[:, h, :], scalar=float(GAM[h] ** 128),
                            in1=PC[:, 0, h * 128:(h + 1) * 128], op0=ALU.mult, op1=ALU.add), r=[PC_u[0]], w=[Sst_u])
                    kb.op("act", lambda e: e.copy(out=Sb, in_=Sst), r=[Sst_u], w=[Sb_u])
                    if j == NPT - 1:
                        kb.dma("pool", pret_o[l], Sst, r=[Sst_u])
                else:
                    for b in range(4):
                        kzb_, kzu = hb.next()
                        kzb = kzb_[:, 0:512]
                        kb.op("dve", lambda e, kzb=kzb, kd=kd, b=b: e.tensor_scalar(
                            out=kzb, in0=kd, scalar1=rowmask[:, b:b + 1], scalar2=None, op0=ALU.mult),
                            r=[kdu, tab_u], w=[kzu])

                        def f(e, kzb=kzb, vt_=vt_):
                            ins = None
                            for h in range(4):
                                ins = e.matmul(PC[:, 0, h * 128:(h + 1) * 128], lhsT=kzb[:, h * 128:(h + 1) * 128],
                                               rhs=vt_[:, h * 128:(h + 1) * 128], start=True, stop=True,
                                               skip_group_check=True)
                            return ins
                        kb.op("pe", f, r=[kzu, vu_], w=[PC_u[0]])
                        s0f, s0fu = t32.next()
                        kb.dma("sp", s0f[:, :].rearrange("p (h e) -> p h e", h=4),
                               sret_in[l, b].rearrange("h d e -> d h e"), w=[s0fu])
                        for h in range(4):
                            kb.op("dve", lambda e, h=h, s0f=s0f: e.scalar_tensor_tensor(
                                out=s0f[:, h * 128:(h + 1) * 128], in0=s0f[:, h * 128:(h + 1) * 128],
                                scalar=float(GAM[h] ** 4), in1=PC[:, 0, h * 128:(h + 1) * 128], op0=ALU.mult, op1=ALU.add),
                                r=[PC_u[0]], w=[s0fu])
                        kb.dma("pool", sret_o[l, b], s0f[:, :].rearrange("p (h e) -> p h e", h=4), r=[s0fu])

            if samp:
                state_update()
            po, pou = pab.next()
            if not samp:
                def f(e, po=po, pr=pr, vt_=vt_, qT=qT):
                    ins = None
                    for h in range(4):
                        o = po[:, h * 128:(h + 1) * 128]
                        e.matmul(o, lhsT=pr[:, h * 128:(h + 1) * 128], rhs=vt_[:, h * 128:(h + 1) * 128],
                                 start=True, stop=False)
                        ins = e.matmul(o, lhsT=qT[:, h, :], rhs=Sb[:, h, :], start=False, stop=True)
                    return ins
                kb.op("pe", f, r=[pru, vu_, qTu, Sb_u], w=[pou])
            else:
                def f(e, po=po, pr=pr, vt_=vt_):
                    ins = None
                    for h in range(4):
                        ins = e.matmul(po[:, h * 128:(h + 1) * 128], lhsT=pr[:, h * 128:(h + 1) * 128],
                                       rhs=vt_[:, h * 128:(h + 1) * 128], start=True, stop=True, skip_group_check=True)
                    return ins
                kb.op("pe", f, r=[pru, vu_], w=[pou])
                oacc, oaccu = t32.next()
                kb.op("act", lambda e, oacc=oacc, po=po: e.copy(out=oacc[:], in_=po), r=[pou], w=[oaccu])
                for b in range(4):
                    sbt, sbu = s0bp.next()
                    kb.dma("pool", sbt, sret_in[l, b].rearrange("h d e -> d h e"), w=[sbu])
                    pq, pqu = pab.next()

                    def f(e, pq=pq, b=b, sbt=sbt):
                        ins = None
                        for h in range(4):
                            ins = e.matmul(pq[0:24, h * 128:(h + 1) * 128], lhsT=qz[:, b, h, :], rhs=sbt[:, h, :],
                                           start=True, stop=True, skip_group_check=True)
                        return ins
                    kb.op("pe", f, r=[qz_u, sbu], w=[pqu])
                    kb.op("dve", lambda e, pq=pq, oacc=oacc: e.tensor_tensor(out=oacc[0:24, :], in0=pq[0:24, :],
                                                                             in1=oacc[0:24, :], op=ALU.add),
                          r=[pqu], w=[oaccu])
            if not samp:
                state_update()
            if samp:
                osb, osu = oacc, oaccu
            else:
                osb, osu = t32.next()
                kb.op("act", lambda e, osb=osb, po=po: e.copy(out=osb[:], in_=po), r=[pou], w=[osu])
            st, su = sm.next()
            for h in range(4):
                kb.op("dve", lambda e, h=h, st=st, osb=osb: e.bn_stats(out=st[:, 6 * h:6 * h + 6],
                                                                        in_=osb[:, h * 128:(h + 1) * 128]),
                      r=[osu], w=[su])
            st2, su2 = sm.next()
            for h in range(4):
                kb.op("dve", lambda e, h=h, st=st, st2=st2: e.bn_aggr(out=st2[:, 2 * h:2 * h + 2],
                                                                      in_=st[:, 6 * h:6 * h + 6]), r=[su], w=[su2])
            s2v = st2[:, 0:8].rearrange("p (h two) -> p h two", two=2)
            kb.op("act", lambda e, st2=st2, s2v=s2v: e.activation(out=st2[:, 8:12], in_=s2v[:, :, 1], func=AF.Sqrt,
                                                                  bias=epsb[:, 0:1]), r=[su2, tab_u], w=[su2])
            kb.op("dve", lambda e, st2=st2: e.reciprocal(out=st2[:, 8:12], in_=st2[:, 8:12]), r=[su2], w=[su2])
            for h in range(4):
                kb.op("dve", lambda e, h=h, osb=osb, st2=st2: e.tensor_scalar(
                    out=osb[:, h * 128:(h + 1) * 128], in0=osb[:, h * 128:(h + 1) * 128],
                    scalar1=st2[:, 2 * h:2 * h + 1], scalar2=st2[:, 8 + h:9 + h], op0=ALU.subtract, op1=ALU.mult),
                    r=[su2], w=[osu])
            sg, sgu = t32.next()
            kb.op("act", lambda e, sg=sg, gt=gt: e.activation(out=sg[:], in_=gt, func=AF.Silu), r=[gu], w=[sgu])
            kb.op("pool", lambda e, osb=osb: e.tensor_tensor(out=osb[:], in0=osb[:], in1=gnw, op=ALU.mult),
                  r=[gnw_u], w=[osu])
            ct, cu = catb.next()
            tgc = "c%d_%d" % (l, j)
            kb.op("pool", lambda e, ct=ct, osb=osb, sg=sg: e.tensor_tensor(out=ct[:, 0:512], in0=osb[:], in1=sg[:],
                                                                          op=ALU.mult), r=[osu, sgu], wa=[cu], tag=tgc)
            a0, a0u = accl.t[0], accl.u[0]
            kb.dma("sp", a0, acc[0][rows, :], r=[acc_u[0]], w=[a0u])
            if not samp:
                a1, a1u = accl.t[1], accl.u[1]
                kb.dma("sp", a1, acc[1][rows, :], r=[acc_u[1]], w=[a1u])
                kb.op("dve", lambda e, a0=a0, a1=a1: e.tensor_tensor(out=a0, in0=a0, in1=a1, op=ALU.add),
                      r=[a1u], w=[a0u])
                a2, a2u = a1, a1u
                kb.dma("sp", a2, acc[2][rows, :], r=[acc_u[2]], w=[a2u])
                kb.op("dve", lambda e, a0=a0, a2=a2: e.tensor_tensor(out=a0, in0=a0, in1=a2, op=ALU.add),
                      r=[a2u], w=[a0u])
            a0v = a0.rearrange("p (e hp c) -> p e hp c", e=2, hp=4)
            rd, rdu = sm.next()
            kb.op("dve", lambda e, rd=rd, a0v=a0v: e.reciprocal(
                out=rd[:, 0:8].rearrange("p (e hp) -> p e hp", e=2), in_=a0v[:, :, :, 64]), r=[a0u], w=[rdu])
            kb.op("dve", lambda e, ct=ct, a0v=a0v, rd=rd: e.tensor_tensor(
                out=ct[:, 512:1024].rearrange("p (hp e d) -> p e hp d", hp=4, e=2), in0=a0v[:, :, :, 0:64],
                in1=rd[:, 0:8].rearrange("p (e hp) -> p e hp", e=2).unsqueeze(3).broadcast_to([128, 2, 4, 64]),
                op=ALU.mult), r=[a0u, rdu], wa=[cu], tag=tgc)
            if dbg and l == 0:
                kb.dma("pool", dbg_cat[rows, :], ct[:], r=[cu])
                kb.dma("pool", dbg_acc[rows, :], a0, r=[a0u])
            cT, cTu = catT2.next()
            transpose8(ct, cu, cT, dict(w=[cTu]), eng="act")
            for hf in range(2):
                ps, psu = pab.next()

                def f(e, ps=ps, cT=cT, hf=hf):
                    ins = None
                    for k in range(8):
                        ins = e.matmul(ps, lhsT=cT[:, k, :], rhs=wo[hf][0][:, k, :], start=(k == 0), stop=(k == 7))
                    return ins
                kb.op("pe", f, r=[cTu, wo[hf][1]], w=[psu])
                kb.op("dve", lambda e, ps=ps, j=j, hf=hf: e.tensor_tensor(
                    out=x[:, j, hf * 512:(hf + 1) * 512], in0=ps, in1=x[:, j, hf * 512:(hf + 1) * 512], op=ALU.add),
                    r=[psu], w=[x_u[j]])

        if stop == 6:
            kb.finish()
            return nc
        tsets = [tb2, tb2x]
        pend = r2_a(0, tsets[0])
        for j in range(NT):
            nxt = r2_a(j + 1, tsets[(j + 1) % 2]) if j + 1 < NT else None
            r2_b(pend)
            pend = nxt
        if dbg and l == 0:
            for j in range(NT):
                kb.dma("pool", dbg_x[j * 128:(j + 1) * 128, :], x[:, j, :], r=[x_u[j]])
        kb.dma("pool", gin2.ap(), x[126:128, NPT - 1, :], r=[x_u[NPT - 1]], w=[gin2_u])
        allgather(gin2.ap(), gin2_u, gout2.ap(), gout2_u)
        alias(hT_u, [xh_u])
        kb.op("dve", lambda e: e.memset(xh, 0.0), w=[xh_u])
        kb.dma("sp", xh[0:2, :], gout2.ap()[0:2, :], r=[gout2_u], w=[xh_u])
        kb.op("dve", lambda e: e.tensor_scalar(out=xh[0:2, :], in0=xh[0:2, :], scalar1=flag[0:2, 0:1], scalar2=None,
                                               op0=ALU.mult), r=[tab_u], w=[xh_u])
        kb.dma("sp", nwb[0][:], n2b[l], w=[nwb_u[0]])
        h_t, h_u = hb.next()
        rmsnorm(xh, xh_u, nwb[1], nwb_u[1], h_t[:], dict(w=[h_u]))
        transpose8(h_t, h_u, hTh[:, :, :], dict(w=[hTh_u]), ncols=2)

        if stop == 7:
            kb.finish()
            return nc
        alias(hT_u + [xh_u], [aT_u])
        alias(arena_units, [wd_u])
        w_down_v = w_down[l].rearrange("(c p) n -> p c n", p=128)
        for c0 in range(0, 22, 2):
            kb.dma("pool", wd[:, c0:c0 + 2, :], w_down_v[:, c0:c0 + 2, :], wa=[wd_u], tag="wd%d" % l)
        w_up_v = w_up[l].rearrange("(k p) n -> p k n", p=128)
        ffn_seq = [(gi_, f2_) for gi_ in range(len(FFN_GROUPS)) for f2_ in range(11)]
        ffn_loaded = {}

        def load_wup(idx):
            if idx >= len(ffn_seq):
                return
            gi_, f2_ = ffn_seq[idx]
            wt_, wu_ = wbuf.next()
            tgw_ = "u%d_%d_%d" % (l, gi_, f2_)
            c_ = f2_ * 256
            for k2 in range(2):
                kb.dma("pool", wt_[:, 4 * k2:4 * k2 + 4, 0:256], w_up_v[:, 4 * k2:4 * k2 + 4, c_:c_ + 256],
                       wa=[wu_], tag=tgw_)
                kb.dma("pool", wt_[:, 4 * k2:4 * k2 + 4, 256:512], w_up_v[:, 4 * k2:4 * k2 + 4, DFF + c_:DFF + c_ + 256],
                       wa=[wu_], tag=tgw_)
            ffn_loaded[idx] = (wt_, wu_)
        load_wup(0)
        for gi, (t0, t1) in enumerate(FFN_GROUPS):
            ntok = (t1 - t0) * 128
            last = (gi == len(FFN_GROUPS) - 1)
            tiles = list(range(t0, t1)) + ([NPT] if last else [])
            for j in tiles:
                h_t, h_u = hb.next()
                rmsnorm(x[:, j, :], x_u[j], nwb[1], nwb_u[1], h_t[:], dict(w=[h_u]))
                if j < NPT:
                    transpose8(h_t, h_u, h2T[:, :, (j - t0) * 128:(j - t0 + 1) * 128],
                               dict(wa=[h2T_u], tag="h2T%d_%d" % (l, gi)), eng="dve")
                else:
                    transpose8(h_t, h_u, h2Ts[:, :, :], dict(w=[h2Ts_u]), ncols=24, eng="dve")
                    kb.op("dve", lambda e: e.memset(h2Ts[:, :, :].rearrange("p k (b s) -> p k b s", b=4)[:, :, :, 0:2],
                                                    0.0), w=[h2Ts_u])
            wins = []
            c = 0
            while c < ntok:
                n_ = min(512, ntok - c)
                wins.append((c, n_))
                c += n_
            for fc in range(22):
                if fc % 2 == 0:
                    wt2, wu = ffn_loaded.pop(gi * 11 + fc // 2)
                    load_wup(gi * 11 + fc // 2 + 1)
                o_ = (fc % 2) * 128
                wt = wt2[:, :, :].rearrange("p k (s c) -> p k s c", s=2)[:, :, :, o_:o_ + 128]
                cs = (fc, 22 + fc)
                if gi == 0:
                    def f(e, wt=wt):
                        ins = None
                        for s_ in range(2):
                            for k in range(8):
                                ins = e.matmul(PC[:, 1, 2 * s_:2 * s_ + 2], lhsT=wt[:, k, s_, :],
                                               rhs=hTh[:, k, :], start=(k == 0), stop=(k == 7), skip_group_check=True)
                        return ins
                    kb.op("pe", f, r=[wu, hTh_u], w=[PC_u[1]])
                    for s_ in range(2):
                        kb.op("act", lambda e, s_=s_, cs=cs: e.copy(out=carry[:, cs[s_], :],
                                                                    in_=PC[:, 1, 2 * s_:2 * s_ + 2]),
                              r=[PC_u[1]], w=[carry_u])
                for (c0, n_) in wins:
                    ug, ugu = pab.next()
                    uv, uvu = pab.next()
                    for s_, (pp, ppu) in enumerate(((ug, ugu), (uv, uvu))):
                        def f(e, pp=pp, s_=s_, wt=wt, c0=c0, n_=n_):
                            ins = None
                            for k in range(8):
                                ins = e.matmul(pp[:, 0:n_], lhsT=wt[:, k, s_, :],
                                               rhs=h2T[:, k, c0:c0 + n_], start=(k == 0), stop=(k == 7))
                            return ins
                        kb.op("pe", f, r=[wu, h2T_u], w=[ppu])
                    cb = []
                    for s_, (pp, ppu) in enumerate(((ug, ugu), (uv, uvu))):
                        ch = cs[s_]
                        A, Au = t32.next()
                        kb.op("act", lambda e, A=A, pp=pp, ch=ch, n_=n_: e.activation(
                            out=A[:, 0:n_], in_=pp[:, 0:n_], func=AF.Identity, scale=cwt[:, ch, 2:3],
                            bias=cwt[:, ch, 3:4]), r=[ppu, cwt_u], w=[Au])
                        kb.op("dve", lambda e, A=A, pp=pp, ch=ch, n_=n_: e.scalar_tensor_tensor(
                            out=A[:, 1:n_], in0=pp[:, 0:n_ - 1], scalar=cwt[:, ch, 1:2], in1=A[:, 1:n_],
                            op0=ALU.mult, op1=ALU.add), r=[ppu, cwt_u], w=[Au])
                        kb.op("dve", lambda e, A=A, ch=ch: e.scalar_tensor_tensor(
                            out=A[:, 0:1], in0=carry[:, ch, 1:2], scalar=cwt[:, ch, 1:2], in1=A[:, 0:1],
                            op0=ALU.mult, op1=ALU.add), r=[carry_u, cwt_u], w=[Au])
                        kb.op("dve", lambda e, A=A, pp=pp, ch=ch, n_=n_: e.scalar_tensor_tensor(
                            out=A[:, 2:n_], in0=pp[:, 0:n_ - 2], scalar=cwt[:, ch, 0:1], in1=A[:, 2:n_],
                            op0=ALU.mult, op1=ALU.add), r=[ppu, cwt_u], w=[Au])
                        kb.op("dve", lambda e, A=A, ch=ch: e.scalar_tensor_tensor(
                            out=A[:, 0:2], in0=carry[:, ch, 0:2], scalar=cwt[:, ch, 0:1], in1=A[:, 0:2],
                            op0=ALU.mult, op1=ALU.add), r=[carry_u, cwt_u], w=[Au])
                        kb.op("act", lambda e, pp=pp, ch=ch, n_=n_: e.copy(out=carry[:, ch, :], in_=pp[:, n_ - 2:n_]),
                              r=[ppu], w=[carry_u])
                        cb.append((A, Au))
                    kb.op("act", lambda e, A=cb[0][0], n_=n_: e.activation(out=A[:, 0:n_], in_=A[:, 0:n_],
                                                                          func=AF.Silu), w=[cb[0][1]])
                    kb.op("pool", lambda e, Ag=cb[0][0], A=cb[1][0], fc=fc, c0=c0, n_=n_: e.tensor_tensor(
                        out=aT[:, fc, c0:c0 + n_], in0=Ag[:, 0:n_], in1=A[:, 0:n_], op=ALU.mult),
                        r=[cb[0][1], cb[1][1]], wa=[aT_u], tag="aT%d_%d" % (l, gi))
                if last:
                    def f(e, wt=wt):
                        ins = None
                        for s_ in range(2):
                            for k in range(8):
                                ins = e.matmul(PC[:, 1, 32 * s_:32 * s_ + 24], lhsT=wt[:, k, s_, :],
                                               rhs=h2Ts[:, k, :], start=(k == 0), stop=(k == 7), skip_group_check=True)
                        return ins
                    kb.op("pe", f, r=[wu, h2Ts_u], w=[PC_u[1]])
                    cb = []
                    for s_ in range(2):
                        ch = cs[s_]
                        us, usu = sm.next()
                        kb.op("dve", lambda e, us=us, s_=s_: e.tensor_copy(out=us[:, 0:24],
                                                                           in_=PC[:, 1, 32 * s_:32 * s_ + 24]),
                              r=[PC_u[1]], w=[usu])
                        kb.op("dve", lambda e, us=us, ch=ch: e.tensor_copy(
                            out=us[:, 0:24].rearrange("p (b s) -> p b s", b=4)[:, :, 0:2], in_=ctx8[:, ch, :, :]),
                            r=[ctx8_u], w=[usu])
                        A, Au = sm.next()
                        kb.op("act", lambda e, A=A, us=us, ch=ch: e.activation(
                            out=A[:, 0:22], in_=us[:, 2:24], func=AF.Identity, scale=cwt[:, ch, 2:3],
                            bias=cwt[:, ch, 3:4]), r=[usu, cwt_u], w=[Au])
                        kb.op("dve", lambda e, A=A, us=us, ch=ch: e.scalar_tensor_tensor(
                            out=A[:, 0:22], in0=us[:, 1:23], scalar=cwt[:, ch, 1:2], in1=A[:, 0:22],
                            op0=ALU.mult, op1=ALU.add), r=[usu, cwt_u], w=[Au])
                        kb.op("dve", lambda e, A=A, us=us, ch=ch: e.scalar_tensor_tensor(
                            out=A[:, 0:22], in0=us[:, 0:22], scalar=cwt[:, ch, 0:1], in1=A[:, 0:22],
                            op0=ALU.mult, op1=ALU.add), r=[usu, cwt_u], w=[Au])
                        kb.op("act", lambda e, us=us, ch=ch: e.copy(
                            out=uout_s[:, ch, :, :], in_=us[:, 0:24].rearrange("p (b s) -> p b s", b=4)[:, :, 4:6]),
                            r=[usu], wa=[uouts_u], tag="uo%d" % l)
                        cb.append((A, Au))
                    kb.op("act", lambda e, A=cb[0][0]: e.activation(out=A[:, 0:22], in_=A[:, 0:22], func=AF.Silu),
                          w=[cb[0][1]])
                    kb.op("pool", lambda e, Ag=cb[0][0], A=cb[1][0], fc=fc: e.tensor_tensor(
                        out=aTs[:, fc, 2:24], in0=Ag[:, 0:22], in1=A[:, 0:22], op=ALU.mult),
                        r=[cb[0][1], cb[1][1]], wa=[aTs_u], tag="aTs%d" % l)
            for j in tiles:
                for hf in range(2):
                    ps, psu = pab.next()
                    if j < NPT:
                        def f(e, ps=ps, j=j, hf=hf):
                            ins = None
                            for fc in range(22):
                                ins = e.matmul(ps, lhsT=aT[:, fc, (j - t0) * 128:(j - t0 + 1) * 128],
                                               rhs=wd[:, fc, hf * 512:(hf + 1) * 512], start=(fc == 0), stop=(fc == 21))
                            return ins
                        kb.op("pe", f, r=[aT_u, wd_u], w=[psu])
                        kb.op("dve", lambda e, ps=ps, j=j, hf=hf: e.tensor_tensor(
                            out=x[:, j, hf * 512:(hf + 1) * 512], in0=ps, in1=x[:, j, hf * 512:(hf + 1) * 512],
                            op=ALU.add), r=[psu], w=[x_u[j]])
                    else:
                        def f(e, ps=ps, hf=hf):
                            ins = None
                            for fc in range(22):
                                ins = e.matmul(ps[0:24, :], lhsT=aTs[:, fc, :], rhs=wd[:, fc, hf * 512:(hf + 1) * 512],
                                               start=(fc == 0), stop=(fc == 21))
                            return ins
                        kb.op("pe", f, r=[aTs_u, wd_u], w=[psu])
                        kb.op("dve", lambda e, ps=ps, j=j, hf=hf: e.tensor_tensor(
                            out=x[0:24, j, hf * 512:(hf + 1) * 512], in0=ps[0:24, :],
                            in1=x[0:24, j, hf * 512:(hf + 1) * 512], op=ALU.add), r=[psu], w=[x_u[j]])
        if stop == 8:
            kb.finish()
            return nc
        kb.dma("pool", pconv_o[l], carry[:], r=[carry_u])
        kb.dma("pool", sconv_o[l], uout_s[:], r=[uouts_u])

    kb.dma("sp", nwb[0][:], fnb, w=[nwb_u[0]])
    for j in range(NT):
        for hf in range(1):
            pass
        yt, yu = t32.next()
        yt2, yu2 = t32.next()
        st, su = sm.next()
        kb.op("act", lambda e, j=j, st=st: e.activation(out=junk[:, :], in_=x[:, j, :], func=AF.Square,
                                                        accum_out=st[:, 0:1]), r=[x_u[j]], w=[junk_u, su])
        kb.op("act", lambda e, st=st: e.activation(out=st[:, 1:2], in_=st[:, 0:1], func=AF.Sqrt, scale=1.0 / D,
                                                   bias=epsb[:, 0:1]), r=[su, tab_u], w=[su])
        kb.op("dve", lambda e, st=st: e.reciprocal(out=st[:, 2:3], in_=st[:, 1:2]), r=[su], w=[su])
        for hf, (yy, yyu) in enumerate(((yt, yu), (yt2, yu2))):
            kb.op("dve", lambda e, j=j, st=st, yy=yy, hf=hf: e.scalar_tensor_tensor(
                out=yy[:], in0=x[:, j, hf * 512:(hf + 1) * 512], scalar=st[:, 2:3],
                in1=nwb[0][:, hf * 512:(hf + 1) * 512], op0=ALU.mult, op1=ALU.mult),
                r=[x_u[j], su, nwb_u[0]], w=[yyu])
            kb.dma("pool", y_o[j * 128:(j + 1) * 128, hf * 512:(hf + 1) * 512], yy[:], r=[yyu])
    kb.finish()
    return nc


def _tables(half):
    f32 = np.float32
    pos = np.zeros((NT, 128), np.float64)
    for j in range(NPT):
        pos[j] = half * 2048 + 128 * j + np.arange(128)
    for b in range(4):
        for t in range(4):
            pos[NPT, srow(b, t)] = PAST + t
    rope = np.zeros((NT, 128, 192), f32)
    inv_r = (10000.0 ** (-np.arange(64, dtype=np.float32) / np.float32(64))).astype(f32)
    inv_a = (10000.0 ** (-np.arange(32, dtype=np.float32) / np.float32(32))).astype(f32)
    ang_r = pos.astype(f32)[:, :, None] * inv_r[None, None, :]
    ang_a = pos.astype(f32)[:, :, None] * inv_a[None, None, :]
    rope[:, :, 0:64] = np.cos(ang_r.astype(np.float64))
    rope[:, :, 64:128] = np.sin(ang_r.astype(np.float64))
    rope[:, :, 128:160] = np.cos(ang_a.astype(np.float64))
    rope[:, :, 160:192] = np.sin(ang_a.astype(np.float64))
    g = np.array(GAM, np.float64)
    sc = 128.0 ** -0.5
    decq = np.ones((128, NT, 4))
    deck = np.ones((128, NT, 4)) * sc
    p = np.arange(128)
    for j in range(NPT):
        decq[:, j, :] = g[None, :] ** (p[:, None] + 1.0)
        deck[:, j, :] = g[None, :] ** (127.0 - p[:, None]) * sc
    for b in range(4):
        for t in range(4):
            decq[srow(b, t), NPT, :] = g ** (t + 1.0)
            deck[srow(b, t), NPT, :] = g ** (3.0 - t) * sc
    decf = np.zeros((128, NPT, 4))
    for j in range(NPT):
        decf[:, j, :] = g[None, :] ** (2047.0 - (128 * j + p[:, None])) * sc
    mretp = np.zeros((128, 4, 128))
    jj, ii = np.meshgrid(p, p, indexing="ij")
    for h in range(4):
        mretp[:, h, :] = (jj <= ii) * (g[h] ** (-(jj + 1.0))) * sc
    mrets = np.zeros((128, 4, 128))
    rowmask = np.zeros((128, 4))
    for b in range(4):
        for tj in range(4):
            rowmask[srow(b, tj), b] = 1.0
            for ti in range(tj, 4):
                for h in range(4):
                    mrets[srow(b, tj), h, srow(b, ti)] = g[h] ** (-(tj + 1.0)) * sc
    amask = np.zeros((128, 3, 128))
    amask[:, 0, :] = (jj <= ii)
    amask[:, 1, :] = (jj >= ii)
    amask[:, 2, :] = (jj >= ii) * float(half)
    smask = np.zeros((128, 9, 4))
    for t in range(4):
        smask[:, 0, t] = (p >= t)
        smask[:, 1 + t, t] = 1.0
    for b in range(4):
        for tp in range(4):
            for t in range(4):
                smask[srow(b, tp), 5 + b, t] = float(tp <= t) + 2.0 * float(tp == t)
    flag = np.full((128, 1), float(half))
    return dict(rope=rope, decq=decq.astype(f32), deck=deck.astype(f32), decf=decf.astype(f32),
                mretp=mretp.astype(f32), mrets=mrets.astype(f32), rowmask=rowmask.astype(f32),
                amask=amask.astype(f32), smask=smask.astype(f32), flag=flag.astype(f32))


_NC_CACHE = {}


def kernel(x_prompt, x_sample, cache_win_k, cache_win_v, state_ret, state_conv,
           norm1_w, w_in, ret_gn_w, w_out, norm2_w, w_up, conv_w, conv_b, w_down, final_norm_w, _nl=NL, _stop=99, _dbg=False):
    f32 = np.float32
    A = lambda a: np.ascontiguousarray(np.asarray(a, dtype=f32))
    x_prompt, x_sample = A(x_prompt), A(x_sample)
    cache_win_k, cache_win_v = np.asarray(cache_win_k, f32), np.asarray(cache_win_v, f32)
    state_ret, state_conv = np.asarray(state_ret, f32), np.asarray(state_conv, f32)
    if (_nl, _stop, _dbg) not in _NC_CACHE:
        _NC_CACHE[(_nl, _stop, _dbg)] = build(_nl, _stop, _dbg)
    nc = _NC_CACHE[(_nl, _stop, _dbg)]
    shared = dict(
        w_in=A(w_in), w_out=A(w_out), w_up=A(w_up), w_down=A(w_down),
        n1b=A(np.broadcast_to(np.asarray(norm1_w, f32)[:, None, :], (NL, 128, D))),
        n2b=A(np.broadcast_to(np.asarray(norm2_w, f32)[:, None, :], (NL, 128, D))),
        fnb=A(np.broadcast_to(np.asarray(final_norm_w, f32)[None, :], (128, D))),
        gnb=A(np.broadcast_to(np.asarray(ret_gn_w, f32)[:, None, :], (NL, 128, 512))),
    )
    cw = np.concatenate([np.asarray(conv_w, f32), np.asarray(conv_b, f32)[:, None, :]], axis=1)
    shared["convT"] = A(cw.reshape(NL, 4, 44, 128).transpose(0, 3, 2, 1))
    tabs = [_tables(0), _tables(1)]
    in_maps = []
    for c in range(8):
        s, half = c // 2, c % 2
        xin = np.zeros((TOK, D), f32)
        xin[0:2048] = x_prompt[s, half * 2048:(half + 1) * 2048]
        for b in range(4):
            xin[2048 + srow(b, 0):2048 + srow(b, 0) + 4] = x_sample[4 * c + b]
        m = dict(shared)
        m["xin"] = xin
        m["ck"] = A(cache_win_k[:, 4 * c:4 * c + 4].reshape(NL, 4, 2048, 512))
        m["cv"] = A(cache_win_v[:, 4 * c:4 * c + 4].reshape(NL, 4, 2048, 512))
        m["sret_in"] = A(state_ret[:, 4 * c:4 * c + 4])
        sc_ = state_conv[:, 4 * c:4 * c + 4]
        m["sconv_in"] = A(sc_.reshape(NL, 4, 2, 44, 128).transpose(0, 4, 3, 1, 2))
        m.update(tabs[half])
        in_maps.append(m)
    res = run_bass_kernel_spmd(nc, in_maps, core_ids=list(range(8)))
    R = res.results
    y_prompt = np.zeros((4, 4096, D), f32)
    y_sample = np.zeros((32, 4, D), f32)
    p_win_k = np.zeros((NL, 4, 2048, 8, 64), f32)
    p_win_v = np.zeros((NL, 4, 2048, 8, 64), f32)
    p_ret = np.zeros((NL, 4, 4, 128, 128), f32)
    p_conv = np.zeros((NL, 4, 2, 2 * DFF), f32)
    s_win_k = np.zeros((NL, 32, 4, 8, 64), f32)
    s_win_v = np.zeros((NL, 32, 4, 8, 64), f32)
    s_ret = np.zeros((NL, 32, 4, 128, 128), f32)
    s_conv = np.zeros((NL, 32, 2, 2 * DFF), f32)
    for c in range(8):
        s, half = c // 2, c % 2
        r = R[c]
        y_prompt[s, half * 2048:(half + 1) * 2048] = r["y"][0:2048]
        for b in range(4):
            rs = slice(2048 + srow(b, 0), 2048 + srow(b, 0) + 4)
            y_sample[4 * c + b] = r["y"][rs]
            ls = slice(srow(b, 0), srow(b, 0) + 4)
            s_win_k[:, 4 * c + b] = r["sk"][:, ls].reshape(NL, 4, 8, 64)
            s_win_v[:, 4 * c + b] = r["sv"][:, ls].reshape(NL, 4, 8, 64)
            s_ret[:, 4 * c + b] = r["sret"][:, b].transpose(0, 2, 1, 3)
        s_conv[:, 4 * c:4 * c + 4] = r["sconvT"].transpose(0, 3, 4, 2, 1).reshape(NL, 4, 2, 2 * DFF)
        if half == 1:
            p_win_k[:, s] = r["pk"].reshape(NL, 2048, 8, 64)
            p_win_v[:, s] = r["pv"].reshape(NL, 2048, 8, 64)
            p_ret[:, s] = r["pret"].transpose(0, 2, 1, 3)
            p_conv[:, s] = r["pconvT"].transpose(0, 3, 2, 1).reshape(NL, 2, 2 * DFF)
    return (y_prompt, y_sample, p_win_k, p_win_v, p_ret, p_conv, s_win_k, s_win_v, s_ret, s_conv)
```

```python
import math
import numpy as np
import concourse.bass as bass
import concourse.mybir as mybir
from concourse.bass_utils import run_bass_kernel_spmd

F32 = mybir.dt.float32
BF16 = mybir.dt.bfloat16
AF = mybir.ActivationFunctionType
ALU = mybir.AluOpType

D = 1024
DIN = 3584
DFF = 2816
NL = 4
NT = 17
NPT = 16
TOK = NT * 128
EPS = 1e-6
GAM = [1.0 - 2.0 ** (-5 - h) for h in range(4)]
PAST = 8192
GR = 4480
VOFF = 2176
SOFF = 4352
FFN_GROUPS = [(0, 6), (6, 11), (11, 16)]


def srow(b, t):
    return 6 * b + 2 + t


class U:
    __slots__ = ("w", "r", "tag", "const")

    def __init__(self, const=False):
        self.w = []
        self.r = []
        self.tag = None
        self.const = const


class KB:
    def __init__(self, nc):
        self.nc = nc
        self.eng = {"pe": nc.tensor, "act": nc.scalar, "dve": nc.vector, "pool": nc.gpsimd, "sp": nc.sync}
        self.csem = {}
        self.ccnt = {}
        for e in ("pe", "act", "dve", "pool"):
            self.csem[e] = nc.alloc_semaphore(name="c_" + e)
            self.ccnt[e] = 0
        self.dsem = {"sp": [nc.alloc_semaphore(name="dsp%d" % i) for i in range(24)],
                     "pool": [nc.alloc_semaphore(name="dpl%d" % i) for i in range(16)]}
        self.dval = {"sp": [0] * 24, "pool": [0] * 16}
        self.dnext = {"sp": 0, "pool": 0}
        self.seen = {e: {} for e in self.eng}
        self.semobj = {}

    def _wait(self, e, evs):
        need = {}
        for (s, v) in evs:
            k = id(s)
            self.semobj[k] = s
            if need.get(k, 0) < v:
                need[k] = v
        sn = self.seen[e]
        for k, v in need.items():
            if sn.get(k, 0) < v:
                self.eng[e].wait_ge(self.semobj[k], v)
                sn[k] = v

    def _collect(self, r, w, wa, tag):
        evs = []
        for u in r:
            evs += u.w
        for u in w:
            evs += u.w
            evs += u.r
        for u in wa:
            evs += u.r
            if u.tag != tag:
                evs += u.w
        return evs

    def _commit(self, ev, r, w, wa, tag):
        for u in r:
            if not u.const:
                u.r.append(ev)
        for u in w:
            u.w = [ev]
            u.r = []
            u.tag = None
        for u in wa:
            if u.tag != tag:
                u.w = []
                u.tag = tag
            u.w.append(ev)
            u.r = []

    def op(self, e, fn, r=(), w=(), wa=(), tag=None):
        self._wait(e, self._collect(r, w, wa, tag))
        ins = fn(self.eng[e])
        self.ccnt[e] += 1
        ins.then_inc(self.csem[e], 1)
        self._commit((self.csem[e], self.ccnt[e]), r, w, wa, tag)

    def dma(self, e, out, in_, r=(), w=(), wa=(), tag=None, **kw):
        evs = self._collect(r, w, wa, tag)
        k = self.dnext[e]
        self.dnext[e] = (k + 1) % len(self.dsem[e])
        sem = self.dsem[e][k]
        if self.dval[e][k] > 0:
            evs.append((sem, self.dval[e][k]))
        self._wait(e, evs)
        self.dval[e][k] += 16
        self.eng[e].dma_start(out=out, in_=in_, **kw).then_inc(sem, 16)
        self._commit((sem, self.dval[e][k]), r, w, wa, tag)

    def finish(self):
        evs = []
        for e in ("sp", "pool"):
            for s, v in zip(self.dsem[e], self.dval[e]):
                if v > 0:
                    evs.append((s, v))
        for e in ("pe", "act", "dve", "pool"):
            if self.ccnt[e] > 0:
                evs.append((self.csem[e], self.ccnt[e]))
        for e in ("sp", "act", "dve", "pe", "pool"):
            self._wait(e, evs)


class Rot:
    def __init__(self, tensors):
        self.t = tensors
        self.u = [U() for _ in tensors]
        self.i = 0

    def next(self):
        k = self.i
        self.i = (k + 1) % len(self.t)
        return self.t[k], self.u[k]


def build(nl=NL, stop=99, dbg=False):
    nc = bass.Bass("TRN2", target_bir_lowering=False)
    kb = KB(nc)

    def din(name, shape, dt=F32):
        return nc.dram_tensor(name, list(shape), dt, kind="ExternalInput").ap()

    def dout(name, shape, dt=F32):
        return nc.dram_tensor(name, list(shape), dt, kind="ExternalOutput").ap()

    def dscr(name, shape, dt):
        return nc.dram_tensor(name, list(shape), dt)

    xin = din("xin", [TOK, D])
    w_in = din("w_in", [NL, D, DIN])
    w_out = din("w_out", [NL, D, D])
    w_up = din("w_up", [NL, D, 2 * DFF])
    w_down = din("w_down", [NL, DFF, D])
    n1b = din("n1b", [NL, 128, D])
    n2b = din("n2b", [NL, 128, D])
    fnb = din("fnb", [128, D])
    gnb = din("gnb", [NL, 128, 512])
    convT = din("convT", [NL, 128, 44, 4])
    sconv_in = din("sconv_in", [NL, 128, 44, 4, 2])
    sret_in = din("sret_in", [NL, 4, 4, 128, 128])
    ck = din("ck", [NL, 4, 2048, 512])
    cv = din("cv", [NL, 4, 2048, 512])
    rope_d = din("rope", [NT, 128, 192])
    decq_d = din("decq", [128, NT, 4])
    deck_d = din("deck", [128, NT, 4])
    decf_d = din("decf", [128, NPT, 4])
    mretp_d = din("mretp", [128, 4, 128])
    mrets_d = din("mrets", [128, 4, 128])
    rowmask_d = din("rowmask", [128, 4])
    amask_d = din("amask", [128, 3, 128])
    smask_d = din("smask", [128, 9, 4])
    flag_d = din("flag", [128, 1])

    y_o = dout("y", [TOK, D])
    pk_o = dout("pk", [NL, 2048, 512])
    pv_o = dout("pv", [NL, 2048, 512])
    sk_o = dout("sk", [NL, 128, 512])
    sv_o = dout("sv", [NL, 128, 512])
    pret_o = dout("pret", [NL, 128, 4, 128])
    pconv_o = dout("pconvT", [NL, 128, 44, 2])
    sret_o = dout("sret", [NL, 4, 128, 4, 128])
    sconv_o = dout("sconvT", [NL, 128, 44, 4, 2])
    if dbg:
        dbg_cat = dout("dbg_cat", [TOK, D], BF16)
        dbg_x = dout("dbg_x", [TOK, D])
        dbg_acc = dout("dbg_acc", [TOK, 520])

    zs = [dscr("zs%d" % g, [TOK, 512], BF16).ap() for g in range(5)]
    zs_u = [U() for _ in range(5)]
    gK = dscr("gK", [2176, 512], BF16)
    gKo = [dscr("gKo%d" % i, [2176, 512], BF16) for i in range(2)]
    rK = dscr("rK", [2176, 512], BF16)
    gV = dscr("gV", [2176, 520], BF16)
    gVo = [dscr("gVo%d" % i, [2176, 520], BF16) for i in range(2)]
    rV = dscr("rV", [2176, 520], BF16)
    rkv_u = U()
    gS = dscr("gS", [128, 512], BF16)
    gSo = dscr("gSo", [256, 512], BF16)
    gin_u = U()
    gout_u = U()
    gin2 = dscr("gin2", [2, D], F32)
    gout2 = dscr("gout2", [4, D], F32)
    gin2_u = U()
    gout2_u = U()
    acc = [dscr("acc%d" % i, [TOK, 520], F32).ap() for i in range(3)]
    acc_u = [U() for _ in range(3)]
    ccsem = nc.alloc_semaphore(name="ccsem")
    cc_cnt = [0]

    def sb(name, shape, dt):
        return nc.alloc_sbuf_tensor("sb_" + name, list(shape), dt)

    x = sb("x", [128, NT, D], F32)
    x_u = [U() for _ in range(NT)]
    BIGN = max(8 * TOK, 22 * 768)
    big = sb("big", [128, BIGN], BF16)
    hT = big[:, 0:8 * TOK].rearrange("p (k t) -> p k t", k=8)
    hT_u = [U() for _ in range(NT)]
    aT = big[:, 0:22 * 768].rearrange("p (c t) -> p c t", c=22)
    aT_u = U()
    xh = big[:, 0:2 * D].bitcast(F32)
    xh_u = U()
    arena = sb("arena", [128, 22 * D], BF16)
    wd = arena[:, :].rearrange("p (c n) -> p c n", c=22)
    wd_u = U()
    a_off = [0]
    arena_units = []

    def carve(shape, dt):
        n = 1
        for s_ in shape[1:]:
            n *= s_
        nb = n * (2 if dt == BF16 else 4) // 2
        nb = (nb + 15) // 16 * 16
        o = a_off[0]
        a_off[0] += nb
        assert a_off[0] <= 22 * D, a_off[0]
        ap = arena[:, o:o + nb]
        if dt == F32:
            ap = ap.bitcast(F32)
        ap = ap[:, 0:n]
        if len(shape) == 3:
            ap = ap.rearrange("p (a b) -> p a b", a=shape[1])
        elif len(shape) == 4:
            ap = ap.rearrange("p (a b c) -> p a b c", a=shape[1], b=shape[2])
        return ap

    def arot(n, shape, dt):
        r_ = Rot([carve(shape, dt) for _ in range(n)])
        arena_units.extend(r_.u)
        return r_

    def aunit():
        u = U()
        arena_units.append(u)
        return u

    ident = sb("ident", [128, 128], BF16)
    ident_u = U(const=True)
    nwb = [sb("nwb0", [128, D], F32)]
    nwb.append(nwb[0])
    nwb_u = [U()]
    nwb_u.append(nwb_u[0])
    cwt = sb("cwt", [128, 44, 4], F32)
    cwt_u = U()
    decq = sb("decq", [128, NT, 4], F32)
    deck = sb("deck", [128, NT, 4], F32)
    decf = sb("decf", [128, NPT, 4], F32)
    mretp = sb("mretp", [128, 4, 128], BF16)
    mrets = sb("mrets", [128, 4, 128], BF16)
    rowmask = sb("rowmask", [128, 4], F32)
    amask = sb("amask", [128, 3, 128], BF16)
    smask = sb("smask", [128, 9, 4], BF16)
    flag = sb("flag", [128, 1], F32)
    epsb = sb("epsb", [128, 1], F32)
    tab_u = U(const=True)
    carry = sb("carry", [128, 44, 2], F32)
    carry_u = U()
    uout_s = sb("uout_s", [128, 44, 4, 2], F32)
    uouts_u = U()
    ctx8 = sb("ctx8", [128, 44, 4, 2], F32)
    ctx8_u = U()
    aTs = sb("aTs", [128, 22, 24], BF16)
    aTs_u = U()
    hTh = sb("hTh", [128, 8, 2], BF16)
    hTh_u = U()
    h2Ts = sb("h2Ts", [128, 8, 24], BF16)
    h2Ts_u = U()

    def rot(name, n, shape, dt):
        return Rot([sb("%s%d" % (name, i), shape, dt) for i in range(n)])

    wbuf = rot("wbuf", 2, [128, 8, 512], BF16)
    wbufF = Rot([wbuf.t[0][:, :, 0:256], wbuf.t[0][:, :, 256:512], wbuf.t[1][:, :, 0:256], wbuf.t[1][:, :, 256:512]])
    t32 = rot("t32", 4, [128, 512], F32)
    t32b = t32
    tb = rot("tb", 3, [128, 512], BF16)
    hb = rot("hb", 2, [128, D], BF16)
    junk = sb("junk", [128, D], BF16)
    junk_u = U()
    sm = rot("sm", 6, [128, 32], F32)
    pes = rot("pes", 2, [128, 32], BF16)
    h2T = sb("h2T", [128, 8, 768], BF16)
    h2T_u = U()

    gnw = carve([128, 512], F32)
    gnw_u = aunit()
    Sst = carve([128, 4, 128], F32)
    Sst_u = aunit()
    Sb = carve([128, 4, 128], BF16)
    Sb_u = aunit()
    s0bp = arot(2, [128, 4, 128], BF16)
    qz = carve([128, 4, 4, 24], BF16)
    qz_u = aunit()
    tb2 = arot(4, [128, 512], BF16)
    ropet = arot(2, [128, 192], F32)
    v65 = arot(5, [128, 8, 65], BF16)
    qkT = arot(2, [128, 8, 128], BF16)
    catT = qkT
    catb = hb
    kTs = arot(3, [128, 4, 128], BF16)
    qTp = arot(2, [128, 4, 128], BF16)
    pm = arot(2, [128, 2, 1024], BF16)
    ob = arot(2, [128, 520], F32)
    accl = ob
    kz = arot(2, [128, 512], BF16)
    _pm0 = pm.t[0].rearrange("p t c -> p (t c)")
    _pm1 = pm.t[1].rearrange("p t c -> p (t c)")
    tb2x = Rot([_pm0[:, i * 512:(i + 1) * 512] for i in range(4)])
    catT2 = Rot([_pm1[:, i * 1024:(i + 1) * 1024].rearrange("p (k t) -> p k t", k=8) for i in range(2)])
    arena_units.extend(tb2x.u + catT2.u)
    sacc = carve([128, 520], F32)
    sacc_u = aunit()

    PA = nc.alloc_psum_tensor("PA", [128, 2, 512], F32)
    PB = nc.alloc_psum_tensor("PB", [128, 2, 512], F32)
    PC = nc.alloc_psum_tensor("PC", [128, 2, 512], F32)
    PT = [nc.alloc_psum_tensor("PT%d" % i, [128, 1024], BF16) for i in range(2)]
    PA_u = [U(), U()]
    PB_u = [U(), U()]
    PC_u = [U(), U()]
    PT_u = [U(), U()]
    pt_i = [0]
    pab = Rot([PA[:, 0, :], PA[:, 1, :], PB[:, 0, :], PB[:, 1, :]])
    pab.u = [PA_u[0], PA_u[1], PB_u[0], PB_u[1]]

    def next_pt():
        k = pt_i[0]
        pt_i[0] = 1 - k
        return PT[k], PT_u[k]

    def alias(frm, to):
        evr = []
        evw = []
        for u in frm:
            evr += u.r
            evw += u.w
        for u in to:
            u.r = u.r + evr
            u.w = u.w + evw

    kb.op("pool", lambda e: e.memset(ident[:], 1.0), w=[ident_u])
    kb.op("pool", lambda e: e.affine_select(out=ident[:], in_=ident[:], pattern=[[-1, 128]],
                                            compare_op=ALU.is_equal, fill=0.0, base=0, channel_multiplier=1),
          w=[ident_u])
    for t_, d_ in ((decq, decq_d), (deck, deck_d), (decf, decf_d), (rowmask, rowmask_d), (flag, flag_d)):
        kb.dma("sp", t_[:], d_, w=[tab_u])
    for t_, d_ in ((amask, amask_d), (smask, smask_d), (mretp, mretp_d), (mrets, mrets_d)):
        kb.dma("pool", t_[:], d_, w=[tab_u])
    for j in range(NT):
        kb.dma("sp", x[:, j, :], xin[j * 128:(j + 1) * 128, :], w=[x_u[j]])
    kb.op("dve", lambda e: e.memset(aTs[:], 0.0), w=[aTs_u])
    kb.op("dve", lambda e: e.memset(epsb[:], EPS), w=[tab_u])
    kb.op("dve", lambda e: e.memset(h2Ts[:], 0.0), w=[h2Ts_u])
    ones_t, ones_u = t32.next()
    kb.op("dve", lambda e: e.memset(ones_t[:], 1.0), w=[ones_u])
    kb.dma("sp", acc[0][2048:2176, 0:512], ones_t[:], r=[ones_u], wa=[acc_u[0]], tag="init")
    kb.dma("sp", acc[0][2048:2176, 8:520], ones_t[:], r=[ones_u], wa=[acc_u[0]], tag="init")

    def rmsnorm(xt_ap, xu, wt, wu, out_ap, out_kw, np_=128):
        st, su = sm.next()
        kb.op("act", lambda e: e.activation(out=junk[0:np_, :], in_=xt_ap, func=AF.Square,
                                            accum_out=st[0:np_, 0:1]), r=[xu], w=[junk_u, su])
        kb.op("act", lambda e: e.activation(out=st[0:np_, 1:2], in_=st[0:np_, 0:1], func=AF.Sqrt, scale=1.0 / D,
                                            bias=epsb[0:np_, 0:1]), r=[su, tab_u], w=[su])
        kb.op("dve", lambda e: e.reciprocal(out=st[0:np_, 2:3], in_=st[0:np_, 1:2]), r=[su], w=[su])
        kb.op("dve", lambda e: e.scalar_tensor_tensor(out=out_ap, in0=xt_ap, scalar=st[0:np_, 2:3],
                                                      in1=wt[0:np_, :], op0=ALU.mult, op1=ALU.mult),
              r=[xu, su, wu], **out_kw)

    def transpose8(src, src_u, dst_ap, dst_kw, ncols=128, eng="act"):
        pt, ptu = next_pt()

        def f(e):
            ins = None
            for k in range(8):
                ins = e.transpose(out=pt[:, k * 128:(k + 1) * 128], in_=src[:, k * 128:(k + 1) * 128],
                                  identity=ident[:])
            return ins
        kb.op("pe", f, r=[src_u, ident_u], w=[ptu])
        pv_ = pt[:, :].rearrange("p (k t) -> p k t", k=8)[:, :, 0:ncols]
        if eng == "act":
            kb.op("act", lambda e: e.copy(out=dst_ap, in_=pv_), r=[ptu], **dst_kw)
        else:
            kb.op("dve", lambda e: e.tensor_copy(out=dst_ap, in_=pv_), r=[ptu], **dst_kw)

    def rope_apply(zt, zu, j, half, H, out_ap, out_u):
        rt, ru = ropet.next()
        kb.dma("sp", rt[:], rope_d[j], w=[ru])
        c0 = 0 if half == 64 else 128
        cos = rt[:, c0:c0 + half]
        sin = rt[:, c0 + half:c0 + 2 * half]
        zv = zt.rearrange("p (h two d) -> p h two d", h=H, two=2)
        ov = out_ap.rearrange("p (h two d) -> p h two d", h=H, two=2)
        t1, t1u = t32.next()
        t1v = t1[:, :].rearrange("p (h two d) -> p h two d", h=H, two=2)
        m, mu = t32.next()
        mv = m[:, :].rearrange("p (two h d) -> p two h d", two=2, h=H)
        cosb = cos.unsqueeze(1).unsqueeze(1).broadcast_to([128, H, 2, half])
        sinb = sin.unsqueeze(1).broadcast_to([128, H, half])
        kb.op("dve", lambda e: e.tensor_tensor(out=t1v, in0=zv, in1=cosb, op=ALU.mult), r=[zu, ru], w=[t1u])
        kb.op("dve", lambda e: e.tensor_tensor(out=mv[:, 0], in0=zv[:, :, 1, :], in1=sinb, op=ALU.mult),
              r=[zu, ru], wa=[mu], tag="m")
        kb.op("dve", lambda e: e.tensor_tensor(out=mv[:, 1], in0=zv[:, :, 0, :], in1=sinb, op=ALU.mult),
              r=[zu, ru], wa=[mu], tag="m")
        kb.op("dve", lambda e: e.tensor_tensor(out=ov[:, :, 0, :], in0=t1v[:, :, 0, :], in1=mv[:, 0],
                                               op=ALU.subtract), r=[t1u, mu], wa=[out_u], tag="o")
        kb.op("dve", lambda e: e.tensor_tensor(out=ov[:, :, 1, :], in0=t1v[:, :, 1, :], in1=mv[:, 1],
                                               op=ALU.add), r=[t1u, mu], wa=[out_u], tag="o")

    def allgather(src, src_u, dst, dst_u):
        evs = kb._collect([src_u], [dst_u], (), None)
        kb._wait("pool", evs)
        cc_cnt[0] += 1
        nc.gpsimd.collective_compute("AllGather", ALU.bypass, replica_groups=[[0, 1], [2, 3], [4, 5], [6, 7]],
                                     ins=[src], outs=[dst]).then_inc(ccsem)
        kb._commit((ccsem, cc_cnt[0]), [src_u], [dst_u], (), None)

    ginK = gK.ap()
    ginV = gV.ap()
    ginS = gS.ap()
    goutK = rK.ap()
    goutV = rV.ap()
    goutS = gSo.ap()[0:128, :]

    for l in range(nl):
        alias([wd_u], arena_units)
        alias([aT_u, xh_u], hT_u)
        kb.dma("sp", nwb[0][:], n1b[l], w=[nwb_u[0]])
        kb.dma("sp", gnw, gnb[l], w=[gnw_u])
        kb.dma("sp", cwt[:], convT[l], w=[cwt_u])
        kb.dma("sp", ctx8[:], sconv_in[l], w=[ctx8_u])
        kb.op("dve", lambda e: e.memset(qz, 0.0), w=[qz_u])
        for i in range(len(v65.t)):
            kb.op("pool", lambda e, i=i: e.memset(v65.t[i], 1.0), w=[v65.u[i]])

        if stop == 0:
            kb.finish()
            return nc
        for j in range(NT):
            h_t, h_u = hb.next()
            rmsnorm(x[:, j, :], x_u[j], nwb[0], nwb_u[0], h_t[:], dict(w=[h_u]))
            transpose8(h_t, h_u, hT[:, :, j * 128:(j + 1) * 128], dict(w=[hT_u[j]]))

        if stop == 1:
            kb.finish()
            return nc
        w_in_v = w_in[l].rearrange("(k p) n -> p k n", p=128)
        def load_win(g):
            wt, wu = wbuf.next()
            for k2 in range(4):
                kb.dma("pool", wt[:, 2 * k2:2 * k2 + 2, :], w_in_v[:, 2 * k2:2 * k2 + 2, g * 512:(g + 1) * 512],
                       wa=[wu], tag="ld%d_%d" % (l, g))
            return wt, wu
        win_next = load_win(0)
        for g in range(7):
            wt, wu = win_next
            if g + 1 < 7:
                win_next = load_win(g + 1)
            for j in range(NT):
                ps, psu = pab.next()

                def f(e, ps=ps, wt=wt, j=j):
                    ins = None
                    for k in range(8):
                        ins = e.matmul(ps, lhsT=hT[:, k, j * 128:(j + 1) * 128], rhs=wt[:, k, :],
                                       start=(k == 0), stop=(k == 7))
                    return ins
                kb.op("pe", f, r=[hT_u[j], wu], w=[psu])
                rows = slice(j * 128, (j + 1) * 128)
                if g in (0, 1, 4):
                    o, ou = tb.next()
                    rope_apply(ps, psu, j, 64 if g < 4 else 32, 4 if g < 4 else 8, o[:, :], ou)
                    kb.dma("pool", zs[g][rows, :], o[:], r=[ou], wa=[zs_u[g]], tag="z%d" % l)
                elif g in (2, 3):
                    o, ou = tb.next()
                    kb.op("act", lambda e, o=o, ps=ps: e.copy(out=o[:], in_=ps), r=[psu], w=[ou])
                    kb.dma("pool", zs[g][rows, :], o[:], r=[ou], wa=[zs_u[g]], tag="z%d" % l)
                elif g == 5:
                    o, ou = t32.next()
                    rope_apply(ps, psu, j, 32, 8, o[:, :], ou)
                    if j < NPT:
                        kb.dma("pool", pk_o[l, rows, :], o[:], r=[ou])
                    else:
                        kb.dma("pool", sk_o[l], o[:], r=[ou])
                    ob_, obu = tb.next()
                    kb.op("act", lambda e, ob_=ob_, o=o: e.copy(out=ob_[:], in_=o[:]), r=[ou], w=[obu])
                    kb.dma("pool", ginK[rows, :], ob_[:], r=[obu], wa=[gin_u], tag="g%d" % l)
                else:
                    z, zu = t32.next()
                    kb.op("act", lambda e, z=z, ps=ps: e.copy(out=z[:], in_=ps), r=[psu], w=[zu])
                    if j < NPT:
                        kb.dma("pool", pv_o[l, rows, :], z[:], r=[zu])
                    else:
                        kb.dma("pool", sv_o[l], z[:], r=[zu])
                    vt, vu = v65.next()
                    kb.op("dve", lambda e, vt=vt, z=z: e.tensor_copy(
                        out=vt[:, :, 0:64], in_=z[:, :].rearrange("p (h d) -> p h d", h=8)), r=[zu], w=[vu])
                    kb.dma("pool", ginV[rows, :], vt.rearrange("p h d -> p (h d)"), r=[vu],
                           wa=[gin_u], tag="g%d" % l)

        if stop == 2:
            kb.finish()
            return nc
        for j in range(NPT):
            rows = slice(j * 128, (j + 1) * 128)
            kt, ku = tb.next()
            vt_, vu_ = tb2.next()
            kb.dma("sp", kt[:], zs[1][rows, :], r=[zs_u[1]], w=[ku])
            kb.dma("sp", vt_, zs[2][rows, :], r=[zs_u[2]], w=[vu_])
            kd, kdu = kz.next()
            kb.op("dve", lambda e, kd=kd, kt=kt, j=j: e.tensor_tensor(
                out=kd.rearrange("p (h d) -> p h d", h=4), in0=kt[:, :].rearrange("p (h d) -> p h d", h=4),
                in1=decf[:, j, :].unsqueeze(2).broadcast_to([128, 4, 128]), op=ALU.mult), r=[ku, tab_u], w=[kdu])

            def f(e, kd=kd, vt_=vt_, j=j):
                ins = None
                for h in range(4):
                    ins = e.matmul(PC[:, 0, h * 128:(h + 1) * 128], lhsT=kd[:, h * 128:(h + 1) * 128],
                                   rhs=vt_[:, h * 128:(h + 1) * 128], start=(j == 0), stop=(j == NPT - 1),
                                   skip_group_check=True)
                return ins
            if j == 0:
                kb.op("pe", f, r=[kdu, vu_], w=[PC_u[0]])
            else:
                kb.op("pe", f, r=[kdu, vu_], wa=[PC_u[0]], tag=None)
        sl, slu = tb.next()
        kb.op("act", lambda e: e.copy(out=sl[:], in_=PC[:, 0, :]), r=[PC_u[0]], w=[slu])
        kb.dma("pool", ginS, sl[:], r=[slu], wa=[gin_u], tag="g%d" % l)

        for hh in range(2):
            allgather(gK.ap()[hh * 1088:(hh + 1) * 1088, :], gin_u, gKo[hh].ap(), gout_u)
            allgather(gV.ap()[hh * 1088:(hh + 1) * 1088, :], gin_u, gVo[hh].ap(), gout_u)
        allgather(gS.ap(), gin_u, gSo.ap(), gout_u)
        for hh in range(2):
            kb.dma("pool", rK.ap()[hh * 1088:(hh + 1) * 1088, :], gKo[hh].ap()[0:1088, :], r=[gout_u],
                   wa=[rkv_u], tag="rkv%d" % l)
            kb.dma("pool", rV.ap()[hh * 1088:(hh + 1) * 1088, :], gVo[hh].ap()[0:1088, :], r=[gout_u],
                   wa=[rkv_u], tag="rkv%d" % l)

        if stop == 3:
            kb.finish()
            return nc
        wo = []
        w_out_v = w_out[l].rearrange("(k p) n -> p k n", p=128)
        for hf in range(2):
            wt, wu = wbuf.next()
            for k2 in range(4):
                kb.dma("pool", wt[:, 2 * k2:2 * k2 + 2, :], w_out_v[:, 2 * k2:2 * k2 + 2, hf * 512:(hf + 1) * 512],
                       wa=[wu], tag="wo%d" % l)
            wo.append((wt, wu))

        def load_kv(rows_ap_k, rows_ap_v, srcs_u, cast=False):
            kt, ku = tb2.next()
            q = "pool" if cast else "sp"
            kb.dma(q, kt, rows_ap_k, r=srcs_u, w=[ku])
            vt, vu = v65.next()
            if cast:
                kb.dma(q, vt[:, :, 0:64], rows_ap_v.rearrange("p (h d) -> p h d", h=8), r=srcs_u, w=[vu])
            else:
                kb.dma(q, vt.rearrange("p h d -> p (h d)"), rows_ap_v, r=srcs_u, w=[vu])
            pt, ptu = next_pt()

            def f(e):
                ins = None
                for k in range(4):
                    ins = e.transpose(out=pt[:, k * 128:(k + 1) * 128], in_=kt[:, k * 128:(k + 1) * 128],
                                      identity=ident[:])
                return ins
            kb.op("pe", f, r=[ku, ident_u], w=[ptu])
            kT, kTu = kTs.next()
            kb.op("act", lambda e: e.copy(out=kT, in_=pt[:, 0:512].rearrange("p (k t) -> p k t", k=4)),
                  r=[ptu], w=[kTu])
            return (kT, kTu, vt, vu)

        def load_q(rows_ap):
            qt, qu = tb2.next()
            kb.dma("sp", qt, rows_ap, r=[zs_u[4]], w=[qu])
            pt, ptu = next_pt()

            def f(e):
                ins = None
                for k in range(4):
                    ins = e.transpose(out=pt[:, k * 128:(k + 1) * 128], in_=qt[:, k * 128:(k + 1) * 128],
                                      identity=ident[:])
                return ins
            kb.op("pe", f, r=[qu, ident_u], w=[ptu])
            qT, qTu = qTp.next()
            kb.op("dve", lambda e: e.tensor_copy(out=qT, in_=pt[:, 0:512].rearrange("p (k t) -> p k t", k=4)),
                  r=[ptu], w=[qTu])
            return qT, qTu

        def scores_exp(kT, kTu, qT, qTu, ncol, P, P_u, mask_ap, pm_ap, pm_kw, q0=0, meng="pool"):
            def f(e):
                ins = None
                for hp in range(4):
                    for ee in range(2):
                        ins = e.matmul(P[:, ee, hp * ncol:(hp + 1) * ncol],
                                       lhsT=kT[64 * ee:64 * ee + 64, hp, :],
                                       rhs=qT[64 * ee:64 * ee + 64, hp, q0:q0 + ncol], start=True, stop=True,
                                       skip_group_check=True)
                return ins
            kb.op("pe", f, r=[kTu, qTu], w=P_u)
            pev = pm_ap.rearrange("p (e c) -> p e c", e=2)
            for ee in range(2):
                kb.op("act", lambda e, ee=ee: e.activation(out=pev[:, ee, :], in_=P[:, ee, 0:4 * ncol],
                                                           func=AF.Exp, scale=0.125),
                      r=[P_u[ee]], **pm_kw)
            pg = pm_ap.rearrange("p (g c) -> p g c", g=8)
            kb.op(meng, lambda e: e.tensor_tensor(
                out=pg, in0=pg, in1=mask_ap.unsqueeze(1).broadcast_to([128, 8, ncol]), op=ALU.mult),
                r=[tab_u], w=[pm_kw["wa"][0]])

        alias(tb2x.u + catT2.u, pm.u)
        blocks = []
        for pi, dil in enumerate((1, 4, 16)):
            span = 128 * dil
            nblk = 2048 // span
            for r_ in range(dil):
                for n in range(nblk):
                    blocks.append((pi, dil, span, nblk, r_, n))
        st_prev = [None]

        def stage_a(blk):
            pi, dil, span, nblk, r_, n = blk
            r0 = n * span + r_
            rs = slice(r0, r0 + 128 * dil, dil) if dil > 1 else slice(r0, r0 + 128)
            if n == 0:
                p0 = (nblk - 1) * span + r_
                ps_ = slice(p0, p0 + 128 * dil, dil) if dil > 1 else slice(p0, p0 + 128)
                prev = load_kv(goutK[ps_, :], goutV[ps_, :], [rkv_u])
            else:
                prev = st_prev[0]
            cur = load_kv(ginK[rs, :], ginV[rs, :], [gin_u])
            qT, qTu = load_q(zs[4][rs, :])
            pmt, pmu = pm.next()
            tg = "pm%d_%d_%d_%d" % (l, pi, r_, n)
            scores_exp(prev[0], prev[1], qT, qTu, 128, PA, PA_u, amask[:, 2 if n == 0 else 1, :],
                       pmt[:, 0, :], dict(wa=[pmu], tag=tg), meng="dve")
            scores_exp(cur[0], cur[1], qT, qTu, 128, PB, PB_u, amask[:, 0, :],
                       pmt[:, 1, :], dict(wa=[pmu], tag=tg), meng="pool")
            st_prev[0] = cur
            return (pi, rs, prev, cur, pmt, pmu)

        def stage_b(st):
            pi, rs, prev, cur, pmt, pmu = st
            pmv = pmt.rearrange("p t (g c) -> p t g c", g=8)

            def f(e):
                ins = None
                for hp in range(4):
                    for ee in range(2):
                        h = 2 * hp + ee
                        g_ = ee * 4 + hp
                        o = PC[:, ee, hp * 65:(hp + 1) * 65]
                        e.matmul(o, lhsT=pmv[:, 0, g_, :], rhs=prev[2][:, h, :], start=True, stop=False)
                        ins = e.matmul(o, lhsT=pmv[:, 1, g_, :], rhs=cur[2][:, h, :], start=False, stop=True)
                return ins
            kb.op("pe", f, r=[pmu, prev[3], cur[3]], w=PC_u)
            o_, ou_ = ob.next()
            kb.op("act", lambda e: e.copy(out=o_.rearrange("p (e c) -> p e c", e=2), in_=PC[:, :, 0:260]),
                  r=PC_u, w=[ou_])
            kb.dma("pool", acc[pi][rs, :], o_, r=[ou_], wa=[acc_u[pi]], tag="acc%d" % l)

        pend = stage_a(blocks[0])
        for bi in range(len(blocks)):
            nxt = stage_a(blocks[bi + 1]) if bi + 1 < len(blocks) else None
            stage_b(pend)
            pend = nxt

        if stop == 4:
            kb.finish()
            return nc
        sq_rows = slice(2048, 2176)
        qTs, qTsu = load_q(zs[4][sq_rows, :])
        for b in range(4):
            specs = [(slice(1920, 2048), 0)]
            specs += [(slice(1536 + r_, 2048, 4), 1 + r_) for r_ in range(4)]
            specs += [(slice(r_, 2048, 16), 1 + r_) for r_ in range(4)]
            specs += [(None, 5 + b)]
            for ti, (rsl, mi) in enumerate(specs):
                if rsl is None:
                    kv = load_kv(ginK[sq_rows, :], ginV[sq_rows, :], [gin_u])
                else:
                    kv = load_kv(ck[l, b, rsl, :], cv[l, b, rsl, :], [], cast=True)
                pe_, peu = pes.next()
                scores_exp(kv[0], kv[1], qTs, qTsu, 4, PA, PA_u, smask[:, mi, :],
                           pe_[:, :], dict(wa=[peu], tag="s%d_%d_%d" % (l, b, ti)), q0=srow(b, 0), meng="dve")

                def f(e, kv=kv, pe_=pe_):
                    ins = None
                    for hp in range(4):
                        for ee in range(2):
                            h = 2 * hp + ee
                            g_ = ee * 4 + hp
                            ins = e.matmul(PC[0:4, ee, hp * 65:(hp + 1) * 65], lhsT=pe_[:, g_ * 4:(g_ + 1) * 4],
                                           rhs=kv[2][:, h, :], start=True, stop=True, skip_group_check=True)
                    return ins
                kb.op("pe", f, r=[peu, kv[3]], w=PC_u)
                sv_ = sacc[0:4, :].rearrange("p (e c) -> p e c", e=2)
                if ti == 0:
                    kb.op("dve", lambda e, sv_=sv_: e.tensor_copy(out=sv_, in_=PC[0:4, :, 0:260]), r=PC_u, w=[sacc_u])
                else:
                    kb.op("dve", lambda e, sv_=sv_: e.tensor_tensor(out=sv_, in0=PC[0:4, :, 0:260], in1=sv_, op=ALU.add),
                          r=PC_u, w=[sacc_u])
            kb.dma("pool", acc[0][2048 + srow(b, 0):2048 + srow(b, 0) + 4, :], sacc[0:4, :], r=[sacc_u],
                   wa=[acc_u[0]], tag="acc%d" % l)

        if stop == 5:
            kb.finish()
            return nc
        s0t, s0u = tb.next()
        kb.dma("sp", s0t[:], goutS, r=[gout_u], w=[s0u])
        kb.op("dve", lambda e: e.tensor_scalar(out=Sst.rearrange("p h d -> p (h d)"), in0=s0t[:],
                                               scalar1=flag[:, 0:1], scalar2=None, op0=ALU.mult),
              r=[s0u, tab_u], w=[Sst_u])
        kb.op("act", lambda e: e.copy(out=Sb, in_=Sst), r=[Sst_u], w=[Sb_u])

        alias(pm.u, tb2x.u + catT2.u)

        def r2_a(j, tset):
            samp = (j == NPT)
            rows = slice(j * 128, (j + 1) * 128)
            qt, qu = tset.next()
            kt, ku = tset.next()
            vt_, vu_ = tset.next()
            gt, gu = tset.next()
            kb.dma("sp", qt, zs[0][rows, :], r=[zs_u[0]], w=[qu])
            kb.dma("sp", kt, zs[1][rows, :], r=[zs_u[1]], w=[ku])
            kb.dma("sp", vt_, zs[2][rows, :], r=[zs_u[2]], w=[vu_])
            kb.dma("sp", gt, zs[3][rows, :], r=[zs_u[3]], w=[gu])
            qk, qku = hb.next()
            kb.op("dve", lambda e, qk=qk, qt=qt, j=j: e.tensor_tensor(
                out=qk[:, 0:512].rearrange("p (h d) -> p h d", h=4), in0=qt.rearrange("p (h d) -> p h d", h=4),
                in1=decq[:, j, :].unsqueeze(2).broadcast_to([128, 4, 128]), op=ALU.mult),
                r=[qu, tab_u], wa=[qku], tag="qk%d_%d" % (l, j))
            kb.op("pool", lambda e, qk=qk, kt=kt: e.tensor_copy(out=qk[:, 512:1024], in_=kt), r=[ku],
                  wa=[qku], tag="qk%d_%d" % (l, j))
            kd, kdu = kz.next()
            kb.op("pool", lambda e, kd=kd, kt=kt, j=j: e.tensor_tensor(
                out=kd.rearrange("p (h d) -> p h d", h=4), in0=kt.rearrange("p (h d) -> p h d", h=4),
                in1=deck[:, j, :].unsqueeze(2).broadcast_to([128, 4, 128]), op=ALU.mult), r=[ku, tab_u], w=[kdu])
            qT, qTu = qkT.next()
            transpose8(qk, qku, qT, dict(w=[qTu]), eng="act")
            ps, psu = pab.next()

            def f(e, ps=ps, qT=qT):
                ins = None
                for h in range(4):
                    ins = e.matmul(ps[:, h * 128:(h + 1) * 128], lhsT=qT[:, 4 + h, :], rhs=qT[:, h, :],
                                   start=True, stop=True, skip_group_check=True)
                return ins
            kb.op("pe", f, r=[qTu], w=[psu])
            pr, pru = tb.next()
            mk = mrets if samp else mretp
            kb.op("dve", lambda e, pr=pr, ps=ps, mk=mk: e.tensor_tensor(
                out=pr[:, :], in0=ps, in1=mk[:, :, :].rearrange("p h i -> p (h i)"), op=ALU.mult),
                r=[psu, tab_u], w=[pru])
            if samp:
                for b in range(4):
                    kb.op("dve", lambda e, b=b, qT=qT: e.tensor_copy(
                        out=qz[:, b, :, srow(b, 0):srow(b, 0) + 4], in_=qT[:, 0:4, srow(b, 0):srow(b, 0) + 4]),
                        r=[qTu], w=[qz_u])
            return dict(j=j, samp=samp, rows=rows, vt_=vt_, vu_=vu_, gt=gt, gu=gu, kd=kd, kdu=kdu, qT=qT, qTu=qTu,
                        pr=pr, pru=pru)

        def r2_b(c):
            j, samp, rows, vt_, vu_, gt, gu = c["j"], c["samp"], c["rows"], c["vt_"], c["vu_"], c["gt"], c["gu"]
            kd, kdu, qT, qTu, pr, pru = c["kd"], c["kdu"], c["qT"], c["qTu"], c["pr"], c["pru"]
            def state_update(kd=kd, kdu=kdu, vt_=vt_, vu_=vu_, j=j, samp=samp):
                if not samp:
                    def f(e, kd=kd, vt_=vt_):
                        ins = None
                        for h in range(4):
                            ins = e.matmul(PC[:, 0, h * 128:(h + 1) * 128], lhsT=kd[:, h * 128:(h + 1) * 128],
                                           rhs=vt_[:, h * 128:(h + 1) * 128], start=True, stop=True, skip_group_check=True)
                        return ins
                    kb.op("pe", f, r=[kdu, vu_], w=[PC_u[0]])
                    for h in range(4):
                        kb.op("dve", lambda e, h=h: e.scalar_tensor_tensor(
                            out=Sst[:, h, :], in0=Sst[:, h, :], scalar=float(GAM[h] ** 128),
                            in1=PC[:, 0, h * 128:(h + 1) * 128], op0=ALU.mult, op1=ALU.add), r=[PC_u[0]], w=[Sst_u])
                    kb.op("act", lambda e: e.copy(out=Sb, in_=Sst), r=[Sst_u], w=[Sb_u])
                    if j == NPT - 1:
                        kb.dma("pool", pret_o[l], Sst, r=[Sst_u])
                else:
                    for b in range(4):
                        kzb_, kzu = hb.next()
                        kzb = kzb_[:, 0:512]
                        kb.op("dve", lambda e, kzb=kzb, kd=kd, b=b: e.tensor_scalar(
                            out=kzb, in0=kd, scalar1=rowmask[:, b:b + 1], scalar2=None, op0=ALU.mult),
                            r=[kdu, tab_u], w=[kzu])

                        def f(e, kzb=kzb, vt_=vt_):
                            ins = None
                            for h in range(4):
                                ins = e.matmul(PC[:, 0, h * 128:(h + 1) * 128], lhsT=kzb[:, h * 128:(h + 1) * 128],
                                               rhs=vt_[:, h * 128:(h + 1) * 128], start=True, stop=True,
                                               skip_group_check=True)
                            return ins
                        kb.op("pe", f, r=[kzu, vu_], w=[PC_u[0]])
                        s0f, s0fu = t32.next()
                        kb.dma("sp", s0f[:, :].rearrange("p (h e) -> p h e", h=4),
                               sret_in[l, b].rearrange("h d e -> d h e"), w=[s0fu])
                        for h in range(4):
                            kb.op("dve", lambda e, h=h, s0f=s0f: e.scalar_tensor_tensor(
                                out=s0f[:, h * 128:(h + 1) * 128], in0=s0f[:, h * 128:(h + 1) * 128],
                                scalar=float(GAM[h] ** 4), in1=PC[:, 0, h * 128:(h + 1) * 128], op0=ALU.mult, op1=ALU.add),
                                r=[PC_u[0]], w=[s0fu])
                        kb.dma("pool", sret_o[l, b], s0f[:, :].rearrange("p (h e) -> p h e", h=4), r=[s0fu])

            if samp:
                state_update()
            po, pou = pab.next()
            if not samp:
                def f(e, po=po, pr=pr, vt_=vt_, qT=qT):
                    ins = None
                    for h in range(4):
                        o = po[:, h * 128:(h + 1) * 128]
                        e.matmul(o, lhsT=pr[:, h * 128:(h + 1) * 128], rhs=vt_[:, h * 128:(h + 1) * 128],
                                 start=True, stop=False)
                        ins = e.matmul(o, lhsT=qT[:, h, :], rhs=Sb[:, h, :], start=False, stop=True)
                    return ins
                kb.op("pe", f, r=[pru, vu_, qTu, Sb_u], w=[pou])
            else:
                def f(e, po=po, pr=pr, vt_=vt_):
                    ins = None
                    for h in range(4):
                        ins = e.matmul(po[:, h * 128:(h + 1) * 128], lhsT=pr[:, h * 128:(h + 1) * 128],
                                       rhs=vt_[:, h * 128:(h + 1) * 128], start=True, stop=True, skip_group_check=True)
                    return ins
                kb.op("pe", f, r=[pru, vu_], w=[pou])
                oacc, oaccu = t32.next()
                kb.op("act", lambda e, oacc=oacc, po=po: e.copy(out=oacc[:], in_=po), r=[pou], w=[oaccu])
                for b in range(4):
                    sbt, sbu = s0bp.next()
                    kb.dma("pool", sbt, sret_in[l, b].rearrange("h d e -> d h e"), w=[sbu])
                    pq, pqu = pab.next()

                    def f(e, pq=pq, b=b, sbt=sbt):
                        ins = None
                        for h in range(4):
                            ins = e.matmul(pq[0:24, h * 128:(h + 1) * 128], lhsT=qz[:, b, h, :], rhs=sbt[:, h, :],
                                           start=True, stop=True, skip_group_check=True)
                        return ins
                    kb.op("pe", f, r=[qz_u, sbu], w=[pqu])
                    kb.op("dve", lambda e, pq=pq, oacc=oacc: e.tensor_tensor(out=oacc[0:24, :], in0=pq[0:24, :],
                                                                             in1=oacc[0:24, :], op=ALU.add),
                          r=[pqu], w=[oaccu])
            if not samp:
                state_update()
            if samp:
                osb, osu = oacc, oaccu
            else:
                osb, osu = t32.next()
                kb.op("act", lambda e, osb=osb, po=po: e.copy(out=osb[:], in_=po), r=[pou], w=[osu])
            st, su = sm.next()
            for h in range(4):
                kb.op("dve", lambda e, h=h, st=st, osb=osb: e.bn_stats(out=st[:, 6 * h:6 * h + 6],
                                                                        in_=osb[:, h * 128:(h + 1) * 128]),
                      r=[osu], w=[su])
            st2, su2 = sm.next()
            for h in range(4):
                kb.op("dve", lambda e, h=h, st=st, st2=st2: e.bn_aggr(out=st2[:, 2 * h:2 * h + 2],
                                                                      in_=st[:, 6 * h:6 * h + 6]), r=[su], w=[su2])
            s2v = st2[:, 0:8].rearrange("p (h two) -> p h two", two=2)
            kb.op("act", lambda e, st2=st2, s2v=s2v: e.activation(out=st2[:, 8:12], in_=s2v[:, :, 1], func=AF.Sqrt,
                                                                  bias=epsb[:, 0:1]), r=[su2, tab_u], w=[su2])
            kb.op("dve", lambda e, st2=st2: e.reciprocal(out=st2[:, 8:12], in_=st2[:, 8:12]), r=[su2], w=[su2])
            for h in range(4):
                kb.op("dve", lambda e, h=h, osb=osb, st2=st2: e.tensor_scalar(
                    out=osb[:, h * 128:(h + 1) * 128], in0=osb[:, h * 128:(h + 1) * 128],
                    scalar1=st2[:, 2 * h:2 * h + 1], scalar2=st2[:, 8 + h:9 + h], op0=ALU.subtract, op1=ALU.mult),
                    r=[su2], w=[osu])
            sg, sgu = t32.next()
            kb.op("act", lambda e, sg=sg, gt=gt: e.activation(out=sg[:], in_=gt, func=AF.Silu), r=[gu], w=[sgu])
            kb.op("pool", lambda e, osb=osb: e.tensor_tensor(out=osb[:], in0=osb[:], in1=gnw, op=ALU.mult),
                  r=[gnw_u], w=[osu])
            ct, cu = catb.next()
            tgc = "c%d_%d" % (l, j)
            kb.op("pool", lambda e, ct=ct, osb=osb, sg=sg: e.tensor_tensor(out=ct[:, 0:512], in0=osb[:], in1=sg[:],
                                                                          op=ALU.mult), r=[osu, sgu], wa=[cu], tag=tgc)
            a0, a0u = accl.t[0], accl.u[0]
            kb.dma("sp", a0, acc[0][rows, :], r=[acc_u[0]], w=[a0u])
            if not samp:
                a1, a1u = accl.t[1], accl.u[1]
                kb.dma("sp", a1, acc[1][rows, :], r=[acc_u[1]], w=[a1u])
                kb.op("dve", lambda e, a0=a0, a1=a1: e.tensor_tensor(out=a0, in0=a0, in1=a1, op=ALU.add),
                      r=[a1u], w=[a0u])
                a2, a2u = a1, a1u
                kb.dma("sp", a2, acc[2][rows, :], r=[acc_u[2]], w=[a2u])
                kb.op("dve", lambda e, a0=a0, a2=a2: e.tensor_tensor(out=a0, in0=a0, in1=a2, op=ALU.add),
                      r=[a2u], w=[a0u])
            a0v = a0.rearrange("p (e hp c) -> p e hp c", e=2, hp=4)
            rd, rdu = sm.next()
            kb.op("dve", lambda e, rd=rd, a0v=a0v: e.reciprocal(
                out=rd[:, 0:8].rearrange("p (e hp) -> p e hp", e=2), in_=a0v[:, :, :, 64]), r=[a0u], w=[rdu])
            kb.op("dve", lambda e, ct=ct, a0v=a0v, rd=rd: e.tensor_tensor(
                out=ct[:, 512:1024].rearrange("p (hp e d) -> p e hp d", hp=4, e=2), in0=a0v[:, :, :, 0:64],
                in1=rd[:, 0:8].rearrange("p (e hp) -> p e hp", e=2).unsqueeze(3).broadcast_to([128, 2, 4, 64]),
                op=ALU.mult), r=[a0u, rdu], wa=[cu], tag=tgc)
            if dbg and l == 0:
                kb.dma("pool", dbg_cat[rows, :], ct[:], r=[cu])
                kb.dma("pool", dbg_acc[rows, :], a0, r=[a0u])
            cT, cTu = catT2.next()
            transpose8(ct, cu, cT, dict(w=[cTu]), eng="act")
            for hf in range(2):
                ps, psu = pab.next()

                def f(e, ps=ps, cT=cT, hf=hf):
                    ins = None
                    for k in range(8):
                        ins = e.matmul(ps, lhsT=cT[:, k, :], rhs=wo[hf][0][:, k, :], start=(k == 0), stop=(k == 7))
                    return ins
                kb.op("pe", f, r=[cTu, wo[hf][1]], w=[psu])
                kb.op("dve", lambda e, ps=ps, j=j, hf=hf: e.tensor_tensor(
                    out=x[:, j, hf * 512:(hf + 1) * 512], in0=ps, in1=x[:, j, hf * 512:(hf + 1) * 512], op=ALU.add),
                    r=[psu], w=[x_u[j]])

        if stop == 6:
            kb.finish()
            return nc
        tsets = [tb2, tb2x]
        pend = r2_a(0, tsets[0])
        for j in range(NT):
            nxt = r2_a(j + 1, tsets[(j + 1) % 2]) if j + 1 < NT else None
            r2_b(pend)
            pend = nxt
        if dbg and l == 0:
            for j in range(NT):
                kb.dma("pool", dbg_x[j * 128:(j + 1) * 128, :], x[:, j, :], r=[x_u[j]])
        kb.dma("pool", gin2.ap(), x[126:128, NPT - 1, :], r=[x_u[NPT - 1]], w=[gin2_u])
        allgather(gin2.ap(), gin2_u, gout2.ap(), gout2_u)
        alias(hT_u, [xh_u])
        kb.op("dve", lambda e: e.memset(xh, 0.0), w=[xh_u])
        kb.dma("sp", xh[0:2, :], gout2.ap()[0:2, :], r=[gout2_u], w=[xh_u])
        kb.op("dve", lambda e: e.tensor_scalar(out=xh[0:2, :], in0=xh[0:2, :], scalar1=flag[0:2, 0:1], scalar2=None,
                                               op0=ALU.mult), r=[tab_u], w=[xh_u])
        kb.dma("sp", nwb[0][:], n2b[l], w=[nwb_u[0]])
        h_t, h_u = hb.next()
        rmsnorm(xh, xh_u, nwb[1], nwb_u[1], h_t[:], dict(w=[h_u]))
        transpose8(h_t, h_u, hTh[:, :, :], dict(w=[hTh_u]), ncols=2)

        if stop == 7:
            kb.finish()
            return nc
        alias(hT_u + [xh_u], [aT_u])
        alias(arena_units, [wd_u])
        w_down_v = w_down[l].rearrange("(c p) n -> p c n", p=128)
        for c0 in range(0, 22, 2):
            kb.dma("pool", wd[:, c0:c0 + 2, :], w_down_v[:, c0:c0 + 2, :], wa=[wd_u], tag="wd%d" % l)
        w_up_v = w_up[l].rearrange("(k p) n -> p k n", p=128)
        ffn_seq = [(gi_, f2_) for gi_ in range(len(FFN_GROUPS)) for f2_ in range(11)]
        ffn_loaded = {}

        def load_wup(idx):
            if idx >= len(ffn_seq):
                return
            gi_, f2_ = ffn_seq[idx]
            wt_, wu_ = wbuf.next()
            tgw_ = "u%d_%d_%d" % (l, gi_, f2_)
            c_ = f2_ * 256
            for k2 in range(2):
                kb.dma("pool", wt_[:, 4 * k2:4 * k2 + 4, 0:256], w_up_v[:, 4 * k2:4 * k2 + 4, c_:c_ + 256],
                       wa=[wu_], tag=tgw_)
                kb.dma("pool", wt_[:, 4 * k2:4 * k2 + 4, 256:512], w_up_v[:, 4 * k2:4 * k2 + 4, DFF + c_:DFF + c_ + 256],
                       wa=[wu_], tag=tgw_)
            ffn_loaded[idx] = (wt_, wu_)
        load_wup(0)
        for gi, (t0, t1) in enumerate(FFN_GROUPS):
            ntok = (t1 - t0) * 128
            last = (gi == len(FFN_GROUPS) - 1)
            tiles = list(range(t0, t1)) + ([NPT] if last else [])
            for j in tiles:
                h_t, h_u = hb.next()
                rmsnorm(x[:, j, :], x_u[j], nwb[1], nwb_u[1], h_t[:], dict(w=[h_u]))
                if j < NPT:
                    transpose8(h_t, h_u, h2T[:, :, (j - t0) * 128:(j - t0 + 1) * 128],
                               dict(wa=[h2T_u], tag="h2T%d_%d" % (l, gi)), eng="dve")
                else:
                    transpose8(h_t, h_u, h2Ts[:, :, :], dict(w=[h2Ts_u]), ncols=24, eng="dve")
                    kb.op("dve", lambda e: e.memset(h2Ts[:, :, :].rearrange("p k (b s) -> p k b s", b=4)[:, :, :, 0:2],
                                                    0.0), w=[h2Ts_u])
            wins = []
            c = 0
            while c < ntok:
                n_ = min(512, ntok - c)
                wins.append((c, n_))
                c += n_
            for fc in range(22):
                if fc % 2 == 0:
                    wt2, wu = ffn_loaded.pop(gi * 11 + fc // 2)
                    load_wup(gi * 11 + fc // 2 + 1)
                o_ = (fc % 2) * 128
                wt = wt2[:, :, :].rearrange("p k (s c) -> p k s c", s=2)[:, :, :, o_:o_ + 128]
                cs = (fc, 22 + fc)
                if gi == 0:
                    def f(e, wt=wt):
                        ins = None
                        for s_ in range(2):
                            for k in range(8):
                                ins = e.matmul(PC[:, 1, 2 * s_:2 * s_ + 2], lhsT=wt[:, k, s_, :],
                                               rhs=hTh[:, k, :], start=(k == 0), stop=(k == 7), skip_group_check=True)
                        return ins
                    kb.op("pe", f, r=[wu, hTh_u], w=[PC_u[1]])
                    for s_ in range(2):
                        kb.op("act", lambda e, s_=s_, cs=cs: e.copy(out=carry[:, cs[s_], :],
                                                                    in_=PC[:, 1, 2 * s_:2 * s_ + 2]),
                              r=[PC_u[1]], w=[carry_u])
                for (c0, n_) in wins:
                    ug, ugu = pab.next()
                    uv, uvu = pab.next()
                    for s_, (pp, ppu) in enumerate(((ug, ugu), (uv, uvu))):
                        def f(e, pp=pp, s_=s_, wt=wt, c0=c0, n_=n_):
                            ins = None
                            for k in range(8):
                                ins = e.matmul(pp[:, 0:n_], lhsT=wt[:, k, s_, :],
                                               rhs=h2T[:, k, c0:c0 + n_], start=(k == 0), stop=(k == 7))
                            return ins
                        kb.op("pe", f, r=[wu, h2T_u], w=[ppu])
                    cb = []
                    for s_, (pp, ppu) in enumerate(((ug, ugu), (uv, uvu))):
                        ch = cs[s_]
                        A, Au = t32.next()
                        kb.op("act", lambda e, A=A, pp=pp, ch=ch, n_=n_: e.activation(
                            out=A[:, 0:n_], in_=pp[:, 0:n_], func=AF.Identity, scale=cwt[:, ch, 2:3],
                            bias=cwt[:, ch, 3:4]), r=[ppu, cwt_u], w=[Au])
                        cb.append((A, Au))
                    for s_, (pp, ppu) in enumerate(((ug, ugu), (uv, uvu))):
                        ch = cs[s_]
                        A, Au = cb[s_]
                        kb.op("dve", lambda e, A=A, pp=pp, ch=ch, n_=n_: e.scalar_tensor_tensor(
                            out=A[:, 1:n_], in0=pp[:, 0:n_ - 1], scalar=cwt[:, ch, 1:2], in1=A[:, 1:n_],
                            op0=ALU.mult, op1=ALU.add), r=[ppu, cwt_u], w=[Au])
                        kb.op("dve", lambda e, A=A, ch=ch: e.scalar_tensor_tensor(
                            out=A[:, 0:1], in0=carry[:, ch, 1:2], scalar=cwt[:, ch, 1:2], in1=A[:, 0:1],
                            op0=ALU.mult, op1=ALU.add), r=[carry_u, cwt_u], w=[Au])
                        kb.op("dve", lambda e, A=A, pp=pp, ch=ch, n_=n_: e.scalar_tensor_tensor(
                            out=A[:, 2:n_], in0=pp[:, 0:n_ - 2], scalar=cwt[:, ch, 0:1], in1=A[:, 2:n_],
                            op0=ALU.mult, op1=ALU.add), r=[ppu, cwt_u], w=[Au])
                        kb.op("dve", lambda e, A=A, ch=ch: e.scalar_tensor_tensor(
                            out=A[:, 0:2], in0=carry[:, ch, 0:2], scalar=cwt[:, ch, 0:1], in1=A[:, 0:2],
                            op0=ALU.mult, op1=ALU.add), r=[carry_u, cwt_u], w=[Au])
                        kb.op("act", lambda e, pp=pp, ch=ch, n_=n_: e.copy(out=carry[:, ch, :], in_=pp[:, n_ - 2:n_]),
                              r=[ppu], w=[carry_u])
                    kb.op("act", lambda e, A=cb[0][0], n_=n_: e.activation(out=A[:, 0:n_], in_=A[:, 0:n_],
                                                                          func=AF.Silu), w=[cb[0][1]])
                    kb.op("pool", lambda e, Ag=cb[0][0], A=cb[1][0], fc=fc, c0=c0, n_=n_: e.tensor_tensor(
                        out=aT[:, fc, c0:c0 + n_], in0=Ag[:, 0:n_], in1=A[:, 0:n_], op=ALU.mult),
                        r=[cb[0][1], cb[1][1]], wa=[aT_u], tag="aT%d_%d" % (l, gi))
                if last:
                    def f(e, wt=wt):
                        ins = None
                        for s_ in range(2):
                            for k in range(8):
                                ins = e.matmul(PC[:, 1, 32 * s_:32 * s_ + 24], lhsT=wt[:, k, s_, :],
                                               rhs=h2Ts[:, k, :], start=(k == 0), stop=(k == 7), skip_group_check=True)
                        return ins
                    kb.op("pe", f, r=[wu, h2Ts_u], w=[PC_u[1]])
                    cb = []
                    for s_ in range(2):
                        ch = cs[s_]
                        us, usu = sm.next()
                        kb.op("dve", lambda e, us=us, s_=s_: e.tensor_copy(out=us[:, 0:24],
                                                                           in_=PC[:, 1, 32 * s_:32 * s_ + 24]),
                              r=[PC_u[1]], w=[usu])
                        kb.op("dve", lambda e, us=us, ch=ch: e.tensor_copy(
                            out=us[:, 0:24].rearrange("p (b s) -> p b s", b=4)[:, :, 0:2], in_=ctx8[:, ch, :, :]),
                            r=[ctx8_u], w=[usu])
                        A, Au = sm.next()
                        kb.op("act", lambda e, A=A, us=us, ch=ch: e.activation(
                            out=A[:, 0:22], in_=us[:, 2:24], func=AF.Identity, scale=cwt[:, ch, 2:3],
                            bias=cwt[:, ch, 3:4]), r=[usu, cwt_u], w=[Au])
                        kb.op("dve", lambda e, A=A, us=us, ch=ch: e.scalar_tensor_tensor(
                            out=A[:, 0:22], in0=us[:, 1:23], scalar=cwt[:, ch, 1:2], in1=A[:, 0:22],
                            op0=ALU.mult, op1=ALU.add), r=[usu, cwt_u], w=[Au])
                        kb.op("dve", lambda e, A=A, us=us, ch=ch: e.scalar_tensor_tensor(
                            out=A[:, 0:22], in0=us[:, 0:22], scalar=cwt[:, ch, 0:1], in1=A[:, 0:22],
                            op0=ALU.mult, op1=ALU.add), r=[usu, cwt_u], w=[Au])
                        kb.op("act", lambda e, us=us, ch=ch: e.copy(
                            out=uout_s[:, ch, :, :], in_=us[:, 0:24].rearrange("p (b s) -> p b s", b=4)[:, :, 4:6]),
                            r=[usu], wa=[uouts_u], tag="uo%d" % l)
                        cb.append((A, Au))
                    kb.op("act", lambda e, A=cb[0][0]: e.activation(out=A[:, 0:22], in_=A[:, 0:22], func=AF.Silu),
                          w=[cb[0][1]])
                    kb.op("pool", lambda e, Ag=cb[0][0], A=cb[1][0], fc=fc: e.tensor_tensor(
                        out=aTs[:, fc, 2:24], in0=Ag[:, 0:22], in1=A[:, 0:22], op=ALU.mult),
                        r=[cb[0][1], cb[1][1]], wa=[aTs_u], tag="aTs%d" % l)
            for j in tiles:
                for hf in range(2):
                    ps, psu = pab.next()
                    if j < NPT:
                        def f(e, ps=ps, j=j, hf=hf):
                            ins = None
                            for fc in range(22):
                                ins = e.matmul(ps, lhsT=aT[:, fc, (j - t0) * 128:(j - t0 + 1) * 128],
                                               rhs=wd[:, fc, hf * 512:(hf + 1) * 512], start=(fc == 0), stop=(fc == 21))
                            return ins
                        kb.op("pe", f, r=[aT_u, wd_u], w=[psu])
                        kb.op("dve", lambda e, ps=ps, j=j, hf=hf: e.tensor_tensor(
                            out=x[:, j, hf * 512:(hf + 1) * 512], in0=ps, in1=x[:, j, hf * 512:(hf + 1) * 512],
                            op=ALU.add), r=[psu], w=[x_u[j]])
                    else:
                        def f(e, ps=ps, hf=hf):
                            ins = None
                            for fc in range(22):
                                ins = e.matmul(ps[0:24, :], lhsT=aTs[:, fc, :], rhs=wd[:, fc, hf * 512:(hf + 1) * 512],
                                               start=(fc == 0), stop=(fc == 21))
                            return ins
                        kb.op("pe", f, r=[aTs_u, wd_u], w=[psu])
                        kb.op("dve", lambda e, ps=ps, j=j, hf=hf: e.tensor_tensor(
                            out=x[0:24, j, hf * 512:(hf + 1) * 512], in0=ps[0:24, :],
                            in1=x[0:24, j, hf * 512:(hf + 1) * 512], op=ALU.add), r=[psu], w=[x_u[j]])
        if stop == 8:
            kb.finish()
            return nc
        kb.dma("pool", pconv_o[l], carry[:], r=[carry_u])
        kb.dma("pool", sconv_o[l], uout_s[:], r=[uouts_u])

    kb.dma("sp", nwb[0][:], fnb, w=[nwb_u[0]])
    for j in range(NT):
        for hf in range(1):
            pass
        yt, yu = t32.next()
        yt2, yu2 = t32.next()
        st, su = sm.next()
        kb.op("act", lambda e, j=j, st=st: e.activation(out=junk[:, :], in_=x[:, j, :], func=AF.Square,
                                                        accum_out=st[:, 0:1]), r=[x_u[j]], w=[junk_u, su])
        kb.op("act", lambda e, st=st: e.activation(out=st[:, 1:2], in_=st[:, 0:1], func=AF.Sqrt, scale=1.0 / D,
                                                   bias=epsb[:, 0:1]), r=[su, tab_u], w=[su])
        kb.op("dve", lambda e, st=st: e.reciprocal(out=st[:, 2:3], in_=st[:, 1:2]), r=[su], w=[su])
        for hf, (yy, yyu) in enumerate(((yt, yu), (yt2, yu2))):
            kb.op("dve", lambda e, j=j, st=st, yy=yy, hf=hf: e.scalar_tensor_tensor(
                out=yy[:], in0=x[:, j, hf * 512:(hf + 1) * 512], scalar=st[:, 2:3],
                in1=nwb[0][:, hf * 512:(hf + 1) * 512], op0=ALU.mult, op1=ALU.mult),
                r=[x_u[j], su, nwb_u[0]], w=[yyu])
            kb.dma("pool", y_o[j * 128:(j + 1) * 128, hf * 512:(hf + 1) * 512], yy[:], r=[yyu])
    kb.finish()
    return nc


def _tables(half):
    f32 = np.float32
    pos = np.zeros((NT, 128), np.float64)
    for j in range(NPT):
        pos[j] = half * 2048 + 128 * j + np.arange(128)
    for b in range(4):
        for t in range(4):
            pos[NPT, srow(b, t)] = PAST + t
    rope = np.zeros((NT, 128, 192), f32)
    inv_r = (10000.0 ** (-np.arange(64, dtype=np.float32) / np.float32(64))).astype(f32)
    inv_a = (10000.0 ** (-np.arange(32, dtype=np.float32) / np.float32(32))).astype(f32)
    ang_r = pos.astype(f32)[:, :, None] * inv_r[None, None, :]
    ang_a = pos.astype(f32)[:, :, None] * inv_a[None, None, :]
    rope[:, :, 0:64] = np.cos(ang_r.astype(np.float64))
    rope[:, :, 64:128] = np.sin(ang_r.astype(np.float64))
    rope[:, :, 128:160] = np.cos(ang_a.astype(np.float64))
    rope[:, :, 160:192] = np.sin(ang_a.astype(np.float64))
    g = np.array(GAM, np.float64)
    sc = 128.0 ** -0.5
    decq = np.ones((128, NT, 4))
    deck = np.ones((128, NT, 4)) * sc
    p = np.arange(128)
    for j in range(NPT):
        decq[:, j, :] = g[None, :] ** (p[:, None] + 1.0)
        deck[:, j, :] = g[None, :] ** (127.0 - p[:, None]) * sc
    for b in range(4):
        for t in range(4):
            decq[srow(b, t), NPT, :] = g ** (t + 1.0)
            deck[srow(b, t), NPT, :] = g ** (3.0 - t) * sc
    decf = np.zeros((128, NPT, 4))
    for j in range(NPT):
        decf[:, j, :] = g[None, :] ** (2047.0 - (128 * j + p[:, None])) * sc
    mretp = np.zeros((128, 4, 128))
    jj, ii = np.meshgrid(p, p, indexing="ij")
    for h in range(4):
        mretp[:, h, :] = (jj <= ii) * (g[h] ** (-(jj + 1.0))) * sc
    mrets = np.zeros((128, 4, 128))
    rowmask = np.zeros((128, 4))
    for b in range(4):
        for tj in range(4):
            rowmask[srow(b, tj), b] = 1.0
            for ti in range(tj, 4):
                for h in range(4):
                    mrets[srow(b, tj), h, srow(b, ti)] = g[h] ** (-(tj + 1.0)) * sc
    amask = np.zeros((128, 3, 128))
    amask[:, 0, :] = (jj <= ii)
    amask[:, 1, :] = (jj >= ii)
    amask[:, 2, :] = (jj >= ii) * float(half)
    smask = np.zeros((128, 9, 4))
    for t in range(4):
        smask[:, 0, t] = (p >= t)
        smask[:, 1 + t, t] = 1.0
    for b in range(4):
        for tp in range(4):
            for t in range(4):
                smask[srow(b, tp), 5 + b, t] = float(tp <= t) + 2.0 * float(tp == t)
    flag = np.full((128, 1), float(half))
    return dict(rope=rope, decq=decq.astype(f32), deck=deck.astype(f32), decf=decf.astype(f32),
                mretp=mretp.astype(f32), mrets=mrets.astype(f32), rowmask=rowmask.astype(f32),
                amask=amask.astype(f32), smask=smask.astype(f32), flag=flag.astype(f32))


_NC_CACHE = {}


def kernel(x_prompt, x_sample, cache_win_k, cache_win_v, state_ret, state_conv,
           norm1_w, w_in, ret_gn_w, w_out, norm2_w, w_up, conv_w, conv_b, w_down, final_norm_w, _nl=NL, _stop=99, _dbg=False):
    f32 = np.float32
    A = lambda a: np.ascontiguousarray(np.asarray(a, dtype=f32))
    x_prompt, x_sample = A(x_prompt), A(x_sample)
    cache_win_k, cache_win_v = np.asarray(cache_win_k, f32), np.asarray(cache_win_v, f32)
    state_ret, state_conv = np.asarray(state_ret, f32), np.asarray(state_conv, f32)
    if (_nl, _stop, _dbg) not in _NC_CACHE:
        _NC_CACHE[(_nl, _stop, _dbg)] = build(_nl, _stop, _dbg)
    nc = _NC_CACHE[(_nl, _stop, _dbg)]
    shared = dict(
        w_in=A(w_in), w_out=A(w_out), w_up=A(w_up), w_down=A(w_down),
        n1b=A(np.broadcast_to(np.asarray(norm1_w, f32)[:, None, :], (NL, 128, D))),
        n2b=A(np.broadcast_to(np.asarray(norm2_w, f32)[:, None, :], (NL, 128, D))),
        fnb=A(np.broadcast_to(np.asarray(final_norm_w, f32)[None, :], (128, D))),
        gnb=A(np.broadcast_to(np.asarray(ret_gn_w, f32)[:, None, :], (NL, 128, 512))),
    )
    cw = np.concatenate([np.asarray(conv_w, f32), np.asarray(conv_b, f32)[:, None, :]], axis=1)
    shared["convT"] = A(cw.reshape(NL, 4, 44, 128).transpose(0, 3, 2, 1))
    tabs = [_tables(0), _tables(1)]
    in_maps = []
    for c in range(8):
        s, half = c // 2, c % 2
        xin = np.zeros((TOK, D), f32)
        xin[0:2048] = x_prompt[s, half * 2048:(half + 1) * 2048]
        for b in range(4):
            xin[2048 + srow(b, 0):2048 + srow(b, 0) + 4] = x_sample[4 * c + b]
        m = dict(shared)
        m["xin"] = xin
        m["ck"] = A(cache_win_k[:, 4 * c:4 * c + 4].reshape(NL, 4, 2048, 512))
        m["cv"] = A(cache_win_v[:, 4 * c:4 * c + 4].reshape(NL, 4, 2048, 512))
        m["sret_in"] = A(state_ret[:, 4 * c:4 * c + 4])
        sc_ = state_conv[:, 4 * c:4 * c + 4]
        m["sconv_in"] = A(sc_.reshape(NL, 4, 2, 44, 128).transpose(0, 4, 3, 1, 2))
        m.update(tabs[half])
        in_maps.append(m)
    res = run_bass_kernel_spmd(nc, in_maps, core_ids=list(range(8)))
    R = res.results
    y_prompt = np.zeros((4, 4096, D), f32)
    y_sample = np.zeros((32, 4, D), f32)
    p_win_k = np.zeros((NL, 4, 2048, 8, 64), f32)
    p_win_v = np.zeros((NL, 4, 2048, 8, 64), f32)
    p_ret = np.zeros((NL, 4, 4, 128, 128), f32)
    p_conv = np.zeros((NL, 4, 2, 2 * DFF), f32)
    s_win_k = np.zeros((NL, 32, 4, 8, 64), f32)
    s_win_v = np.zeros((NL, 32, 4, 8, 64), f32)
    s_ret = np.zeros((NL, 32, 4, 128, 128), f32)
    s_conv = np.zeros((NL, 32, 2, 2 * DFF), f32)
    for c in range(8):
        s, half = c // 2, c % 2
        r = R[c]
        y_prompt[s, half * 2048:(half + 1) * 2048] = r["y"][0:2048]
        for b in range(4):
            rs = slice(2048 + srow(b, 0), 2048 + srow(b, 0) + 4)
            y_sample[4 * c + b] = r["y"][rs]
            ls = slice(srow(b, 0), srow(b, 0) + 4)
            s_win_k[:, 4 * c + b] = r["sk"][:, ls].reshape(NL, 4, 8, 64)
            s_win_v[:, 4 * c + b] = r["sv"][:, ls].reshape(NL, 4, 8, 64)
            s_ret[:, 4 * c + b] = r["sret"][:, b].transpose(0, 2, 1, 3)
        s_conv[:, 4 * c:4 * c + 4] = r["sconvT"].transpose(0, 3, 4, 2, 1).reshape(NL, 4, 2, 2 * DFF)
        if half == 1:
            p_win_k[:, s] = r["pk"].reshape(NL, 2048, 8, 64)
            p_win_v[:, s] = r["pv"].reshape(NL, 2048, 8, 64)
            p_ret[:, s] = r["pret"].transpose(0, 2, 1, 3)
            p_conv[:, s] = r["pconvT"].transpose(0, 3, 2, 1).reshape(NL, 2, 2 * DFF)
    return (y_prompt, y_sample, p_win_k, p_win_v, p_ret, p_conv, s_win_k, s_win_v, s_ret, s_conv)
```

```python
import math
import numpy as np
import concourse.bass as bass
import concourse.mybir as mybir
from concourse.bass_utils import run_bass_kernel_spmd

F32 = mybir.dt.float32
BF16 = mybir.dt.bfloat16
AF = mybir.ActivationFunctionType
ALU = mybir.AluOpType

D = 1024
DIN = 3584
DFF = 2816
NL = 4
NT = 17
NPT = 16
TOK = NT * 128
EPS = 1e-6
GAM = [1.0 - 2.0 ** (-5 - h) for h in range(4)]
PAST = 8192
GR = 4480
VOFF = 2176
SOFF = 4352
FFN_GROUPS = [(0, 6), (6, 11), (11, 16)]


def srow(b, t):
    return 6 * b + 2 + t


class U:
    __slots__ = ("w", "r", "tag", "const")

    def __init__(self, const=False):
        self.w = []
        self.r = []
        self.tag = None
        self.const = const


class KB:
    def __init__(self, nc):
        self.nc = nc
        self.eng = {"pe": nc.tensor, "act": nc.scalar, "dve": nc.vector, "pool": nc.gpsimd, "sp": nc.sync}
        self.csem = {}
        self.ccnt = {}
        for e in ("pe", "act", "dve", "pool"):
            self.csem[e] = nc.alloc_semaphore(name="c_" + e)
            self.ccnt[e] = 0
        self.dsem = {"sp": [nc.alloc_semaphore(name="dsp%d" % i) for i in range(24)],
                     "pool": [nc.alloc_semaphore(name="dpl%d" % i) for i in range(16)]}
        self.dval = {"sp": [0] * 24, "pool": [0] * 16}
        self.dnext = {"sp": 0, "pool": 0}
        self.seen = {e: {} for e in self.eng}
        self.semobj = {}

    def _wait(self, e, evs):
        need = {}
        for (s, v) in evs:
            k = id(s)
            self.semobj[k] = s
            if need.get(k, 0) < v:
                need[k] = v
        sn = self.seen[e]
        for k, v in need.items():
            if sn.get(k, 0) < v:
                self.eng[e].wait_ge(self.semobj[k], v)
                sn[k] = v

    def _collect(self, r, w, wa, tag):
        evs = []
        for u in r:
            evs += u.w
        for u in w:
            evs += u.w
            evs += u.r
        for u in wa:
            evs += u.r
            if u.tag != tag:
                evs += u.w
        return evs

    def _commit(self, ev, r, w, wa, tag):
        for u in r:
            if not u.const:
                u.r.append(ev)
        for u in w:
            u.w = [ev]
            u.r = []
            u.tag = None
        for u in wa:
            if u.tag != tag:
                u.w = []
                u.tag = tag
            u.w.append(ev)
            u.r = []

    def op(self, e, fn, r=(), w=(), wa=(), tag=None):
        self._wait(e, self._collect(r, w, wa, tag))
        ins = fn(self.eng[e])
        self.ccnt[e] += 1
        ins.then_inc(self.csem[e], 1)
        self._commit((self.csem[e], self.ccnt[e]), r, w, wa, tag)

    def dma(self, e, out, in_, r=(), w=(), wa=(), tag=None, **kw):
        evs = self._collect(r, w, wa, tag)
        k = self.dnext[e]
        self.dnext[e] = (k + 1) % len(self.dsem[e])
        sem = self.dsem[e][k]
        if self.dval[e][k] > 0:
            evs.append((sem, self.dval[e][k]))
        self._wait(e, evs)
        self.dval[e][k] += 16
        self.eng[e].dma_start(out=out, in_=in_, **kw).then_inc(sem, 16)
        self._commit((sem, self.dval[e][k]), r, w, wa, tag)

    def finish(self):
        evs = []
        for e in ("sp", "pool"):
            for s, v in zip(self.dsem[e], self.dval[e]):
                if v > 0:
                    evs.append((s, v))
        for e in ("pe", "act", "dve", "pool"):
            if self.ccnt[e] > 0:
                evs.append((self.csem[e], self.ccnt[e]))
        for e in ("sp", "act", "dve", "pe", "pool"):
            self._wait(e, evs)


class Rot:
    def __init__(self, tensors):
        self.t = tensors
        self.u = [U() for _ in tensors]
        self.i = 0

    def next(self):
        k = self.i
        self.i = (k + 1) % len(self.t)
        return self.t[k], self.u[k]


def build(nl=NL, stop=99, dbg=False):
    nc = bass.Bass("TRN2", target_bir_lowering=False)
    kb = KB(nc)

    def din(name, shape, dt=F32):
        return nc.dram_tensor(name, list(shape), dt, kind="ExternalInput").ap()

    def dout(name, shape, dt=F32):
        return nc.dram_tensor(name, list(shape), dt, kind="ExternalOutput").ap()

    def dscr(name, shape, dt):
        return nc.dram_tensor(name, list(shape), dt)

    xin = din("xin", [TOK, D])
    w_in = din("w_in", [NL, D, DIN])
    w_out = din("w_out", [NL, D, D])
    w_up = din("w_up", [NL, D, 2 * DFF])
    w_down = din("w_down", [NL, DFF, D])
    n1b = din("n1b", [NL, 128, D])
    n2b = din("n2b", [NL, 128, D])
    fnb = din("fnb", [128, D])
    gnb = din("gnb", [NL, 128, 512])
    convT = din("convT", [NL, 128, 44, 4])
    sconv_in = din("sconv_in", [NL, 128, 44, 4, 2])
    sret_in = din("sret_in", [NL, 4, 4, 128, 128])
    ck = din("ck", [NL, 4, 2048, 512])
    cv = din("cv", [NL, 4, 2048, 512])
    rope_d = din("rope", [NT, 128, 192])
    decq_d = din("decq", [128, NT, 4])
    deck_d = din("deck", [128, NT, 4])
    decf_d = din("decf", [128, NPT, 4])
    mretp_d = din("mretp", [128, 4, 128])
    mrets_d = din("mrets", [128, 4, 128])
    rowmask_d = din("rowmask", [128, 4])
    amask_d = din("amask", [128, 3, 128])
    smask_d = din("smask", [128, 9, 4])
    flag_d = din("flag", [128, 1])

    y_o = dout("y", [TOK, D])
    pk_o = dout("pk", [NL, 2048, 512])
    pv_o = dout("pv", [NL, 2048, 512])
    sk_o = dout("sk", [NL, 128, 512])
    sv_o = dout("sv", [NL, 128, 512])
    pret_o = dout("pret", [NL, 128, 4, 128])
    pconv_o = dout("pconvT", [NL, 128, 44, 2])
    sret_o = dout("sret", [NL, 4, 128, 4, 128])
    sconv_o = dout("sconvT", [NL, 128, 44, 4, 2])
    if dbg:
        dbg_cat = dout("dbg_cat", [TOK, D], BF16)
        dbg_x = dout("dbg_x", [TOK, D])
        dbg_acc = dout("dbg_acc", [TOK, 520])

    zs = [dscr("zs%d" % g, [TOK, 512], BF16).ap() for g in range(5)]
    zs_u = [U() for _ in range(5)]
    gK = dscr("gK", [2176, 512], BF16)
    gKo = [dscr("gKo%d" % i, [2176, 512], BF16) for i in range(2)]
    rK = dscr("rK", [2176, 512], BF16)
    gV = dscr("gV", [2176, 520], BF16)
    gVo = [dscr("gVo%d" % i, [2176, 520], BF16) for i in range(2)]
    rV = dscr("rV", [2176, 520], BF16)
    rkv_u = U()
    gS = dscr("gS", [128, 512], BF16)
    gSo = dscr("gSo", [256, 512], BF16)
    gin_u = U()
    gout_u = U()
    gin2 = dscr("gin2", [2, D], F32)
    gout2 = dscr("gout2", [4, D], F32)
    gin2_u = U()
    gout2_u = U()
    acc = [dscr("acc%d" % i, [TOK, 520], F32).ap() for i in range(3)]
    acc_u = [U() for _ in range(3)]
    ccsem = nc.alloc_semaphore(name="ccsem")
    cc_cnt = [0]

    def sb(name, shape, dt):
        return nc.alloc_sbuf_tensor("sb_" + name, list(shape), dt)

    x = sb("x", [128, NT, D], F32)
    x_u = [U() for _ in range(NT)]
    BIGN = max(8 * TOK, 22 * 768)
    big = sb("big", [128, BIGN], BF16)
    hT = big[:, 0:8 * TOK].rearrange("p (k t) -> p k t", k=8)
    hT_u = [U() for _ in range(NT)]
    aT = big[:, 0:22 * 768].rearrange("p (c t) -> p c t", c=22)
    aT_u = U()
    xh = big[:, 0:2 * D].bitcast(F32)
    xh_u = U()
    arena = sb("arena", [128, 22 * D], BF16)
    wd = arena[:, :].rearrange("p (c n) -> p c n", c=22)
    wd_u = U()
    a_off = [0]
    arena_units = []

    def carve(shape, dt):
        n = 1
        for s_ in shape[1:]:
            n *= s_
        nb = n * (2 if dt == BF16 else 4) // 2
        nb = (nb + 15) // 16 * 16
        o = a_off[0]
        a_off[0] += nb
        assert a_off[0] <= 22 * D, a_off[0]
        ap = arena[:, o:o + nb]
        if dt == F32:
            ap = ap.bitcast(F32)
        ap = ap[:, 0:n]
        if len(shape) == 3:
            ap = ap.rearrange("p (a b) -> p a b", a=shape[1])
        elif len(shape) == 4:
            ap = ap.rearrange("p (a b c) -> p a b c", a=shape[1], b=shape[2])
        return ap

    def arot(n, shape, dt):
        r_ = Rot([carve(shape, dt) for _ in range(n)])
        arena_units.extend(r_.u)
        return r_

    def aunit():
        u = U()
        arena_units.append(u)
        return u

    ident = sb("ident", [128, 128], BF16)
    ident_u = U(const=True)
    nwb = [sb("nwb0", [128, D], F32)]
    nwb.append(nwb[0])
    nwb_u = [U()]
    nwb_u.append(nwb_u[0])
    cwt = sb("cwt", [128, 44, 4], F32)
    cwt_u = U()
    decq = sb("decq", [128, NT, 4], F32)
    deck = sb("deck", [128, NT, 4], F32)
    decf = sb("decf", [128, NPT, 4], F32)
    mretp = sb("mretp", [128, 4, 128], BF16)
    mrets = sb("mrets", [128, 4, 128], BF16)
    rowmask = sb("rowmask", [128, 4], F32)
    amask = sb("amask", [128, 3, 128], BF16)
    smask = sb("smask", [128, 9, 4], BF16)
    flag = sb("flag", [128, 1], F32)
    epsb = sb("epsb", [128, 1], F32)
    tab_u = U(const=True)
    carry = sb("carry", [128, 44, 2], F32)
    carry_u = U()
    uout_s = sb("uout_s", [128, 44, 4, 2], F32)
    uouts_u = U()
    ctx8 = sb("ctx8", [128, 44, 4, 2], F32)
    ctx8_u = U()
    aTs = sb("aTs", [128, 22, 24], BF16)
    aTs_u = U()
    hTh = sb("hTh", [128, 8, 2], BF16)
    hTh_u = U()
    h2Ts = sb("h2Ts", [128, 8, 24], BF16)
    h2Ts_u = U()

    def rot(name, n, shape, dt):
        return Rot([sb("%s%d" % (name, i), shape, dt) for i in range(n)])

    wbuf = rot("wbuf", 2, [128, 8, 512], BF16)
    wbufF = Rot([wbuf.t[0][:, :, 0:256], wbuf.t[0][:, :, 256:512], wbuf.t[1][:, :, 0:256], wbuf.t[1][:, :, 256:512]])
    t32 = rot("t32", 4, [128, 512], F32)
    t32b = t32
    tb = rot("tb", 3, [128, 512], BF16)
    hb = rot("hb", 2, [128, D], BF16)
    junk = sb("junk", [128, D], BF16)
    junk_u = U()
    sm = rot("sm", 6, [128, 32], F32)
    pes = rot("pes", 2, [128, 32], BF16)
    h2T = sb("h2T", [128, 8, 768], BF16)
    h2T_u = U()

    gnw = carve([128, 512], F32)
    gnw_u = aunit()
    Sst = carve([128, 4, 128], F32)
    Sst_u = aunit()
    Sb = carve([128, 4, 128], BF16)
    Sb_u = aunit()
    s0bp = arot(2, [128, 4, 128], BF16)
    qz = carve([128, 4, 4, 24], BF16)
    qz_u = aunit()
    tb2 = arot(4, [128, 512], BF16)
    ropet = arot(2, [128, 192], F32)
    v65 = arot(5, [128, 8, 65], BF16)
    qkT = arot(2, [128, 8, 128], BF16)
    catT = qkT
    catb = hb
    kTs = arot(3, [128, 4, 128], BF16)
    qTp = arot(2, [128, 4, 128], BF16)
    pm = arot(2, [128, 2, 1024], BF16)
    ob = arot(2, [128, 520], F32)
    accl = ob
    kz = arot(2, [128, 512], BF16)
    _pm0 = pm.t[0].rearrange("p t c -> p (t c)")
    _pm1 = pm.t[1].rearrange("p t c -> p (t c)")
    tb2x = Rot([_pm0[:, i * 512:(i + 1) * 512] for i in range(4)])
    catT2 = Rot([_pm1[:, i * 1024:(i + 1) * 1024].rearrange("p (k t) -> p k t", k=8) for i in range(2)])
    arena_units.extend(tb2x.u + catT2.u)
    sacc = carve([128, 520], F32)
    sacc_u = aunit()

    PA = nc.alloc_psum_tensor("PA", [128, 2, 512], F32)
    PB = nc.alloc_psum_tensor("PB", [128, 2, 512], F32)
    PC = nc.alloc_psum_tensor("PC", [128, 2, 512], F32)
    PT = [nc.alloc_psum_tensor("PT%d" % i, [128, 1024], BF16) for i in range(2)]
    PA_u = [U(), U()]
    PB_u = [U(), U()]
    PC_u = [U(), U()]
    PT_u = [U(), U()]
    pt_i = [0]
    pab = Rot([PA[:, 0, :], PA[:, 1, :], PB[:, 0, :], PB[:, 1, :]])
    pab.u = [PA_u[0], PA_u[1], PB_u[0], PB_u[1]]

    def next_pt():
        k = pt_i[0]
        pt_i[0] = 1 - k
        return PT[k], PT_u[k]

    def alias(frm, to):
        evr = []
        evw = []
        for u in frm:
            evr += u.r
            evw += u.w
        for u in to:
            u.r = u.r + evr
            u.w = u.w + evw

    kb.op("pool", lambda e: e.memset(ident[:], 1.0), w=[ident_u])
    kb.op("pool", lambda e: e.affine_select(out=ident[:], in_=ident[:], pattern=[[-1, 128]],
                                            compare_op=ALU.is_equal, fill=0.0, base=0, channel_multiplier=1),
          w=[ident_u])
    for t_, d_ in ((decq, decq_d), (deck, deck_d), (decf, decf_d), (rowmask, rowmask_d), (flag, flag_d)):
        kb.dma("sp", t_[:], d_, w=[tab_u])
    for t_, d_ in ((amask, amask_d), (smask, smask_d), (mretp, mretp_d), (mrets, mrets_d)):
        kb.dma("pool", t_[:], d_, w=[tab_u])
    for j in range(NT):
        kb.dma("sp", x[:, j, :], xin[j * 128:(j + 1) * 128, :], w=[x_u[j]])
    kb.op("dve", lambda e: e.memset(aTs[:], 0.0), w=[aTs_u])
    kb.op("dve", lambda e: e.memset(epsb[:], EPS), w=[tab_u])
    kb.op("dve", lambda e: e.memset(h2Ts[:], 0.0), w=[h2Ts_u])
    ones_t, ones_u = t32.next()
    kb.op("dve", lambda e: e.memset(ones_t[:], 1.0), w=[ones_u])
    kb.dma("sp", acc[0][2048:2176, 0:512], ones_t[:], r=[ones_u], wa=[acc_u[0]], tag="init")
    kb.dma("sp", acc[0][2048:2176, 8:520], ones_t[:], r=[ones_u], wa=[acc_u[0]], tag="init")

    def rmsnorm(xt_ap, xu, wt, wu, out_ap, out_kw, np_=128):
        st, su = sm.next()
        kb.op("act", lambda e: e.activation(out=junk[0:np_, :], in_=xt_ap, func=AF.Square,
                                            accum_out=st[0:np_, 0:1]), r=[xu], w=[junk_u, su])
        kb.op("act", lambda e: e.activation(out=st[0:np_, 1:2], in_=st[0:np_, 0:1], func=AF.Sqrt, scale=1.0 / D,
                                            bias=epsb[0:np_, 0:1]), r=[su, tab_u], w=[su])
        kb.op("dve", lambda e: e.reciprocal(out=st[0:np_, 2:3], in_=st[0:np_, 1:2]), r=[su], w=[su])
        kb.op("dve", lambda e: e.scalar_tensor_tensor(out=out_ap, in0=xt_ap, scalar=st[0:np_, 2:3],
                                                      in1=wt[0:np_, :], op0=ALU.mult, op1=ALU.mult),
              r=[xu, su, wu], **out_kw)

    def transpose8(src, src_u, dst_ap, dst_kw, ncols=128, eng="act"):
        pt, ptu = next_pt()

        def f(e):
            ins = None
            for k in range(8):
                ins = e.transpose(out=pt[:, k * 128:(k + 1) * 128], in_=src[:, k * 128:(k + 1) * 128],
                                  identity=ident[:])
            return ins
        kb.op("pe", f, r=[src_u, ident_u], w=[ptu])
        pv_ = pt[:, :].rearrange("p (k t) -> p k t", k=8)[:, :, 0:ncols]
        if eng == "act":
            kb.op("act", lambda e: e.copy(out=dst_ap, in_=pv_), r=[ptu], **dst_kw)
        else:
            kb.op("dve", lambda e: e.tensor_copy(out=dst_ap, in_=pv_), r=[ptu], **dst_kw)

    def rope_apply(zt, zu, j, half, H, out_ap, out_u):
        rt, ru = ropet.next()
        kb.dma("sp", rt[:], rope_d[j], w=[ru])
        c0 = 0 if half == 64 else 128
        cos = rt[:, c0:c0 + half]
        sin = rt[:, c0 + half:c0 + 2 * half]
        zv = zt.rearrange("p (h two d) -> p h two d", h=H, two=2)
        ov = out_ap.rearrange("p (h two d) -> p h two d", h=H, two=2)
        t1, t1u = t32.next()
        t1v = t1[:, :].rearrange("p (h two d) -> p h two d", h=H, two=2)
        m, mu = t32.next()
        mv = m[:, :].rearrange("p (two h d) -> p two h d", two=2, h=H)
        cosb = cos.unsqueeze(1).unsqueeze(1).broadcast_to([128, H, 2, half])
        sinb = sin.unsqueeze(1).broadcast_to([128, H, half])
        kb.op("dve", lambda e: e.tensor_tensor(out=t1v, in0=zv, in1=cosb, op=ALU.mult), r=[zu, ru], w=[t1u])
        kb.op("dve", lambda e: e.tensor_tensor(out=mv[:, 0], in0=zv[:, :, 1, :], in1=sinb, op=ALU.mult),
              r=[zu, ru], wa=[mu], tag="m")
        kb.op("dve", lambda e: e.tensor_tensor(out=mv[:, 1], in0=zv[:, :, 0, :], in1=sinb, op=ALU.mult),
              r=[zu, ru], wa=[mu], tag="m")
        kb.op("dve", lambda e: e.tensor_tensor(out=ov[:, :, 0, :], in0=t1v[:, :, 0, :], in1=mv[:, 0],
                                               op=ALU.subtract), r=[t1u, mu], wa=[out_u], tag="o")
        kb.op("dve", lambda e: e.tensor_tensor(out=ov[:, :, 1, :], in0=t1v[:, :, 1, :], in1=mv[:, 1],
                                               op=ALU.add), r=[t1u, mu], wa=[out_u], tag="o")

    def allgather(src, src_u, dst, dst_u):
        evs = kb._collect([src_u], [dst_u], (), None)
        kb._wait("pool", evs)
        cc_cnt[0] += 1
        nc.gpsimd.collective_compute("AllGather", ALU.bypass, replica_groups=[[0, 1], [2, 3], [4, 5], [6, 7]],
                                     ins=[src], outs=[dst]).then_inc(ccsem)
        kb._commit((ccsem, cc_cnt[0]), [src_u], [dst_u], (), None)

    ginK = gK.ap()
    ginV = gV.ap()
    ginS = gS.ap()
    goutK = rK.ap()
    goutV = rV.ap()
    goutS = gSo.ap()[0:128, :]

    for l in range(nl):
        alias([wd_u], arena_units)
        alias([aT_u, xh_u], hT_u)
        kb.dma("sp", nwb[0][:], n1b[l], w=[nwb_u[0]])
        kb.dma("sp", gnw, gnb[l], w=[gnw_u])
        kb.dma("sp", cwt[:], convT[l], w=[cwt_u])
        kb.dma("sp", ctx8[:], sconv_in[l], w=[ctx8_u])
        kb.op("dve", lambda e: e.memset(qz, 0.0), w=[qz_u])
        for i in range(len(v65.t)):
            kb.op("pool", lambda e, i=i: e.memset(v65.t[i], 1.0), w=[v65.u[i]])

        if stop == 0:
            kb.finish()
            return nc
        for j in range(NT):
            h_t, h_u = hb.next()
            rmsnorm(x[:, j, :], x_u[j], nwb[0], nwb_u[0], h_t[:], dict(w=[h_u]))
            transpose8(h_t, h_u, hT[:, :, j * 128:(j + 1) * 128], dict(w=[hT_u[j]]))

        if stop == 1:
            kb.finish()
            return nc
        w_in_v = w_in[l].rearrange("(k p) n -> p k n", p=128)
        def load_win(g):
            wt, wu = wbuf.next()
            for k2 in range(4):
                kb.dma("pool", wt[:, 2 * k2:2 * k2 + 2, :], w_in_v[:, 2 * k2:2 * k2 + 2, g * 512:(g + 1) * 512],
                       wa=[wu], tag="ld%d_%d" % (l, g))
            return wt, wu
        win_next = load_win(0)
        for g in range(7):
            wt, wu = win_next
            if g + 1 < 7:
                win_next = load_win(g + 1)
            for j in range(NT):
                ps, psu = pab.next()

                def f(e, ps=ps, wt=wt, j=j):
                    ins = None
                    for k in range(8):
                        ins = e.matmul(ps, lhsT=hT[:, k, j * 128:(j + 1) * 128], rhs=wt[:, k, :],
                                       start=(k == 0), stop=(k == 7))
                    return ins
                kb.op("pe", f, r=[hT_u[j], wu], w=[psu])
                rows = slice(j * 128, (j + 1) * 128)
                if g in (0, 1, 4):
                    o, ou = tb.next()
                    rope_apply(ps, psu, j, 64 if g < 4 else 32, 4 if g < 4 else 8, o[:, :], ou)
                    kb.dma("pool", zs[g][rows, :], o[:], r=[ou], wa=[zs_u[g]], tag="z%d" % l)
                elif g in (2, 3):
                    o, ou = tb.next()
                    kb.op("act", lambda e, o=o, ps=ps: e.copy(out=o[:], in_=ps), r=[psu], w=[ou])
                    kb.dma("pool", zs[g][rows, :], o[:], r=[ou], wa=[zs_u[g]], tag="z%d" % l)
                elif g == 5:
                    o, ou = t32.next()
                    rope_apply(ps, psu, j, 32, 8, o[:, :], ou)
                    if j < NPT:
                        kb.dma("pool", pk_o[l, rows, :], o[:], r=[ou])
                    else:
                        kb.dma("pool", sk_o[l], o[:], r=[ou])
                    ob_, obu = tb.next()
                    kb.op("act", lambda e, ob_=ob_, o=o: e.copy(out=ob_[:], in_=o[:]), r=[ou], w=[obu])
                    kb.dma("pool", ginK[rows, :], ob_[:], r=[obu], wa=[gin_u], tag="g%d" % l)
                else:
                    z, zu = t32.next()
                    kb.op("act", lambda e, z=z, ps=ps: e.copy(out=z[:], in_=ps), r=[psu], w=[zu])
                    if j < NPT:
                        kb.dma("pool", pv_o[l, rows, :], z[:], r=[zu])
                    else:
                        kb.dma("pool", sv_o[l], z[:], r=[zu])
                    vt, vu = v65.next()
                    kb.op("dve", lambda e, vt=vt, z=z: e.tensor_copy(
                        out=vt[:, :, 0:64], in_=z[:, :].rearrange("p (h d) -> p h d", h=8)), r=[zu], w=[vu])
                    kb.dma("pool", ginV[rows, :], vt.rearrange("p h d -> p (h d)"), r=[vu],
                           wa=[gin_u], tag="g%d" % l)

        if stop == 2:
            kb.finish()
            return nc
        for j in range(NPT):
            rows = slice(j * 128, (j + 1) * 128)
            kt, ku = tb.next()
            vt_, vu_ = tb2.next()
            kb.dma("sp", kt[:], zs[1][rows, :], r=[zs_u[1]], w=[ku])
            kb.dma("sp", vt_, zs[2][rows, :], r=[zs_u[2]], w=[vu_])
            kd, kdu = kz.next()
            kb.op("dve", lambda e, kd=kd, kt=kt, j=j: e.tensor_tensor(
                out=kd.rearrange("p (h d) -> p h d", h=4), in0=kt[:, :].rearrange("p (h d) -> p h d", h=4),
                in1=decf[:, j, :].unsqueeze(2).broadcast_to([128, 4, 128]), op=ALU.mult), r=[ku, tab_u], w=[kdu])

            def f(e, kd=kd, vt_=vt_, j=j):
                ins = None
                for h in range(4):
                    ins = e.matmul(PC[:, 0, h * 128:(h + 1) * 128], lhsT=kd[:, h * 128:(h + 1) * 128],
                                   rhs=vt_[:, h * 128:(h + 1) * 128], start=(j == 0), stop=(j == NPT - 1),
                                   skip_group_check=True)
                return ins
            if j == 0:
                kb.op("pe", f, r=[kdu, vu_], w=[PC_u[0]])
            else:
                kb.op("pe", f, r=[kdu, vu_], wa=[PC_u[0]], tag=None)
        sl, slu = tb.next()
        kb.op("act", lambda e: e.copy(out=sl[:], in_=PC[:, 0, :]), r=[PC_u[0]], w=[slu])
        kb.dma("pool", ginS, sl[:], r=[slu], wa=[gin_u], tag="g%d" % l)

        for hh in range(2):
            allgather(gK.ap()[hh * 1088:(hh + 1) * 1088, :], gin_u, gKo[hh].ap(), gout_u)
            allgather(gV.ap()[hh * 1088:(hh + 1) * 1088, :], gin_u, gVo[hh].ap(), gout_u)
        allgather(gS.ap(), gin_u, gSo.ap(), gout_u)
        for hh in range(2):
            kb.dma("pool", rK.ap()[hh * 1088:(hh + 1) * 1088, :], gKo[hh].ap()[0:1088, :], r=[gout_u],
                   wa=[rkv_u], tag="rkv%d" % l)
            kb.dma("pool", rV.ap()[hh * 1088:(hh + 1) * 1088, :], gVo[hh].ap()[0:1088, :], r=[gout_u],
                   wa=[rkv_u], tag="rkv%d" % l)

        if stop == 3:
            kb.finish()
            return nc
        wo = []
        w_out_v = w_out[l].rearrange("(k p) n -> p k n", p=128)
        for hf in range(2):
            wt, wu = wbuf.next()
            for k2 in range(4):
                kb.dma("pool", wt[:, 2 * k2:2 * k2 + 2, :], w_out_v[:, 2 * k2:2 * k2 + 2, hf * 512:(hf + 1) * 512],
                       wa=[wu], tag="wo%d" % l)
            wo.append((wt, wu))

        def load_kv(rows_ap_k, rows_ap_v, srcs_u, cast=False):
            kt, ku = tb2.next()
            q = "pool" if cast else "sp"
            kb.dma(q, kt, rows_ap_k, r=srcs_u, w=[ku])
            vt, vu = v65.next()
            if cast:
                kb.dma(q, vt[:, :, 0:64], rows_ap_v.rearrange("p (h d) -> p h d", h=8), r=srcs_u, w=[vu])
            else:
                kb.dma(q, vt.rearrange("p h d -> p (h d)"), rows_ap_v, r=srcs_u, w=[vu])
            pt, ptu = next_pt()

            def f(e):
                ins = None
                for k in range(4):
                    ins = e.transpose(out=pt[:, k * 128:(k + 1) * 128], in_=kt[:, k * 128:(k + 1) * 128],
                                      identity=ident[:])
                return ins
            kb.op("pe", f, r=[ku, ident_u], w=[ptu])
            kT, kTu = kTs.next()
            kb.op("act", lambda e: e.copy(out=kT, in_=pt[:, 0:512].rearrange("p (k t) -> p k t", k=4)),
                  r=[ptu], w=[kTu])
            return (kT, kTu, vt, vu)

        def load_q(rows_ap):
            qt, qu = tb2.next()
            kb.dma("sp", qt, rows_ap, r=[zs_u[4]], w=[qu])
            pt, ptu = next_pt()

            def f(e):
                ins = None
                for k in range(4):
                    ins = e.transpose(out=pt[:, k * 128:(k + 1) * 128], in_=qt[:, k * 128:(k + 1) * 128],
                                      identity=ident[:])
                return ins
            kb.op("pe", f, r=[qu, ident_u], w=[ptu])
            qT, qTu = qTp.next()
            kb.op("dve", lambda e: e.tensor_copy(out=qT, in_=pt[:, 0:512].rearrange("p (k t) -> p k t", k=4)),
                  r=[ptu], w=[qTu])
            return qT, qTu

        def scores_exp(kT, kTu, qT, qTu, ncol, P, P_u, mask_ap, pm_ap, pm_kw, q0=0, meng="pool"):
            def f(e):
                ins = None
                for hp in range(4):
                    for ee in range(2):
                        ins = e.matmul(P[:, ee, hp * ncol:(hp + 1) * ncol],
                                       lhsT=kT[64 * ee:64 * ee + 64, hp, :],
                                       rhs=qT[64 * ee:64 * ee + 64, hp, q0:q0 + ncol], start=True, stop=True,
                                       skip_group_check=True)
                return ins
            kb.op("pe", f, r=[kTu, qTu], w=P_u)
            pev = pm_ap.rearrange("p (e c) -> p e c", e=2)
            for ee in range(2):
                kb.op("act", lambda e, ee=ee: e.activation(out=pev[:, ee, :], in_=P[:, ee, 0:4 * ncol],
                                                           func=AF.Exp, scale=0.125),
                      r=[P_u[ee]], **pm_kw)
            pg = pm_ap.rearrange("p (g c) -> p g c", g=8)
            kb.op(meng, lambda e: e.tensor_tensor(
                out=pg, in0=pg, in1=mask_ap.unsqueeze(1).broadcast_to([128, 8, ncol]), op=ALU.mult),
                r=[tab_u], w=[pm_kw["wa"][0]])

        alias(tb2x.u + catT2.u, pm.u)
        blocks = []
        for pi, dil in enumerate((1, 4, 16)):
            span = 128 * dil
            nblk = 2048 // span
            for r_ in range(dil):
                for n in range(nblk):
                    blocks.append((pi, dil, span, nblk, r_, n))
        st_prev = [None]

        def stage_a(blk):
            pi, dil, span, nblk, r_, n = blk
            r0 = n * span + r_
            rs = slice(r0, r0 + 128 * dil, dil) if dil > 1 else slice(r0, r0 + 128)
            if n == 0:
                p0 = (nblk - 1) * span + r_
                ps_ = slice(p0, p0 + 128 * dil, dil) if dil > 1 else slice(p0, p0 + 128)
                prev = load_kv(goutK[ps_, :], goutV[ps_, :], [rkv_u])
            else:
                prev = st_prev[0]
            cur = load_kv(ginK[rs, :], ginV[rs, :], [gin_u])
            qT, qTu = load_q(zs[4][rs, :])
            pmt, pmu = pm.next()
            tg = "pm%d_%d_%d_%d" % (l, pi, r_, n)
            scores_exp(prev[0], prev[1], qT, qTu, 128, PA, PA_u, amask[:, 2 if n == 0 else 1, :],
                       pmt[:, 0, :], dict(wa=[pmu], tag=tg), meng="dve")
            scores_exp(cur[0], cur[1], qT, qTu, 128, PB, PB_u, amask[:, 0, :],
                       pmt[:, 1, :], dict(wa=[pmu], tag=tg), meng="pool")
            st_prev[0] = cur
            return (pi, rs, prev, cur, pmt, pmu)

        def stage_b(st):
            pi, rs, prev, cur, pmt, pmu = st
            pmv = pmt.rearrange("p t (g c) -> p t g c", g=8)

            def f(e):
                ins = None
                for hp in range(4):
                    for ee in range(2):
                        h = 2 * hp + ee
                        g_ = ee * 4 + hp
                        o = PC[:, ee, hp * 65:(hp + 1) * 65]
                        e.matmul(o, lhsT=pmv[:, 0, g_, :], rhs=prev[2][:, h, :], start=True, stop=False)
                        ins = e.matmul(o, lhsT=pmv[:, 1, g_, :], rhs=cur[2][:, h, :], start=False, stop=True)
                return ins
            kb.op("pe", f, r=[pmu, prev[3], cur[3]], w=PC_u)
            o_, ou_ = ob.next()
            kb.op("act", lambda e: e.copy(out=o_.rearrange("p (e c) -> p e c", e=2), in_=PC[:, :, 0:260]),
                  r=PC_u, w=[ou_])
            kb.dma("pool", acc[pi][rs, :], o_, r=[ou_], wa=[acc_u[pi]], tag="acc%d" % l)

        pend = stage_a(blocks[0])
        for bi in range(len(blocks)):
            nxt = stage_a(blocks[bi + 1]) if bi + 1 < len(blocks) else None
            stage_b(pend)
            pend = nxt

        if stop == 4:
            kb.finish()
            return nc
        sq_rows = slice(2048, 2176)
        qTs, qTsu = load_q(zs[4][sq_rows, :])
        for b in range(4):
            specs = [(slice(1920, 2048), 0)]
            specs += [(slice(1536 + r_, 2048, 4), 1 + r_) for r_ in range(4)]
            specs += [(slice(r_, 2048, 16), 1 + r_) for r_ in range(4)]
            specs += [(None, 5 + b)]
            for ti, (rsl, mi) in enumerate(specs):
                if rsl is None:
                    kv = load_kv(ginK[sq_rows, :], ginV[sq_rows, :], [gin_u])
                else:
                    kv = load_kv(ck[l, b, rsl, :], cv[l, b, rsl, :], [], cast=True)
                pe_, peu = pes.next()
                scores_exp(kv[0], kv[1], qTs, qTsu, 4, PA, PA_u, smask[:, mi, :],
                           pe_[:, :], dict(wa=[peu], tag="s%d_%d_%d" % (l, b, ti)), q0=srow(b, 0), meng="dve")

                def f(e, kv=kv, pe_=pe_):
                    ins = None
                    for hp in range(4):
                        for ee in range(2):
                            h = 2 * hp + ee
                            g_ = ee * 4 + hp
                            ins = e.matmul(PC[0:4, ee, hp * 65:(hp + 1) * 65], lhsT=pe_[:, g_ * 4:(g_ + 1) * 4],
                                           rhs=kv[2][:, h, :], start=True, stop=True, skip_group_check=True)
                    return ins
                kb.op("pe", f, r=[peu, kv[3]], w=PC_u)
                sv_ = sacc[0:4, :].rearrange("p (e c) -> p e c", e=2)
                if ti == 0:
                    kb.op("dve", lambda e, sv_=sv_: e.tensor_copy(out=sv_, in_=PC[0:4, :, 0:260]), r=PC_u, w=[sacc_u])
                else:
                    kb.op("dve", lambda e, sv_=sv_: e.tensor_tensor(out=sv_, in0=PC[0:4, :, 0:260], in1=sv_, op=ALU.add),
                          r=PC_u, w=[sacc_u])
            kb.dma("pool", acc[0][2048 + srow(b, 0):2048 + srow(b, 0) + 4, :], sacc[0:4, :], r=[sacc_u],
                   wa=[acc_u[0]], tag="acc%d" % l)

        if stop == 5:
            kb.finish()
            return nc
        s0t, s0u = tb.next()
        kb.dma("sp", s0t[:], goutS, r=[gout_u], w=[s0u])
        kb.op("dve", lambda e: e.tensor_scalar(out=Sst.rearrange("p h d -> p (h d)"), in0=s0t[:],
                                               scalar1=flag[:, 0:1], scalar2=None, op0=ALU.mult),
              r=[s0u, tab_u], w=[Sst_u])
        kb.op("act", lambda e: e.copy(out=Sb, in_=Sst), r=[Sst_u], w=[Sb_u])

        alias(pm.u, tb2x.u + catT2.u)

        def r2_a(j, tset):
            samp = (j == NPT)
            rows = slice(j * 128, (j + 1) * 128)
            qt, qu = tset.next()
            kt, ku = tset.next()
            vt_, vu_ = tset.next()
            gt, gu = tset.next()
            kb.dma("sp", qt, zs[0][rows, :], r=[zs_u[0]], w=[qu])
            kb.dma("sp", kt, zs[1][rows, :], r=[zs_u[1]], w=[ku])
            kb.dma("sp", vt_, zs[2][rows, :], r=[zs_u[2]], w=[vu_])
            kb.dma("sp", gt, zs[3][rows, :], r=[zs_u[3]], w=[gu])
            qk, qku = hb.next()
            kb.op("dve", lambda e, qk=qk, qt=qt, j=j: e.tensor_tensor(
                out=qk[:, 0:512].rearrange("p (h d) -> p h d", h=4), in0=qt.rearrange("p (h d) -> p h d", h=4),
                in1=decq[:, j, :].unsqueeze(2).broadcast_to([128, 4, 128]), op=ALU.mult),
                r=[qu, tab_u], wa=[qku], tag="qk%d_%d" % (l, j))
            kb.op("pool", lambda e, qk=qk, kt=kt: e.tensor_copy(out=qk[:, 512:1024], in_=kt), r=[ku],
                  wa=[qku], tag="qk%d_%d" % (l, j))
            kd, kdu = kz.next()
            kb.op("pool", lambda e, kd=kd, kt=kt, j=j: e.tensor_tensor(
                out=kd.rearrange("p (h d) -> p h d", h=4), in0=kt.rearrange("p (h d) -> p h d", h=4),
                in1=deck[:, j, :].unsqueeze(2).broadcast_to([128, 4, 128]), op=ALU.mult), r=[ku, tab_u], w=[kdu])
            qT, qTu = qkT.next()
            transpose8(qk, qku, qT, dict(w=[qTu]), eng="act")
            ps, psu = pab.next()

            def f(e, ps=ps, qT=qT):
                ins = None
                for h in range(4):
                    ins = e.matmul(ps[:, h * 128:(h + 1) * 128], lhsT=qT[:, 4 + h, :], rhs=qT[:, h, :],
                                   start=True, stop=True, skip_group_check=True)
                return ins
            kb.op("pe", f, r=[qTu], w=[psu])
            pr, pru = tb.next()
            mk = mrets if samp else mretp
            kb.op("dve", lambda e, pr=pr, ps=ps, mk=mk: e.tensor_tensor(
                out=pr[:, :], in0=ps, in1=mk[:, :, :].rearrange("p h i -> p (h i)"), op=ALU.mult),
                r=[psu, tab_u], w=[pru])
            if samp:
                for b in range(4):
                    kb.op("dve", lambda e, b=b, qT=qT: e.tensor_copy(
                        out=qz[:, b, :, srow(b, 0):srow(b, 0) + 4], in_=qT[:, 0:4, srow(b, 0):srow(b, 0) + 4]),
                        r=[qTu], w=[qz_u])
            return dict(j=j, samp=samp, rows=rows, vt_=vt_, vu_=vu_, gt=gt, gu=gu, kd=kd, kdu=kdu, qT=qT, qTu=qTu,
                        pr=pr, pru=pru)

        def r2_b(c):
            j, samp, rows, vt_, vu_, gt, gu = c["j"], c["samp"], c["rows"], c["vt_"], c["vu_"], c["gt"], c["gu"]
            kd, kdu, qT, qTu, pr, pru = c["kd"], c["kdu"], c["qT"], c["qTu"], c["pr"], c["pru"]
            def state_update(kd=kd, kdu=kdu, vt_=vt_, vu_=vu_, j=j, samp=samp):
                if not samp:
                    def f(e, kd=kd, vt_=vt_):
                        ins = None
                        for h in range(4):
                            ins = e.matmul(PC[:, 0, h * 128:(h + 1) * 128], lhsT=kd[:, h * 128:(h + 1) * 128],
                                           rhs=vt_[:, h * 128:(h + 1) * 128], start=True, stop=True, skip_group_check=True)
                        return ins
                    kb.op("pe", f, r=[kdu, vu_], w=[PC_u[0]])
                    for h in range(4):
                        kb.op("dve", lambda e, h=h: e.scalar_tensor_tensor(
                            out=Sst[:, h, :], in0=Sst[:, h, :], scalar=float(GAM[h] ** 128),
                            in1=PC[:, 0, h * 128:(h + 1) * 128], op0=ALU.mult, op1=ALU.add), r=[PC_u[0]], w=[Sst_u])
                    kb.op("act", lambda e: e.copy(out=Sb, in_=Sst), r=[Sst_u], w=[Sb_u])
                    if j == NPT - 1:
                        kb.dma("pool", pret_o[l], Sst, r=[Sst_u])
                else:
                    for b in range(4):
                        kzb_, kzu = hb.next()
                        kzb = kzb_[:, 0:512]
                        kb.op("dve", lambda e, kzb=kzb, kd=kd, b=b: e.tensor_scalar(
                            out=kzb, in0=kd, scalar1=rowmask[:, b:b + 1], scalar2=None, op0=ALU.mult),
                            r=[kdu, tab_u], w=[kzu])

                        def f(e, kzb=kzb, vt_=vt_):
                            ins = None
                            for h in range(4):
                                ins = e.matmul(PC[:, 0, h * 128:(h + 1) * 128], lhsT=kzb[:, h * 128:(h + 1) * 128],
                                               rhs=vt_[:, h * 128:(h + 1) * 128], start=True, stop=True,
                                               skip_group_check=True)
                            return ins
                        kb.op("pe", f, r=[kzu, vu_], w=[PC_u[0]])
                        s0f, s0fu = t32.next()
                        kb.dma("sp", s0f[:, :].rearrange("p (h e) -> p h e", h=4),
                               sret_in[l, b].rearrange("h d e -> d h e"), w=[s0fu])
                        for h in range(4):
                            kb.op("dve", lambda e, h=h, s0f=s0f: e.scalar_tensor_tensor(
                                out=s0f[:, h * 128:(h + 1) * 128], in0=s0f[:, h * 128:(h + 1) * 128],
                                scalar=float(GAM[h] ** 4), in1=PC[:, 0, h * 128:(h + 1) * 128], op0=ALU.mult, op1=ALU.add),
                                r=[PC_u[0]], w=[s0fu])
                        kb.dma("pool", sret_o[l, b], s0f[:, :].rearrange("p (h e) -> p h e", h=4), r=[s0fu])

            if samp:
                state_update()
            po, pou = pab.next()
            if not samp:
                def f(e, po=po, pr=pr, vt_=vt_, qT=qT):
                    ins = None
                    for h in range(4):
                        o = po[:, h * 128:(h + 1) * 128]
                        e.matmul(o, lhsT=pr[:, h * 128:(h + 1) * 128], rhs=vt_[:, h * 128:(h + 1) * 128],
                                 start=True, stop=False)
                        ins = e.matmul(o, lhsT=qT[:, h, :], rhs=Sb[:, h, :], start=False, stop=True)
                    return ins
                kb.op("pe", f, r=[pru, vu_, qTu, Sb_u], w=[pou])
            else:
                def f(e, po=po, pr=pr, vt_=vt_):
                    ins = None
                    for h in range(4):
                        ins = e.matmul(po[:, h * 128:(h + 1) * 128], lhsT=pr[:, h * 128:(h + 1) * 128],
                                       rhs=vt_[:, h * 128:(h + 1) * 128], start=True, stop=True, skip_group_check=True)
                    return ins
                kb.op("pe", f, r=[pru, vu_], w=[pou])
                oacc, oaccu = t32.next()
                kb.op("act", lambda e, oacc=oacc, po=po: e.copy(out=oacc[:], in_=po), r=[pou], w=[oaccu])
                for b in range(4):
                    sbt, sbu = s0bp.next()
                    kb.dma("pool", sbt, sret_in[l, b].rearrange("h d e -> d h e"), w=[sbu])
                    pq, pqu = pab.next()

                    def f(e, pq=pq, b=b, sbt=sbt):
                        ins = None
                        for h in range(4):
                            ins = e.matmul(pq[0:24, h * 128:(h + 1) * 128], lhsT=qz[:, b, h, :], rhs=sbt[:, h, :],
                                           start=True, stop=True, skip_group_check=True)
                        return ins
                    kb.op("pe", f, r=[qz_u, sbu], w=[pqu])
                    kb.op("dve", lambda e, pq=pq, oacc=oacc: e.tensor_tensor(out=oacc[0:24, :], in0=pq[0:24, :],
                                                                             in1=oacc[0:24, :], op=ALU.add),
                          r=[pqu], w=[oaccu])
            if not samp:
                state_update()
            if samp:
                osb, osu = oacc, oaccu
            else:
                osb, osu = t32.next()
                kb.op("act", lambda e, osb=osb, po=po: e.copy(out=osb[:], in_=po), r=[pou], w=[osu])
            st, su = sm.next()
            for h in range(4):
                kb.op("dve", lambda e, h=h, st=st, osb=osb: e.bn_stats(out=st[:, 6 * h:6 * h + 6],
                                                                        in_=osb[:, h * 128:(h + 1) * 128]),
                      r=[osu], w=[su])
            st2, su2 = sm.next()
            for h in range(4):
                kb.op("dve", lambda e, h=h, st=st, st2=st2: e.bn_aggr(out=st2[:, 2 * h:2 * h + 2],
                                                                      in_=st[:, 6 * h:6 * h + 6]), r=[su], w=[su2])
            s2v = st2[:, 0:8].rearrange("p (h two) -> p h two", two=2)
            kb.op("act", lambda e, st2=st2, s2v=s2v: e.activation(out=st2[:, 8:12], in_=s2v[:, :, 1], func=AF.Sqrt,
                                                                  bias=epsb[:, 0:1]), r=[su2, tab_u], w=[su2])
            kb.op("dve", lambda e, st2=st2: e.reciprocal(out=st2[:, 8:12], in_=st2[:, 8:12]), r=[su2], w=[su2])
            for h in range(4):
                kb.op("dve", lambda e, h=h, osb=osb, st2=st2: e.tensor_scalar(
                    out=osb[:, h * 128:(h + 1) * 128], in0=osb[:, h * 128:(h + 1) * 128],
                    scalar1=st2[:, 2 * h:2 * h + 1], scalar2=st2[:, 8 + h:9 + h], op0=ALU.subtract, op1=ALU.mult),
                    r=[su2], w=[osu])
            sg, sgu = t32.next()
            kb.op("act", lambda e, sg=sg, gt=gt: e.activation(out=sg[:], in_=gt, func=AF.Silu), r=[gu], w=[sgu])
            kb.op("pool", lambda e, osb=osb: e.tensor_tensor(out=osb[:], in0=osb[:], in1=gnw, op=ALU.mult),
                  r=[gnw_u], w=[osu])
            ct, cu = catb.next()
            tgc = "c%d_%d" % (l, j)
            kb.op("pool", lambda e, ct=ct, osb=osb, sg=sg: e.tensor_tensor(out=ct[:, 0:512], in0=osb[:], in1=sg[:],
                                                                          op=ALU.mult), r=[osu, sgu], wa=[cu], tag=tgc)
            a0, a0u = accl.t[0], accl.u[0]
            kb.dma("sp", a0, acc[0][rows, :], r=[acc_u[0]], w=[a0u])
            if not samp:
                a1, a1u = accl.t[1], accl.u[1]
                kb.dma("sp", a1, acc[1][rows, :], r=[acc_u[1]], w=[a1u])
                kb.op("dve", lambda e, a0=a0, a1=a1: e.tensor_tensor(out=a0, in0=a0, in1=a1, op=ALU.add),
                      r=[a1u], w=[a0u])
                a2, a2u = a1, a1u
                kb.dma("sp", a2, acc[2][rows, :], r=[acc_u[2]], w=[a2u])
                kb.op("dve", lambda e, a0=a0, a2=a2: e.tensor_tensor(out=a0, in0=a0, in1=a2, op=ALU.add),
                      r=[a2u], w=[a0u])
            a0v = a0.rearrange("p (e hp c) -> p e hp c", e=2, hp=4)
            rd, rdu = sm.next()
            kb.op("dve", lambda e, rd=rd, a0v=a0v: e.reciprocal(
                out=rd[:, 0:8].rearrange("p (e hp) -> p e hp", e=2), in_=a0v[:, :, :, 64]), r=[a0u], w=[rdu])
            kb.op("dve", lambda e, ct=ct, a0v=a0v, rd=rd: e.tensor_tensor(
                out=ct[:, 512:1024].rearrange("p (hp e d) -> p e hp d", hp=4, e=2), in0=a0v[:, :, :, 0:64],
                in1=rd[:, 0:8].rearrange("p (e hp) -> p e hp", e=2).unsqueeze(3).broadcast_to([128, 2, 4, 64]),
                op=ALU.mult), r=[a0u, rdu], wa=[cu], tag=tgc)
            if dbg and l == 0:
                kb.dma("pool", dbg_cat[rows, :], ct[:], r=[cu])
                kb.dma("pool", dbg_acc[rows, :], a0, r=[a0u])
            cT, cTu = catT2.next()
            transpose8(ct, cu, cT, dict(w=[cTu]), eng="act")
            for hf in range(2):
                ps, psu = pab.next()

                def f(e, ps=ps, cT=cT, hf=hf):
                    ins = None
                    for k in range(8):
                        ins = e.matmul(ps, lhsT=cT[:, k, :], rhs=wo[hf][0][:, k, :], start=(k == 0), stop=(k == 7))
                    return ins
                kb.op("pe", f, r=[cTu, wo[hf][1]], w=[psu])
                kb.op("dve", lambda e, ps=ps, j=j, hf=hf: e.tensor_tensor(
                    out=x[:, j, hf * 512:(hf + 1) * 512], in0=ps, in1=x[:, j, hf * 512:(hf + 1) * 512], op=ALU.add),
                    r=[psu], w=[x_u[j]])

        if stop == 6:
            kb.finish()
            return nc
        tsets = [tb2, tb2x]
        pend = r2_a(0, tsets[0])
        for j in range(NT):
            nxt = r2_a(j + 1, tsets[(j + 1) % 2]) if j + 1 < NT else None
            r2_b(pend)
            pend = nxt
        if dbg and l == 0:
            for j in range(NT):
                kb.dma("pool", dbg_x[j * 128:(j + 1) * 128, :], x[:, j, :], r=[x_u[j]])
        kb.dma("pool", gin2.ap(), x[126:128, NPT - 1, :], r=[x_u[NPT - 1]], w=[gin2_u])
        allgather(gin2.ap(), gin2_u, gout2.ap(), gout2_u)
        alias(hT_u, [xh_u])
        kb.op("dve", lambda e: e.memset(xh, 0.0), w=[xh_u])
        kb.dma("sp", xh[0:2, :], gout2.ap()[0:2, :], r=[gout2_u], w=[xh_u])
        kb.op("dve", lambda e: e.tensor_scalar(out=xh[0:2, :], in0=xh[0:2, :], scalar1=flag[0:2, 0:1], scalar2=None,
                                               op0=ALU.mult), r=[tab_u], w=[xh_u])
        kb.dma("sp", nwb[0][:], n2b[l], w=[nwb_u[0]])
        h_t, h_u = hb.next()
        rmsnorm(xh, xh_u, nwb[1], nwb_u[1], h_t[:], dict(w=[h_u]))
        transpose8(h_t, h_u, hTh[:, :, :], dict(w=[hTh_u]), ncols=2)

        if stop == 7:
            kb.finish()
            return nc
        alias(hT_u + [xh_u], [aT_u])
        alias(arena_units, [wd_u])
        w_down_v = w_down[l].rearrange("(c p) n -> p c n", p=128)
        for c0 in range(0, 22, 2):
            kb.dma("pool", wd[:, c0:c0 + 2, :], w_down_v[:, c0:c0 + 2, :], wa=[wd_u], tag="wd%d" % l)
        w_up_v = w_up[l].rearrange("(k p) n -> p k n", p=128)
        ffn_seq = [(gi_, f2_) for gi_ in range(len(FFN_GROUPS)) for f2_ in range(11)]
        ffn_loaded = {}

        def load_wup(idx):
            if idx >= len(ffn_seq):
                return
            gi_, f2_ = ffn_seq[idx]
            wt_, wu_ = wbuf.next()
            tgw_ = "u%d_%d_%d" % (l, gi_, f2_)
            c_ = f2_ * 256
            for k2 in range(2):
                kb.dma("pool", wt_[:, 4 * k2:4 * k2 + 4, 0:256], w_up_v[:, 4 * k2:4 * k2 + 4, c_:c_ + 256],
                       wa=[wu_], tag=tgw_)
                kb.dma("pool", wt_[:, 4 * k2:4 * k2 + 4, 256:512], w_up_v[:, 4 * k2:4 * k2 + 4, DFF + c_:DFF + c_ + 256],
                       wa=[wu_], tag=tgw_)
            ffn_loaded[idx] = (wt_, wu_)
        load_wup(0)
        for gi, (t0, t1) in enumerate(FFN_GROUPS):
            ntok = (t1 - t0) * 128
            last = (gi == len(FFN_GROUPS) - 1)
            tiles = list(range(t0, t1)) + ([NPT] if last else [])
            for j in tiles:
                h_t, h_u = hb.next()
                rmsnorm(x[:, j, :], x_u[j], nwb[1], nwb_u[1], h_t[:], dict(w=[h_u]))
                if j < NPT:
                    transpose8(h_t, h_u, h2T[:, :, (j - t0) * 128:(j - t0 + 1) * 128],
                               dict(wa=[h2T_u], tag="h2T%d_%d" % (l, gi)), eng="dve")
                else:
                    transpose8(h_t, h_u, h2Ts[:, :, :], dict(w=[h2Ts_u]), ncols=24, eng="dve")
                    kb.op("dve", lambda e: e.memset(h2Ts[:, :, :].rearrange("p k (b s) -> p k b s", b=4)[:, :, :, 0:2],
                                                    0.0), w=[h2Ts_u])
            wins = []
            c = 0
            while c < ntok:
                n_ = min(512, ntok - c)
                wins.append((c, n_))
                c += n_
            for fc in range(22):
                if fc % 2 == 0:
                    wt2, wu = ffn_loaded.pop(gi * 11 + fc // 2)
                    load_wup(gi * 11 + fc // 2 + 1)
                o_ = (fc % 2) * 128
                wt = wt2[:, :, :].rearrange("p k (s c) -> p k s c", s=2)[:, :, :, o_:o_ + 128]
                cs = (fc, 22 + fc)
                if gi == 0:
                    def f(e, wt=wt):
                        ins = None
                        for s_ in range(2):
                            for k in range(8):
                                ins = e.matmul(PC[:, 1, 2 * s_:2 * s_ + 2], lhsT=wt[:, k, s_, :],
                                               rhs=hTh[:, k, :], start=(k == 0), stop=(k == 7), skip_group_check=True)
                        return ins
                    kb.op("pe", f, r=[wu, hTh_u], w=[PC_u[1]])
                    for s_ in range(2):
                        kb.op("act", lambda e, s_=s_, cs=cs: e.copy(out=carry[:, cs[s_], :],
                                                                    in_=PC[:, 1, 2 * s_:2 * s_ + 2]),
                              r=[PC_u[1]], w=[carry_u])
                for (c0, n_) in wins:
                    ug, ugu = pab.next()
                    uv, uvu = pab.next()
                    for s_, (pp, ppu) in enumerate(((ug, ugu), (uv, uvu))):
                        def f(e, pp=pp, s_=s_, wt=wt, c0=c0, n_=n_):
                            ins = None
                            for k in range(8):
                                ins = e.matmul(pp[:, 0:n_], lhsT=wt[:, k, s_, :],
                                               rhs=h2T[:, k, c0:c0 + n_], start=(k == 0), stop=(k == 7))
                            return ins
                        kb.op("pe", f, r=[wu, h2T_u], w=[ppu])
                    cb = []
                    for s_, (pp, ppu) in enumerate(((ug, ugu), (uv, uvu))):
                        ch = cs[s_]
                        A, Au = t32.next()
                        kb.op("act", lambda e, A=A, pp=pp, ch=ch, n_=n_: e.activation(
                            out=A[:, 0:n_], in_=pp[:, 0:n_], func=AF.Identity, scale=cwt[:, ch, 2:3],
                            bias=cwt[:, ch, 3:4]), r=[ppu, cwt_u], w=[Au])
                        cb.append((A, Au))
                    for s_, (pp, ppu) in enumerate(((ug, ugu), (uv, uvu))):
                        ch = cs[s_]
                        A, Au = cb[s_]
                        kb.op("dve", lambda e, A=A, pp=pp, ch=ch, n_=n_: e.scalar_tensor_tensor(
                            out=A[:, 1:n_], in0=pp[:, 0:n_ - 1], scalar=cwt[:, ch, 1:2], in1=A[:, 1:n_],
                            op0=ALU.mult, op1=ALU.add), r=[ppu, cwt_u], w=[Au])
                        kb.op("dve", lambda e, A=A, ch=ch: e.scalar_tensor_tensor(
                            out=A[:, 0:1], in0=carry[:, ch, 1:2], scalar=cwt[:, ch, 1:2], in1=A[:, 0:1],
                            op0=ALU.mult, op1=ALU.add), r=[carry_u, cwt_u], w=[Au])
                        kb.op("dve", lambda e, A=A, pp=pp, ch=ch, n_=n_: e.scalar_tensor_tensor(
                            out=A[:, 2:n_], in0=pp[:, 0:n_ - 2], scalar=cwt[:, ch, 0:1], in1=A[:, 2:n_],
                            op0=ALU.mult, op1=ALU.add), r=[ppu, cwt_u], w=[Au])
                        kb.op("dve", lambda e, A=A, ch=ch: e.scalar_tensor_tensor(
                            out=A[:, 0:2], in0=carry[:, ch, 0:2], scalar=cwt[:, ch, 0:1], in1=A[:, 0:2],
                            op0=ALU.mult, op1=ALU.add), r=[carry_u, cwt_u], w=[Au])
                        kb.op("dve", lambda e, pp=pp, ch=ch, n_=n_: e.tensor_copy(out=carry[:, ch, :], in_=pp[:, n_ - 2:n_]),
                              r=[ppu], w=[carry_u])
                    kb.op("act", lambda e, A=cb[0][0], n_=n_: e.activation(out=A[:, 0:n_], in_=A[:, 0:n_],
                                                                          func=AF.Silu), w=[cb[0][1]])
                    kb.op("pool", lambda e, Ag=cb[0][0], A=cb[1][0], fc=fc, c0=c0, n_=n_: e.tensor_tensor(
                        out=aT[:, fc, c0:c0 + n_], in0=Ag[:, 0:n_], in1=A[:, 0:n_], op=ALU.mult),
                        r=[cb[0][1], cb[1][1]], wa=[aT_u], tag="aT%d_%d" % (l, gi))
                if last:
                    def f(e, wt=wt):
                        ins = None
                        for s_ in range(2):
                            for k in range(8):
                                ins = e.matmul(PC[:, 1, 32 * s_:32 * s_ + 24], lhsT=wt[:, k, s_, :],
                                               rhs=h2Ts[:, k, :], start=(k == 0), stop=(k == 7), skip_group_check=True)
                        return ins
                    kb.op("pe", f, r=[wu, h2Ts_u], w=[PC_u[1]])
                    cb = []
                    for s_ in range(2):
                        ch = cs[s_]
                        us, usu = sm.next()
                        kb.op("dve", lambda e, us=us, s_=s_: e.tensor_copy(out=us[:, 0:24],
                                                                           in_=PC[:, 1, 32 * s_:32 * s_ + 24]),
                              r=[PC_u[1]], w=[usu])
                        kb.op("dve", lambda e, us=us, ch=ch: e.tensor_copy(
                            out=us[:, 0:24].rearrange("p (b s) -> p b s", b=4)[:, :, 0:2], in_=ctx8[:, ch, :, :]),
                            r=[ctx8_u], w=[usu])
                        A, Au = sm.next()
                        kb.op("act", lambda e, A=A, us=us, ch=ch: e.activation(
                            out=A[:, 0:22], in_=us[:, 2:24], func=AF.Identity, scale=cwt[:, ch, 2:3],
                            bias=cwt[:, ch, 3:4]), r=[usu, cwt_u], w=[Au])
                        kb.op("dve", lambda e, A=A, us=us, ch=ch: e.scalar_tensor_tensor(
                            out=A[:, 0:22], in0=us[:, 1:23], scalar=cwt[:, ch, 1:2], in1=A[:, 0:22],
                            op0=ALU.mult, op1=ALU.add), r=[usu, cwt_u], w=[Au])
                        kb.op("dve", lambda e, A=A, us=us, ch=ch: e.scalar_tensor_tensor(
                            out=A[:, 0:22], in0=us[:, 0:22], scalar=cwt[:, ch, 0:1], in1=A[:, 0:22],
                            op0=ALU.mult, op1=ALU.add), r=[usu, cwt_u], w=[Au])
                        kb.op("act", lambda e, us=us, ch=ch: e.copy(
                            out=uout_s[:, ch, :, :], in_=us[:, 0:24].rearrange("p (b s) -> p b s", b=4)[:, :, 4:6]),
                            r=[usu], wa=[uouts_u], tag="uo%d" % l)
                        cb.append((A, Au))
                    kb.op("act", lambda e, A=cb[0][0]: e.activation(out=A[:, 0:22], in_=A[:, 0:22], func=AF.Silu),
                          w=[cb[0][1]])
                    kb.op("pool", lambda e, Ag=cb[0][0], A=cb[1][0], fc=fc: e.tensor_tensor(
                        out=aTs[:, fc, 2:24], in0=Ag[:, 0:22], in1=A[:, 0:22], op=ALU.mult),
                        r=[cb[0][1], cb[1][1]], wa=[aTs_u], tag="aTs%d" % l)
            for j in tiles:
                for hf in range(2):
                    ps, psu = pab.next()
                    if j < NPT:
                        def f(e, ps=ps, j=j, hf=hf):
                            ins = None
                            for fc in range(22):
                                ins = e.matmul(ps, lhsT=aT[:, fc, (j - t0) * 128:(j - t0 + 1) * 128],
                                               rhs=wd[:, fc, hf * 512:(hf + 1) * 512], start=(fc == 0), stop=(fc == 21))
                            return ins
                        kb.op("pe", f, r=[aT_u, wd_u], w=[psu])
                        kb.op("dve", lambda e, ps=ps, j=j, hf=hf: e.tensor_tensor(
                            out=x[:, j, hf * 512:(hf + 1) * 512], in0=ps, in1=x[:, j, hf * 512:(hf + 1) * 512],
                            op=ALU.add), r=[psu], w=[x_u[j]])
                    else:
                        def f(e, ps=ps, hf=hf):
                            ins = None
                            for fc in range(22):
                                ins = e.matmul(ps[0:24, :], lhsT=aTs[:, fc, :], rhs=wd[:, fc, hf * 512:(hf + 1) * 512],
                                               start=(fc == 0), stop=(fc == 21))
                            return ins
                        kb.op("pe", f, r=[aTs_u, wd_u], w=[psu])
                        kb.op("dve", lambda e, ps=ps, j=j, hf=hf: e.tensor_tensor(
                            out=x[0:24, j, hf * 512:(hf + 1) * 512], in0=ps[0:24, :],
                            in1=x[0:24, j, hf * 512:(hf + 1) * 512], op=ALU.add), r=[psu], w=[x_u[j]])
        if stop == 8:
            kb.finish()
            return nc
        kb.dma("pool", pconv_o[l], carry[:], r=[carry_u])
        kb.dma("pool", sconv_o[l], uout_s[:], r=[uouts_u])

    kb.dma("sp", nwb[0][:], fnb, w=[nwb_u[0]])
    for j in range(NT):
        for hf in range(1):
            pass
        yt, yu = t32.next()
        yt2, yu2 = t32.next()
        st, su = sm.next()
        kb.op("act", lambda e, j=j, st=st: e.activation(out=junk[:, :], in_=x[:, j, :], func=AF.Square,
                                                        accum_out=st[:, 0:1]), r=[x_u[j]], w=[junk_u, su])
        kb.op("act", lambda e, st=st: e.activation(out=st[:, 1:2], in_=st[:, 0:1], func=AF.Sqrt, scale=1.0 / D,
                                                   bias=epsb[:, 0:1]), r=[su, tab_u], w=[su])
        kb.op("dve", lambda e, st=st: e.reciprocal(out=st[:, 2:3], in_=st[:, 1:2]), r=[su], w=[su])
        for hf, (yy, yyu) in enumerate(((yt, yu), (yt2, yu2))):
            kb.op("dve", lambda e, j=j, st=st, yy=yy, hf=hf: e.scalar_tensor_tensor(
                out=yy[:], in0=x[:, j, hf * 512:(hf + 1) * 512], scalar=st[:, 2:3],
                in1=nwb[0][:, hf * 512:(hf + 1) * 512], op0=ALU.mult, op1=ALU.mult),
                r=[x_u[j], su, nwb_u[0]], w=[yyu])
            kb.dma("pool", y_o[j * 128:(j + 1) * 128, hf * 512:(hf + 1) * 512], yy[:], r=[yyu])
    kb.finish()
    return nc


def _tables(half):
    f32 = np.float32
    pos = np.zeros((NT, 128), np.float64)
    for j in range(NPT):
        pos[j] = half * 2048 + 128 * j + np.arange(128)
    for b in range(4):
        for t in range(4):
            pos[NPT, srow(b, t)] = PAST + t
    rope = np.zeros((NT, 128, 192), f32)
    inv_r = (10000.0 ** (-np.arange(64, dtype=np.float32) / np.float32(64))).astype(f32)
    inv_a = (10000.0 ** (-np.arange(32, dtype=np.float32) / np.float32(32))).astype(f32)
    ang_r = pos.astype(f32)[:, :, None] * inv_r[None, None, :]
    ang_a = pos.astype(f32)[:, :, None] * inv_a[None, None, :]
    rope[:, :, 0:64] = np.cos(ang_r.astype(np.float64))
    rope[:, :, 64:128] = np.sin(ang_r.astype(np.float64))
    rope[:, :, 128:160] = np.cos(ang_a.astype(np.float64))
    rope[:, :, 160:192] = np.sin(ang_a.astype(np.float64))
    g = np.array(GAM, np.float64)
    sc = 128.0 ** -0.5
    decq = np.ones((128, NT, 4))
    deck = np.ones((128, NT, 4)) * sc
    p = np.arange(128)
    for j in range(NPT):
        decq[:, j, :] = g[None, :] ** (p[:, None] + 1.0)
        deck[:, j, :] = g[None, :] ** (127.0 - p[:, None]) * sc
    for b in range(4):
        for t in range(4):
            decq[srow(b, t), NPT, :] = g ** (t + 1.0)
            deck[srow(b, t), NPT, :] = g ** (3.0 - t) * sc
    decf = np.zeros((128, NPT, 4))
    for j in range(NPT):
        decf[:, j, :] = g[None, :] ** (2047.0 - (128 * j + p[:, None])) * sc
    mretp = np.zeros((128, 4, 128))
    jj, ii = np.meshgrid(p, p, indexing="ij")
    for h in range(4):
        mretp[:, h, :] = (jj <= ii) * (g[h] ** (-(jj + 1.0))) * sc
    mrets = np.zeros((128, 4, 128))
    rowmask = np.zeros((128, 4))
    for b in range(4):
        for tj in range(4):
            rowmask[srow(b, tj), b] = 1.0
            for ti in range(tj, 4):
                for h in range(4):
                    mrets[srow(b, tj), h, srow(b, ti)] = g[h] ** (-(tj + 1.0)) * sc
    amask = np.zeros((128, 3, 128))
    amask[:, 0, :] = (jj <= ii)
    amask[:, 1, :] = (jj >= ii)
    amask[:, 2, :] = (jj >= ii) * float(half)
    smask = np.zeros((128, 9, 4))
    for t in range(4):
        smask[:, 0, t] = (p >= t)
        smask[:, 1 + t, t] = 1.0
    for b in range(4):
        for tp in range(4):
            for t in range(4):
                smask[srow(b, tp), 5 + b, t] = float(tp <= t) + 2.0 * float(tp == t)
    flag = np.full((128, 1), float(half))
    return dict(rope=rope, decq=decq.astype(f32), deck=deck.astype(f32), decf=decf.astype(f32),
                mretp=mretp.astype(f32), mrets=mrets.astype(f32), rowmask=rowmask.astype(f32),
                amask=amask.astype(f32), smask=smask.astype(f32), flag=flag.astype(f32))


_NC_CACHE = {}


def kernel(x_prompt, x_sample, cache_win_k, cache_win_v, state_ret, state_conv,
           norm1_w, w_in, ret_gn_w, w_out, norm2_w, w_up, conv_w, conv_b, w_down, final_norm_w, _nl=NL, _stop=99, _dbg=False):
    f32 = np.float32
    A = lambda a: np.ascontiguousarray(np.asarray(a, dtype=f32))
    x_prompt, x_sample = A(x_prompt), A(x_sample)
    cache_win_k, cache_win_v = np.asarray(cache_win_k, f32), np.asarray(cache_win_v, f32)
    state_ret, state_conv = np.asarray(state_ret, f32), np.asarray(state_conv, f32)
    if (_nl, _stop, _dbg) not in _NC_CACHE:
        _NC_CACHE[(_nl, _stop, _dbg)] = build(_nl, _stop, _dbg)
    nc = _NC_CACHE[(_nl, _stop, _dbg)]
    shared = dict(
        w_in=A(w_in), w_out=A(w_out), w_up=A(w_up), w_down=A(w_down),
        n1b=A(np.broadcast_to(np.asarray(norm1_w, f32)[:, None, :], (NL, 128, D))),
        n2b=A(np.broadcast_to(np.asarray(norm2_w, f32)[:, None, :], (NL, 128, D))),
        fnb=A(np.broadcast_to(np.asarray(final_norm_w, f32)[None, :], (128, D))),
        gnb=A(np.broadcast_to(np.asarray(ret_gn_w, f32)[:, None, :], (NL, 128, 512))),
    )
    cw = np.concatenate([np.asarray(conv_w, f32), np.asarray(conv_b, f32)[:, None, :]], axis=1)
    shared["convT"] = A(cw.reshape(NL, 4, 44, 128).transpose(0, 3, 2, 1))
    tabs = [_tables(0), _tables(1)]
    in_maps = []
    for c in range(8):
        s, half = c // 2, c % 2
        xin = np.zeros((TOK, D), f32)
        xin[0:2048] = x_prompt[s, half * 2048:(half + 1) * 2048]
        for b in range(4):
            xin[2048 + srow(b, 0):2048 + srow(b, 0) + 4] = x_sample[4 * c + b]
        m = dict(shared)
        m["xin"] = xin
        m["ck"] = A(cache_win_k[:, 4 * c:4 * c + 4].reshape(NL, 4, 2048, 512))
        m["cv"] = A(cache_win_v[:, 4 * c:4 * c + 4].reshape(NL, 4, 2048, 512))
        m["sret_in"] = A(state_ret[:, 4 * c:4 * c + 4])
        sc_ = state_conv[:, 4 * c:4 * c + 4]
        m["sconv_in"] = A(sc_.reshape(NL, 4, 2, 44, 128).transpose(0, 4, 3, 1, 2))
        m.update(tabs[half])
        in_maps.append(m)
    res = run_bass_kernel_spmd(nc, in_maps, core_ids=list(range(8)))
    R = res.results
    y_prompt = np.zeros((4, 4096, D), f32)
    y_sample = np.zeros((32, 4, D), f32)
    p_win_k = np.zeros((NL, 4, 2048, 8, 64), f32)
    p_win_v = np.zeros((NL, 4, 2048, 8, 64), f32)
    p_ret = np.zeros((NL, 4, 4, 128, 128), f32)
    p_conv = np.zeros((NL, 4, 2, 2 * DFF), f32)
    s_win_k = np.zeros((NL, 32, 4, 8, 64), f32)
    s_win_v = np.zeros((NL, 32, 4, 8, 64), f32)
    s_ret = np.zeros((NL, 32, 4, 128, 128), f32)
    s_conv = np.zeros((NL, 32, 2, 2 * DFF), f32)
    for c in range(8):
        s, half = c // 2, c % 2
        r = R[c]
        y_prompt[s, half * 2048:(half + 1) * 2048] = r["y"][0:2048]
        for b in range(4):
            rs = slice(2048 + srow(b, 0), 2048 + srow(b, 0) + 4)
            y_sample[4 * c + b] = r["y"][rs]
            ls = slice(srow(b, 0), srow(b, 0) + 4)
            s_win_k[:, 4 * c + b] = r["sk"][:, ls].reshape(NL, 4, 8, 64)
            s_win_v[:, 4 * c + b] = r["sv"][:, ls].reshape(NL, 4, 8, 64)
            s_ret[:, 4 * c + b] = r["sret"][:, b].transpose(0, 2, 1, 3)
        s_conv[:, 4 * c:4 * c + 4] = r["sconvT"].transpose(0, 3, 4, 2, 1).reshape(NL, 4, 2, 2 * DFF)
        if half == 1:
            p_win_k[:, s] = r["pk"].reshape(NL, 2048, 8, 64)
            p_win_v[:, s] = r["pv"].reshape(NL, 2048, 8, 64)
            p_ret[:, s] = r["pret"].transpose(0, 2, 1, 3)
            p_conv[:, s] = r["pconvT"].transpose(0, 3, 2, 1).reshape(NL, 2, 2 * DFF)
    return (y_prompt, y_sample, p_win_k, p_win_v, p_ret, p_conv, s_win_k, s_win_v, s_ret, s_conv)
```

```python
import math
import numpy as np
import concourse.bass as bass
import concourse.mybir as mybir
from concourse.bass_utils import run_bass_kernel_spmd

F32 = mybir.dt.float32
BF16 = mybir.dt.bfloat16
AF = mybir.ActivationFunctionType
ALU = mybir.AluOpType

D = 1024
DIN = 3584
DFF = 2816
NL = 4
NT = 17
NPT = 16
TOK = NT * 128
EPS = 1e-6
GAM = [1.0 - 2.0 ** (-5 - h) for h in range(4)]
PAST = 8192
GR = 4480
VOFF = 2176
SOFF = 4352
FFN_GROUPS = [(0, 6), (6, 11), (11, 16)]


def srow(b, t):
    return 6 * b + 2 + t


class U:
    __slots__ = ("w", "r", "tag", "const")

    def __init__(self, const=False):
        self.w = []
        self.r = []
        self.tag = None
        self.const = const


class KB:
    def __init__(self, nc):
        self.nc = nc
        self.eng = {"pe": nc.tensor, "act": nc.scalar, "dve": nc.vector, "pool": nc.gpsimd, "sp": nc.sync}
        self.csem = {}
        self.ccnt = {}
        for e in ("pe", "act", "dve", "pool"):
            self.csem[e] = nc.alloc_semaphore(name="c_" + e)
            self.ccnt[e] = 0
        self.dsem = {"sp": [nc.alloc_semaphore(name="dsp%d" % i) for i in range(24)],
                     "pool": [nc.alloc_semaphore(name="dpl%d" % i) for i in range(16)]}
        self.dval = {"sp": [0] * 24, "pool": [0] * 16}
        self.dnext = {"sp": 0, "pool": 0}
        self.seen = {e: {} for e in self.eng}
        self.semobj = {}

    def _wait(self, e, evs):
        need = {}
        for (s, v) in evs:
            k = id(s)
            self.semobj[k] = s
            if need.get(k, 0) < v:
                need[k] = v
        sn = self.seen[e]
        for k, v in need.items():
            if sn.get(k, 0) < v:
                self.eng[e].wait_ge(self.semobj[k], v)
                sn[k] = v

    def _collect(self, r, w, wa, tag):
        evs = []
        for u in r:
            evs += u.w
        for u in w:
            evs += u.w
            evs += u.r
        for u in wa:
            evs += u.r
            if u.tag != tag:
                evs += u.w
        return evs

    def _commit(self, ev, r, w, wa, tag):
        for u in r:
            if not u.const:
                u.r.append(ev)
        for u in w:
            u.w = [ev]
            u.r = []
            u.tag = None
        for u in wa:
            if u.tag != tag:
                u.w = []
                u.tag = tag
            u.w.append(ev)
            u.r = []

    def op(self, e, fn, r=(), w=(), wa=(), tag=None):
        self._wait(e, self._collect(r, w, wa, tag))
        ins = fn(self.eng[e])
        self.ccnt[e] += 1
        ins.then_inc(self.csem[e], 1)
        self._commit((self.csem[e], self.ccnt[e]), r, w, wa, tag)

    def dma(self, e, out, in_, r=(), w=(), wa=(), tag=None, **kw):
        evs = self._collect(r, w, wa, tag)
        k = self.dnext[e]
        self.dnext[e] = (k + 1) % len(self.dsem[e])
        sem = self.dsem[e][k]
        if self.dval[e][k] > 0:
            evs.append((sem, self.dval[e][k]))
        self._wait(e, evs)
        self.dval[e][k] += 16
        self.eng[e].dma_start(out=out, in_=in_, **kw).then_inc(sem, 16)
        self._commit((sem, self.dval[e][k]), r, w, wa, tag)

    def finish(self):
        evs = []
        for e in ("sp", "pool"):
            for s, v in zip(self.dsem[e], self.dval[e]):
                if v > 0:
                    evs.append((s, v))
        for e in ("pe", "act", "dve", "pool"):
            if self.ccnt[e] > 0:
                evs.append((self.csem[e], self.ccnt[e]))
        for e in ("sp", "act", "dve", "pe", "pool"):
            self._wait(e, evs)


class Rot:
    def __init__(self, tensors):
        self.t = tensors
        self.u = [U() for _ in tensors]
        self.i = 0

    def next(self):
        k = self.i
        self.i = (k + 1) % len(self.t)
        return self.t[k], self.u[k]


def build(nl=NL, stop=99, dbg=False):
    nc = bass.Bass("TRN2", target_bir_lowering=False)
    kb = KB(nc)

    def din(name, shape, dt=F32):
        return nc.dram_tensor(name, list(shape), dt, kind="ExternalInput").ap()

    def dout(name, shape, dt=F32):
        return nc.dram_tensor(name, list(shape), dt, kind="ExternalOutput").ap()

    def dscr(name, shape, dt):
        return nc.dram_tensor(name, list(shape), dt)

    xin = din("xin", [TOK, D])
    w_in = din("w_in", [NL, D, DIN])
    w_out = din("w_out", [NL, D, D])
    w_up = din("w_up", [NL, D, 2 * DFF])
    w_down = din("w_down", [NL, DFF, D])
    n1b = din("n1b", [NL, 128, D])
    n2b = din("n2b", [NL, 128, D])
    fnb = din("fnb", [128, D])
    gnb = din("gnb", [NL, 128, 512])
    convT = din("convT", [NL, 128, 44, 4])
    sconv_in = din("sconv_in", [NL, 128, 44, 4, 2])
    sret_in = din("sret_in", [NL, 4, 4, 128, 128])
    ck = din("ck", [NL, 4, 2048, 512])
    cv = din("cv", [NL, 4, 2048, 512])
    rope_d = din("rope", [NT, 128, 192])
    decq_d = din("decq", [128, NT, 4])
    deck_d = din("deck", [128, NT, 4])
    decf_d = din("decf", [128, NPT, 4])
    mretp_d = din("mretp", [128, 4, 128])
    mrets_d = din("mrets", [128, 4, 128])
    rowmask_d = din("rowmask", [128, 4])
    amask_d = din("amask", [128, 3, 128])
    smask_d = din("smask", [128, 9, 4])
    flag_d = din("flag", [128, 1])

    y_o = dout("y", [TOK, D])
    pk_o = dout("pk", [NL, 2048, 512])
    pv_o = dout("pv", [NL, 2048, 512])
    sk_o = dout("sk", [NL, 128, 512])
    sv_o = dout("sv", [NL, 128, 512])
    pret_o = dout("pret", [NL, 128, 4, 128])
    pconv_o = dout("pconvT", [NL, 128, 44, 2])
    sret_o = dout("sret", [NL, 4, 128, 4, 128])
    sconv_o = dout("sconvT", [NL, 128, 44, 4, 2])
    if dbg:
        dbg_cat = dout("dbg_cat", [TOK, D], BF16)
        dbg_x = dout("dbg_x", [TOK, D])
        dbg_acc = dout("dbg_acc", [TOK, 520])

    zs = [dscr("zs%d" % g, [TOK, 512], BF16).ap() for g in range(5)]
    zs_u = [U() for _ in range(5)]
    gK = dscr("gK", [2176, 512], BF16)
    gKo = [dscr("gKo%d" % i, [2176, 512], BF16) for i in range(2)]
    rK = dscr("rK", [2176, 512], BF16)
    gV = dscr("gV", [2176, 520], BF16)
    gVo = [dscr("gVo%d" % i, [2176, 520], BF16) for i in range(2)]
    rV = dscr("rV", [2176, 520], BF16)
    rkv_u = U()
    gS = dscr("gS", [128, 512], BF16)
    gSo = dscr("gSo", [256, 512], BF16)
    gin_u = U()
    gout_u = U()
    gin2 = dscr("gin2", [2, D], F32)
    gout2 = dscr("gout2", [4, D], F32)
    gin2_u = U()
    gout2_u = U()
    acc = [dscr("acc%d" % i, [TOK, 520], F32).ap() for i in range(3)]
    acc_u = [U() for _ in range(3)]
    ccsem = nc.alloc_semaphore(name="ccsem")
    cc_cnt = [0]

    def sb(name, shape, dt):
        return nc.alloc_sbuf_tensor("sb_" + name, list(shape), dt)

    x = sb("x", [128, NT, D], F32)
    x_u = [U() for _ in range(NT)]
    BIGN = max(8 * TOK, 22 * 768)
    big = sb("big", [128, BIGN], BF16)
    hT = big[:, 0:8 * TOK].rearrange("p (k t) -> p k t", k=8)
    hT_u = [U() for _ in range(NT)]
    aT = big[:, 0:22 * 768].rearrange("p (c t) -> p c t", c=22)
    aT_u = U()
    xh = big[:, 0:2 * D].bitcast(F32)
    xh_u = U()
    arena = sb("arena", [128, 22 * D], BF16)
    wd = arena[:, :].rearrange("p (c n) -> p c n", c=22)
    wd_u = U()
    a_off = [0]
    arena_units = []

    def carve(shape, dt):
        n = 1
        for s_ in shape[1:]:
            n *= s_
        nb = n * (2 if dt == BF16 else 4) // 2
        nb = (nb + 15) // 16 * 16
        o = a_off[0]
        a_off[0] += nb
        assert a_off[0] <= 22 * D, a_off[0]
        ap = arena[:, o:o + nb]
        if dt == F32:
            ap = ap.bitcast(F32)
        ap = ap[:, 0:n]
        if len(shape) == 3:
            ap = ap.rearrange("p (a b) -> p a b", a=shape[1])
        elif len(shape) == 4:
            ap = ap.rearrange("p (a b c) -> p a b c", a=shape[1], b=shape[2])
        return ap

    def arot(n, shape, dt):
        r_ = Rot([carve(shape, dt) for _ in range(n)])
        arena_units.extend(r_.u)
        return r_

    def aunit():
        u = U()
        arena_units.append(u)
        return u

    ident = sb("ident", [128, 128], BF16)
    ident_u = U(const=True)
    nwb = [sb("nwb0", [128, D], F32)]
    nwb.append(nwb[0])
    nwb_u = [U()]
    nwb_u.append(nwb_u[0])
    cwt = sb("cwt", [128, 44, 4], F32)
    cwt_u = U()
    decq = sb("decq", [128, NT, 4], F32)
    deck = sb("deck", [128, NT, 4], F32)
    decf = sb("decf", [128, NPT, 4], F32)
    mretp = sb("mretp", [128, 4, 128], BF16)
    mrets = sb("mrets", [128, 4, 128], BF16)
    rowmask = sb("rowmask", [128, 4], F32)
    amask = sb("amask", [128, 3, 128], BF16)
    smask = sb("smask", [128, 9, 4], BF16)
    flag = sb("flag", [128, 1], F32)
    epsb = sb("epsb", [128, 1], F32)
    tab_u = U(const=True)
    carry = sb("carry", [128, 44, 2], F32)
    carry_u = U()
    uout_s = sb("uout_s", [128, 44, 4, 2], F32)
    uouts_u = U()
    ctx8 = sb("ctx8", [128, 44, 4, 2], F32)
    ctx8_u = U()
    aTs = sb("aTs", [128, 22, 24], BF16)
    aTs_u = U()
    hTh = sb("hTh", [128, 8, 2], BF16)
    hTh_u = U()
    h2Ts = sb("h2Ts", [128, 8, 24], BF16)
    h2Ts_u = U()

    def rot(name, n, shape, dt):
        return Rot([sb("%s%d" % (name, i), shape, dt) for i in range(n)])

    wbuf = rot("wbuf", 2, [128, 8, 512], BF16)
    wbufF = Rot([wbuf.t[0][:, :, 0:256], wbuf.t[0][:, :, 256:512], wbuf.t[1][:, :, 0:256], wbuf.t[1][:, :, 256:512]])
    t32 = rot("t32", 4, [128, 512], F32)
    t32b = t32
    tb = rot("tb", 3, [128, 512], BF16)
    hb = rot("hb", 2, [128, D], BF16)
    junk = sb("junk", [128, D], BF16)
    junk_u = U()
    sm = rot("sm", 6, [128, 32], F32)
    pes = rot("pes", 2, [128, 32], BF16)
    h2T = sb("h2T", [128, 8, 768], BF16)
    h2T_u = U()

    gnw = carve([128, 512], F32)
    gnw_u = aunit()
    Sst = carve([128, 4, 128], F32)
    Sst_u = aunit()
    Sb = carve([128, 4, 128], BF16)
    Sb_u = aunit()
    s0bp = arot(2, [128, 4, 128], BF16)
    qz = carve([128, 4, 4, 24], BF16)
    qz_u = aunit()
    tb2 = arot(4, [128, 512], BF16)
    ropet = arot(2, [128, 192], F32)
    v65 = arot(5, [128, 8, 65], BF16)
    qkT = arot(2, [128, 8, 128], BF16)
    catT = qkT
    catb = hb
    kTs = arot(3, [128, 4, 128], BF16)
    qTp = arot(2, [128, 4, 128], BF16)
    pm = arot(2, [128, 2, 1024], BF16)
    ob = arot(2, [128, 520], F32)
    accl = ob
    kz = arot(2, [128, 512], BF16)
    _pm0 = pm.t[0].rearrange("p t c -> p (t c)")
    _pm1 = pm.t[1].rearrange("p t c -> p (t c)")
    tb2x = Rot([_pm0[:, i * 512:(i + 1) * 512] for i in range(4)])
    catT2 = Rot([_pm1[:, i * 1024:(i + 1) * 1024].rearrange("p (k t) -> p k t", k=8) for i in range(2)])
    arena_units.extend(tb2x.u + catT2.u)
    sacc = carve([128, 520], F32)
    sacc_u = aunit()

    PA = nc.alloc_psum_tensor("PA", [128, 2, 512], F32)
    PB = nc.alloc_psum_tensor("PB", [128, 2, 512], F32)
    PC = nc.alloc_psum_tensor("PC", [128, 2, 512], F32)
    PT = [nc.alloc_psum_tensor("PT%d" % i, [128, 1024], BF16) for i in range(2)]
    PA_u = [U(), U()]
    PB_u = [U(), U()]
    PC_u = [U(), U()]
    PT_u = [U(), U()]
    pt_i = [0]
    pab = Rot([PA[:, 0, :], PA[:, 1, :], PB[:, 0, :], PB[:, 1, :]])
    pab.u = [PA_u[0], PA_u[1], PB_u[0], PB_u[1]]

    def next_pt():
        k = pt_i[0]
        pt_i[0] = 1 - k
        return PT[k], PT_u[k]

    def alias(frm, to):
        evr = []
        evw = []
        for u in frm:
            evr += u.r
            evw += u.w
        for u in to:
            u.r = u.r + evr
            u.w = u.w + evw

    kb.op("pool", lambda e: e.memset(ident[:], 1.0), w=[ident_u])
    kb.op("pool", lambda e: e.affine_select(out=ident[:], in_=ident[:], pattern=[[-1, 128]],
                                            compare_op=ALU.is_equal, fill=0.0, base=0, channel_multiplier=1),
          w=[ident_u])
    for t_, d_ in ((decq, decq_d), (deck, deck_d), (decf, decf_d), (rowmask, rowmask_d), (flag, flag_d)):
        kb.dma("sp", t_[:], d_, w=[tab_u])
    for t_, d_ in ((amask, amask_d), (smask, smask_d), (mretp, mretp_d), (mrets, mrets_d)):
        kb.dma("pool", t_[:], d_, w=[tab_u])
    for j in range(NT):
        kb.dma("sp", x[:, j, :], xin[j * 128:(j + 1) * 128, :], w=[x_u[j]])
    kb.op("dve", lambda e: e.memset(aTs[:], 0.0), w=[aTs_u])
    kb.op("dve", lambda e: e.memset(epsb[:], EPS), w=[tab_u])
    kb.op("dve", lambda e: e.memset(h2Ts[:], 0.0), w=[h2Ts_u])
    ones_t, ones_u = t32.next()
    kb.op("dve", lambda e: e.memset(ones_t[:], 1.0), w=[ones_u])
    kb.dma("sp", acc[0][2048:2176, 0:512], ones_t[:], r=[ones_u], wa=[acc_u[0]], tag="init")
    kb.dma("sp", acc[0][2048:2176, 8:520], ones_t[:], r=[ones_u], wa=[acc_u[0]], tag="init")

    def rmsnorm(xt_ap, xu, wt, wu, out_ap, out_kw, np_=128):
        st, su = sm.next()
        kb.op("act", lambda e: e.activation(out=junk[0:np_, :], in_=xt_ap, func=AF.Square,
                                            accum_out=st[0:np_, 0:1]), r=[xu], w=[junk_u, su])
        kb.op("act", lambda e: e.activation(out=st[0:np_, 1:2], in_=st[0:np_, 0:1], func=AF.Sqrt, scale=1.0 / D,
                                            bias=epsb[0:np_, 0:1]), r=[su, tab_u], w=[su])
        kb.op("dve", lambda e: e.reciprocal(out=st[0:np_, 2:3], in_=st[0:np_, 1:2]), r=[su], w=[su])
        kb.op("dve", lambda e: e.scalar_tensor_tensor(out=out_ap, in0=xt_ap, scalar=st[0:np_, 2:3],
                                                      in1=wt[0:np_, :], op0=ALU.mult, op1=ALU.mult),
              r=[xu, su, wu], **out_kw)

    def transpose8(src, src_u, dst_ap, dst_kw, ncols=128, eng="act"):
        pt, ptu = next_pt()

        def f(e):
            ins = None
            for k in range(8):
                ins = e.transpose(out=pt[:, k * 128:(k + 1) * 128], in_=src[:, k * 128:(k + 1) * 128],
                                  identity=ident[:])
            return ins
        kb.op("pe", f, r=[src_u, ident_u], w=[ptu])
        pv_ = pt[:, :].rearrange("p (k t) -> p k t", k=8)[:, :, 0:ncols]
        if eng == "act":
            kb.op("act", lambda e: e.copy(out=dst_ap, in_=pv_), r=[ptu], **dst_kw)
        else:
            kb.op("dve", lambda e: e.tensor_copy(out=dst_ap, in_=pv_), r=[ptu], **dst_kw)

    def rope_apply(zt, zu, j, half, H, out_ap, out_u):
        rt, ru = ropet.next()
        kb.dma("sp", rt[:], rope_d[j], w=[ru])
        c0 = 0 if half == 64 else 128
        cos = rt[:, c0:c0 + half]
        sin = rt[:, c0 + half:c0 + 2 * half]
        zv = zt.rearrange("p (h two d) -> p h two d", h=H, two=2)
        ov = out_ap.rearrange("p (h two d) -> p h two d", h=H, two=2)
        t1, t1u = t32.next()
        t1v = t1[:, :].rearrange("p (h two d) -> p h two d", h=H, two=2)
        m, mu = t32.next()
        mv = m[:, :].rearrange("p (two h d) -> p two h d", two=2, h=H)
        cosb = cos.unsqueeze(1).unsqueeze(1).broadcast_to([128, H, 2, half])
        sinb = sin.unsqueeze(1).broadcast_to([128, H, half])
        kb.op("dve", lambda e: e.tensor_tensor(out=t1v, in0=zv, in1=cosb, op=ALU.mult), r=[zu, ru], w=[t1u])
        kb.op("dve", lambda e: e.tensor_tensor(out=mv[:, 0], in0=zv[:, :, 1, :], in1=sinb, op=ALU.mult),
              r=[zu, ru], wa=[mu], tag="m")
        kb.op("dve", lambda e: e.tensor_tensor(out=mv[:, 1], in0=zv[:, :, 0, :], in1=sinb, op=ALU.mult),
              r=[zu, ru], wa=[mu], tag="m")
        kb.op("dve", lambda e: e.tensor_tensor(out=ov[:, :, 0, :], in0=t1v[:, :, 0, :], in1=mv[:, 0],
                                               op=ALU.subtract), r=[t1u, mu], wa=[out_u], tag="o")
        kb.op("dve", lambda e: e.tensor_tensor(out=ov[:, :, 1, :], in0=t1v[:, :, 1, :], in1=mv[:, 1],
                                               op=ALU.add), r=[t1u, mu], wa=[out_u], tag="o")

    def allgather(src, src_u, dst, dst_u):
        evs = kb._collect([src_u], [dst_u], (), None)
        kb._wait("pool", evs)
        cc_cnt[0] += 1
        nc.gpsimd.collective_compute("AllGather", ALU.bypass, replica_groups=[[0, 1], [2, 3], [4, 5], [6, 7]],
                                     ins=[src], outs=[dst]).then_inc(ccsem)
        kb._commit((ccsem, cc_cnt[0]), [src_u], [dst_u], (), None)

    ginK = gK.ap()
    ginV = gV.ap()
    ginS = gS.ap()
    goutK = rK.ap()
    goutV = rV.ap()
    goutS = gSo.ap()[0:128, :]

    for l in range(nl):
        alias([wd_u], arena_units)
        alias([aT_u, xh_u], hT_u)
        kb.dma("sp", nwb[0][:], n1b[l], w=[nwb_u[0]])
        kb.dma("sp", gnw, gnb[l], w=[gnw_u])
        kb.dma("sp", cwt[:], convT[l], w=[cwt_u])
        kb.dma("sp", ctx8[:], sconv_in[l], w=[ctx8_u])
        kb.op("dve", lambda e: e.memset(qz, 0.0), w=[qz_u])
        for i in range(len(v65.t)):
            kb.op("pool", lambda e, i=i: e.memset(v65.t[i], 1.0), w=[v65.u[i]])

        if stop == 0:
            kb.finish()
            return nc
        for j in range(NT):
            h_t, h_u = hb.next()
            rmsnorm(x[:, j, :], x_u[j], nwb[0], nwb_u[0], h_t[:], dict(w=[h_u]))
            transpose8(h_t, h_u, hT[:, :, j * 128:(j + 1) * 128], dict(w=[hT_u[j]]))

        if stop == 1:
            kb.finish()
            return nc
        w_in_v = w_in[l].rearrange("(k p) n -> p k n", p=128)
        def load_win(g):
            wt, wu = wbuf.next()
            for k2 in range(4):
                kb.dma("pool", wt[:, 2 * k2:2 * k2 + 2, :], w_in_v[:, 2 * k2:2 * k2 + 2, g * 512:(g + 1) * 512],
                       wa=[wu], tag="ld%d_%d" % (l, g))
            return wt, wu
        win_next = load_win(0)
        for g in range(7):
            wt, wu = win_next
            if g + 1 < 7:
                win_next = load_win(g + 1)
            for j in range(NT):
                ps, psu = pab.next()

                def f(e, ps=ps, wt=wt, j=j):
                    ins = None
                    for k in range(8):
                        ins = e.matmul(ps, lhsT=hT[:, k, j * 128:(j + 1) * 128], rhs=wt[:, k, :],
                                       start=(k == 0), stop=(k == 7))
                    return ins
                kb.op("pe", f, r=[hT_u[j], wu], w=[psu])
                rows = slice(j * 128, (j + 1) * 128)
                if g in (0, 1, 4):
                    o, ou = tb.next()
                    rope_apply(ps, psu, j, 64 if g < 4 else 32, 4 if g < 4 else 8, o[:, :], ou)
                    kb.dma("pool", zs[g][rows, :], o[:], r=[ou], wa=[zs_u[g]], tag="z%d" % l)
                elif g in (2, 3):
                    o, ou = tb.next()
                    kb.op("act", lambda e, o=o, ps=ps: e.copy(out=o[:], in_=ps), r=[psu], w=[ou])
                    kb.dma("pool", zs[g][rows, :], o[:], r=[ou], wa=[zs_u[g]], tag="z%d" % l)
                elif g == 5:
                    o, ou = t32.next()
                    rope_apply(ps, psu, j, 32, 8, o[:, :], ou)
                    if j < NPT:
                        kb.dma("pool", pk_o[l, rows, :], o[:], r=[ou])
                    else:
                        kb.dma("pool", sk_o[l], o[:], r=[ou])
                    ob_, obu = tb.next()
                    kb.op("act", lambda e, ob_=ob_, o=o: e.copy(out=ob_[:], in_=o[:]), r=[ou], w=[obu])
                    kb.dma("pool", ginK[rows, :], ob_[:], r=[obu], wa=[gin_u], tag="g%d" % l)
                else:
                    z, zu = t32.next()
                    kb.op("act", lambda e, z=z, ps=ps: e.copy(out=z[:], in_=ps), r=[psu], w=[zu])
                    if j < NPT:
                        kb.dma("pool", pv_o[l, rows, :], z[:], r=[zu])
                    else:
                        kb.dma("pool", sv_o[l], z[:], r=[zu])
                    vt, vu = v65.next()
                    kb.op("dve", lambda e, vt=vt, z=z: e.tensor_copy(
                        out=vt[:, :, 0:64], in_=z[:, :].rearrange("p (h d) -> p h d", h=8)), r=[zu], w=[vu])
                    kb.dma("pool", ginV[rows, :], vt.rearrange("p h d -> p (h d)"), r=[vu],
                           wa=[gin_u], tag="g%d" % l)

        if stop == 2:
            kb.finish()
            return nc
        for j in range(NPT):
            rows = slice(j * 128, (j + 1) * 128)
            kt, ku = tb.next()
            vt_, vu_ = tb2.next()
            kb.dma("sp", kt[:], zs[1][rows, :], r=[zs_u[1]], w=[ku])
            kb.dma("sp", vt_, zs[2][rows, :], r=[zs_u[2]], w=[vu_])
            kd, kdu = kz.next()
            kb.op("dve", lambda e, kd=kd, kt=kt, j=j: e.tensor_tensor(
                out=kd.rearrange("p (h d) -> p h d", h=4), in0=kt[:, :].rearrange("p (h d) -> p h d", h=4),
                in1=decf[:, j, :].unsqueeze(2).broadcast_to([128, 4, 128]), op=ALU.mult), r=[ku, tab_u], w=[kdu])

            def f(e, kd=kd, vt_=vt_, j=j):
                ins = None
                for h in range(4):
                    ins = e.matmul(PC[:, 0, h * 128:(h + 1) * 128], lhsT=kd[:, h * 128:(h + 1) * 128],
                                   rhs=vt_[:, h * 128:(h + 1) * 128], start=(j == 0), stop=(j == NPT - 1),
                                   skip_group_check=True)
                return ins
            if j == 0:
                kb.op("pe", f, r=[kdu, vu_], w=[PC_u[0]])
            else:
                kb.op("pe", f, r=[kdu, vu_], wa=[PC_u[0]], tag=None)
        sl, slu = tb.next()
        kb.op("act", lambda e: e.copy(out=sl[:], in_=PC[:, 0, :]), r=[PC_u[0]], w=[slu])
        kb.dma("pool", ginS, sl[:], r=[slu], wa=[gin_u], tag="g%d" % l)

        for hh in range(2):
            allgather(gK.ap()[hh * 1088:(hh + 1) * 1088, :], gin_u, gKo[hh].ap(), gout_u)
            allgather(gV.ap()[hh * 1088:(hh + 1) * 1088, :], gin_u, gVo[hh].ap(), gout_u)
        allgather(gS.ap(), gin_u, gSo.ap(), gout_u)
        for hh in range(2):
            kb.dma("pool", rK.ap()[hh * 1088:(hh + 1) * 1088, :], gKo[hh].ap()[0:1088, :], r=[gout_u],
                   wa=[rkv_u], tag="rkv%d" % l)
            kb.dma("pool", rV.ap()[hh * 1088:(hh + 1) * 1088, :], gVo[hh].ap()[0:1088, :], r=[gout_u],
                   wa=[rkv_u], tag="rkv%d" % l)

        if stop == 3:
            kb.finish()
            return nc
        wo = []
        w_out_v = w_out[l].rearrange("(k p) n -> p k n", p=128)
        for hf in range(2):
            wt, wu = wbuf.next()
            for k2 in range(4):
                kb.dma("pool", wt[:, 2 * k2:2 * k2 + 2, :], w_out_v[:, 2 * k2:2 * k2 + 2, hf * 512:(hf + 1) * 512],
                       wa=[wu], tag="wo%d" % l)
            wo.append((wt, wu))

        def load_kv(rows_ap_k, rows_ap_v, srcs_u, cast=False):
            kt, ku = tb2.next()
            q = "pool" if cast else "sp"
            kb.dma(q, kt, rows_ap_k, r=srcs_u, w=[ku])
            vt, vu = v65.next()
            if cast:
                kb.dma(q, vt[:, :, 0:64], rows_ap_v.rearrange("p (h d) -> p h d", h=8), r=srcs_u, w=[vu])
            else:
                kb.dma(q, vt.rearrange("p h d -> p (h d)"), rows_ap_v, r=srcs_u, w=[vu])
            pt, ptu = next_pt()

            def f(e):
                ins = None
                for k in range(4):
                    ins = e.transpose(out=pt[:, k * 128:(k + 1) * 128], in_=kt[:, k * 128:(k + 1) * 128],
                                      identity=ident[:])
                return ins
            kb.op("pe", f, r=[ku, ident_u], w=[ptu])
            kT, kTu = kTs.next()
            kb.op("dve", lambda e: e.tensor_copy(out=kT, in_=pt[:, 0:512].rearrange("p (k t) -> p k t", k=4)),
                  r=[ptu], w=[kTu])
            return (kT, kTu, vt, vu)

        def load_q(rows_ap):
            qt, qu = tb2.next()
            kb.dma("sp", qt, rows_ap, r=[zs_u[4]], w=[qu])
            pt, ptu = next_pt()

            def f(e):
                ins = None
                for k in range(4):
                    ins = e.transpose(out=pt[:, k * 128:(k + 1) * 128], in_=qt[:, k * 128:(k + 1) * 128],
                                      identity=ident[:])
                return ins
            kb.op("pe", f, r=[qu, ident_u], w=[ptu])
            qT, qTu = qTp.next()
            kb.op("dve", lambda e: e.tensor_copy(out=qT, in_=pt[:, 0:512].rearrange("p (k t) -> p k t", k=4)),
                  r=[ptu], w=[qTu])
            return qT, qTu

        def scores_exp(kT, kTu, qT, qTu, ncol, P, P_u, mask_ap, pm_ap, pm_kw, q0=0, meng="pool"):
            def f(e):
                ins = None
                for hp in range(4):
                    for ee in range(2):
                        ins = e.matmul(P[:, ee, hp * ncol:(hp + 1) * ncol],
                                       lhsT=kT[64 * ee:64 * ee + 64, hp, :],
                                       rhs=qT[64 * ee:64 * ee + 64, hp, q0:q0 + ncol], start=True, stop=True,
                                       skip_group_check=True)
                return ins
            kb.op("pe", f, r=[kTu, qTu], w=P_u)
            pev = pm_ap.rearrange("p (e c) -> p e c", e=2)
            for ee in range(2):
                kb.op("act", lambda e, ee=ee: e.activation(out=pev[:, ee, :], in_=P[:, ee, 0:4 * ncol],
                                                           func=AF.Exp, scale=0.125),
                      r=[P_u[ee]], **pm_kw)
            pg = pm_ap.rearrange("p (g c) -> p g c", g=8)
            kb.op(meng, lambda e: e.tensor_tensor(
                out=pg, in0=pg, in1=mask_ap.unsqueeze(1).broadcast_to([128, 8, ncol]), op=ALU.mult),
                r=[tab_u], w=[pm_kw["wa"][0]])

        alias(tb2x.u + catT2.u, pm.u)
        blocks = []
        for pi, dil in enumerate((1, 4, 16)):
            span = 128 * dil
            nblk = 2048 // span
            for r_ in range(dil):
                for n in range(nblk):
                    blocks.append((pi, dil, span, nblk, r_, n))
        st_prev = [None]

        def stage_a(blk):
            pi, dil, span, nblk, r_, n = blk
            r0 = n * span + r_
            rs = slice(r0, r0 + 128 * dil, dil) if dil > 1 else slice(r0, r0 + 128)
            if n == 0:
                p0 = (nblk - 1) * span + r_
                ps_ = slice(p0, p0 + 128 * dil, dil) if dil > 1 else slice(p0, p0 + 128)
                prev = load_kv(goutK[ps_, :], goutV[ps_, :], [rkv_u])
            else:
                prev = st_prev[0]
            cur = load_kv(ginK[rs, :], ginV[rs, :], [gin_u])
            qT, qTu = load_q(zs[4][rs, :])
            pmt, pmu = pm.next()
            tg = "pm%d_%d_%d_%d" % (l, pi, r_, n)
            scores_exp(prev[0], prev[1], qT, qTu, 128, PA, PA_u, amask[:, 2 if n == 0 else 1, :],
                       pmt[:, 0, :], dict(wa=[pmu], tag=tg), meng="dve")
            scores_exp(cur[0], cur[1], qT, qTu, 128, PB, PB_u, amask[:, 0, :],
                       pmt[:, 1, :], dict(wa=[pmu], tag=tg), meng="pool")
            st_prev[0] = cur
            return (pi, rs, prev, cur, pmt, pmu)

        def stage_b(st):
            pi, rs, prev, cur, pmt, pmu = st
            pmv = pmt.rearrange("p t (g c) -> p t g c", g=8)

            def f(e):
                ins = None
                for hp in range(4):
                    for ee in range(2):
                        h = 2 * hp + ee
                        g_ = ee * 4 + hp
                        o = PC[:, ee, hp * 65:(hp + 1) * 65]
                        e.matmul(o, lhsT=pmv[:, 0, g_, :], rhs=prev[2][:, h, :], start=True, stop=False)
                        ins = e.matmul(o, lhsT=pmv[:, 1, g_, :], rhs=cur[2][:, h, :], start=False, stop=True)
                return ins
            kb.op("pe", f, r=[pmu, prev[3], cur[3]], w=PC_u)
            o_, ou_ = ob.next()
            kb.op("dve", lambda e: e.tensor_copy(out=o_.rearrange("p (e c) -> p e c", e=2), in_=PC[:, :, 0:260]),
                  r=PC_u, w=[ou_])
            kb.dma("pool", acc[pi][rs, :], o_, r=[ou_], wa=[acc_u[pi]], tag="acc%d" % l)

        pend = stage_a(blocks[0])
        for bi in range(len(blocks)):
            nxt = stage_a(blocks[bi + 1]) if bi + 1 < len(blocks) else None
            stage_b(pend)
            pend = nxt

        if stop == 4:
            kb.finish()
            return nc
        sq_rows = slice(2048, 2176)
        qTs, qTsu = load_q(zs[4][sq_rows, :])
        for b in range(4):
            specs = [(slice(1920, 2048), 0)]
            specs += [(slice(1536 + r_, 2048, 4), 1 + r_) for r_ in range(4)]
            specs += [(slice(r_, 2048, 16), 1 + r_) for r_ in range(4)]
            specs += [(None, 5 + b)]
            for ti, (rsl, mi) in enumerate(specs):
                if rsl is None:
                    kv = load_kv(ginK[sq_rows, :], ginV[sq_rows, :], [gin_u])
                else:
                    kv = load_kv(ck[l, b, rsl, :], cv[l, b, rsl, :], [], cast=True)
                pe_, peu = pes.next()
                scores_exp(kv[0], kv[1], qTs, qTsu, 4, PA, PA_u, smask[:, mi, :],
                           pe_[:, :], dict(wa=[peu], tag="s%d_%d_%d" % (l, b, ti)), q0=srow(b, 0), meng="dve")

                def f(e, kv=kv, pe_=pe_):
                    ins = None
                    for hp in range(4):
                        for ee in range(2):
                            h = 2 * hp + ee
                            g_ = ee * 4 + hp
                            ins = e.matmul(PC[0:4, ee, hp * 65:(hp + 1) * 65], lhsT=pe_[:, g_ * 4:(g_ + 1) * 4],
                                           rhs=kv[2][:, h, :], start=True, stop=True, skip_group_check=True)
                    return ins
                kb.op("pe", f, r=[peu, kv[3]], w=PC_u)
                sv_ = sacc[0:4, :].rearrange("p (e c) -> p e c", e=2)
                if ti == 0:
                    kb.op("dve", lambda e, sv_=sv_: e.tensor_copy(out=sv_, in_=PC[0:4, :, 0:260]), r=PC_u, w=[sacc_u])
                else:
                    kb.op("dve", lambda e, sv_=sv_: e.tensor_tensor(out=sv_, in0=PC[0:4, :, 0:260], in1=sv_, op=ALU.add),
                          r=PC_u, w=[sacc_u])
            kb.dma("pool", acc[0][2048 + srow(b, 0):2048 + srow(b, 0) + 4, :], sacc[0:4, :], r=[sacc_u],
                   wa=[acc_u[0]], tag="acc%d" % l)

        if stop == 5:
            kb.finish()
            return nc
        s0t, s0u = tb.next()
        kb.dma("sp", s0t[:], goutS, r=[gout_u], w=[s0u])
        kb.op("dve", lambda e: e.tensor_scalar(out=Sst.rearrange("p h d -> p (h d)"), in0=s0t[:],
                                               scalar1=flag[:, 0:1], scalar2=None, op0=ALU.mult),
              r=[s0u, tab_u], w=[Sst_u])
        kb.op("act", lambda e: e.copy(out=Sb, in_=Sst), r=[Sst_u], w=[Sb_u])

        alias(pm.u, tb2x.u + catT2.u)

        def r2_a(j, tset):
            samp = (j == NPT)
            rows = slice(j * 128, (j + 1) * 128)
            qt, qu = tset.next()
            kt, ku = tset.next()
            vt_, vu_ = tset.next()
            gt, gu = tset.next()
            kb.dma("sp", qt, zs[0][rows, :], r=[zs_u[0]], w=[qu])
            kb.dma("sp", kt, zs[1][rows, :], r=[zs_u[1]], w=[ku])
            kb.dma("sp", vt_, zs[2][rows, :], r=[zs_u[2]], w=[vu_])
            kb.dma("sp", gt, zs[3][rows, :], r=[zs_u[3]], w=[gu])
            qk, qku = hb.next()
            kb.op("dve", lambda e, qk=qk, qt=qt, j=j: e.tensor_tensor(
                out=qk[:, 0:512].rearrange("p (h d) -> p h d", h=4), in0=qt.rearrange("p (h d) -> p h d", h=4),
                in1=decq[:, j, :].unsqueeze(2).broadcast_to([128, 4, 128]), op=ALU.mult),
                r=[qu, tab_u], wa=[qku], tag="qk%d_%d" % (l, j))
            kb.op("pool", lambda e, qk=qk, kt=kt: e.tensor_copy(out=qk[:, 512:1024], in_=kt), r=[ku],
                  wa=[qku], tag="qk%d_%d" % (l, j))
            kd, kdu = kz.next()
            kb.op("pool", lambda e, kd=kd, kt=kt, j=j: e.tensor_tensor(
                out=kd.rearrange("p (h d) -> p h d", h=4), in0=kt.rearrange("p (h d) -> p h d", h=4),
                in1=deck[:, j, :].unsqueeze(2).broadcast_to([128, 4, 128]), op=ALU.mult), r=[ku, tab_u], w=[kdu])
            qT, qTu = qkT.next()
            transpose8(qk, qku, qT, dict(w=[qTu]), eng="act")
            ps, psu = pab.next()

            def f(e, ps=ps, qT=qT):
                ins = None
                for h in range(4):
                    ins = e.matmul(ps[:, h * 128:(h + 1) * 128], lhsT=qT[:, 4 + h, :], rhs=qT[:, h, :],
                                   start=True, stop=True, skip_group_check=True)
                return ins
            kb.op("pe", f, r=[qTu], w=[psu])
            pr, pru = tb.next()
            mk = mrets if samp else mretp
            kb.op("dve", lambda e, pr=pr, ps=ps, mk=mk: e.tensor_tensor(
                out=pr[:, :], in0=ps, in1=mk[:, :, :].rearrange("p h i -> p (h i)"), op=ALU.mult),
                r=[psu, tab_u], w=[pru])
            if samp:
                for b in range(4):
                    kb.op("dve", lambda e, b=b, qT=qT: e.tensor_copy(
                        out=qz[:, b, :, srow(b, 0):srow(b, 0) + 4], in_=qT[:, 0:4, srow(b, 0):srow(b, 0) + 4]),
                        r=[qTu], w=[qz_u])
            return dict(j=j, samp=samp, rows=rows, vt_=vt_, vu_=vu_, gt=gt, gu=gu, kd=kd, kdu=kdu, qT=qT, qTu=qTu,
                        pr=pr, pru=pru)

        def r2_b(c):
            j, samp, rows, vt_, vu_, gt, gu = c["j"], c["samp"], c["rows"], c["vt_"], c["vu_"], c["gt"], c["gu"]
            kd, kdu, qT, qTu, pr, pru = c["kd"], c["kdu"], c["qT"], c["qTu"], c["pr"], c["pru"]
            def state_update(kd=kd, kdu=kdu, vt_=vt_, vu_=vu_, j=j, samp=samp):
                if not samp:
                    def f(e, kd=kd, vt_=vt_):
                        ins = None
                        for h in range(4):
                            ins = e.matmul(PC[:, 0, h * 128:(h + 1) * 128], lhsT=kd[:, h * 128:(h + 1) * 128],
                                           rhs=vt_[:, h * 128:(h + 1) * 128], start=True, stop=True, skip_group_check=True)
                        return ins
                    kb.op("pe", f, r=[kdu, vu_], w=[PC_u[0]])
                    for h in range(4):
                        kb.op("dve", lambda e, h=h: e.scalar_tensor_tensor(
                            out=Sst[:, h, :], in0=Sst[:, h, :], scalar=float(GAM[h] ** 128),
                            in1=PC[:, 0, h * 128:(h + 1) * 128], op0=ALU.mult, op1=ALU.add), r=[PC_u[0]], w=[Sst_u])
                    kb.op("act", lambda e: e.copy(out=Sb, in_=Sst), r=[Sst_u], w=[Sb_u])
                    if j == NPT - 1:
                        kb.dma("pool", pret_o[l], Sst, r=[Sst_u])
                else:
                    for b in range(4):
                        kzb_, kzu = hb.next()
                        kzb = kzb_[:, 0:512]
                        kb.op("dve", lambda e, kzb=kzb, kd=kd, b=b: e.tensor_scalar(
                            out=kzb, in0=kd, scalar1=rowmask[:, b:b + 1], scalar2=None, op0=ALU.mult),
                            r=[kdu, tab_u], w=[kzu])

                        def f(e, kzb=kzb, vt_=vt_):
                            ins = None
                            for h in range(4):
                                ins = e.matmul(PC[:, 0, h * 128:(h + 1) * 128], lhsT=kzb[:, h * 128:(h + 1) * 128],
                                               rhs=vt_[:, h * 128:(h + 1) * 128], start=True, stop=True,
                                               skip_group_check=True)
                            return ins
                        kb.op("pe", f, r=[kzu, vu_], w=[PC_u[0]])
                        s0f, s0fu = t32.next()
                        kb.dma("sp", s0f[:, :].rearrange("p (h e) -> p h e", h=4),
                               sret_in[l, b].rearrange("h d e -> d h e"), w=[s0fu])
                        for h in range(4):
                            kb.op("dve", lambda e, h=h, s0f=s0f: e.scalar_tensor_tensor(
                                out=s0f[:, h * 128:(h + 1) * 128], in0=s0f[:, h * 128:(h + 1) * 128],
                                scalar=float(GAM[h] ** 4), in1=PC[:, 0, h * 128:(h + 1) * 128], op0=ALU.mult, op1=ALU.add),
                                r=[PC_u[0]], w=[s0fu])
                        kb.dma("pool", sret_o[l, b], s0f[:, :].rearrange("p (h e) -> p h e", h=4), r=[s0fu])

            if samp:
                state_update()
            po, pou = pab.next()
            if not samp:
                def f(e, po=po, pr=pr, vt_=vt_, qT=qT):
                    ins = None
                    for h in range(4):
                        o = po[:, h * 128:(h + 1) * 128]
                        e.matmul(o, lhsT=pr[:, h * 128:(h + 1) * 128], rhs=vt_[:, h * 128:(h + 1) * 128],
                                 start=True, stop=False)
                        ins = e.matmul(o, lhsT=qT[:, h, :], rhs=Sb[:, h, :], start=False, stop=True)
                    return ins
                kb.op("pe", f, r=[pru, vu_, qTu, Sb_u], w=[pou])
            else:
                def f(e, po=po, pr=pr, vt_=vt_):
                    ins = None
                    for h in range(4):
                        ins = e.matmul(po[:, h * 128:(h + 1) * 128], lhsT=pr[:, h * 128:(h + 1) * 128],
                                       rhs=vt_[:, h * 128:(h + 1) * 128], start=True, stop=True, skip_group_check=True)
                    return ins
                kb.op("pe", f, r=[pru, vu_], w=[pou])
                oacc, oaccu = t32.next()
                kb.op("act", lambda e, oacc=oacc, po=po: e.copy(out=oacc[:], in_=po), r=[pou], w=[oaccu])
                for b in range(4):
                    sbt, sbu = s0bp.next()
                    kb.dma("pool", sbt, sret_in[l, b].rearrange("h d e -> d h e"), w=[sbu])
                    pq, pqu = pab.next()

                    def f(e, pq=pq, b=b, sbt=sbt):
                        ins = None
                        for h in range(4):
                            ins = e.matmul(pq[0:24, h * 128:(h + 1) * 128], lhsT=qz[:, b, h, :], rhs=sbt[:, h, :],
                                           start=True, stop=True, skip_group_check=True)
                        return ins
                    kb.op("pe", f, r=[qz_u, sbu], w=[pqu])
                    kb.op("dve", lambda e, pq=pq, oacc=oacc: e.tensor_tensor(out=oacc[0:24, :], in0=pq[0:24, :],
                                                                             in1=oacc[0:24, :], op=ALU.add),
                          r=[pqu], w=[oaccu])
            if not samp:
                state_update()
            if samp:
                osb, osu = oacc, oaccu
            else:
                osb, osu = t32.next()
                kb.op("act", lambda e, osb=osb, po=po: e.copy(out=osb[:], in_=po), r=[pou], w=[osu])
            st, su = sm.next()
            for h in range(4):
                kb.op("dve", lambda e, h=h, st=st, osb=osb: e.bn_stats(out=st[:, 6 * h:6 * h + 6],
                                                                        in_=osb[:, h * 128:(h + 1) * 128]),
                      r=[osu], w=[su])
            st2, su2 = sm.next()
            for h in range(4):
                kb.op("dve", lambda e, h=h, st=st, st2=st2: e.bn_aggr(out=st2[:, 2 * h:2 * h + 2],
                                                                      in_=st[:, 6 * h:6 * h + 6]), r=[su], w=[su2])
            s2v = st2[:, 0:8].rearrange("p (h two) -> p h two", two=2)
            kb.op("act", lambda e, st2=st2, s2v=s2v: e.activation(out=st2[:, 8:12], in_=s2v[:, :, 1], func=AF.Sqrt,
                                                                  bias=epsb[:, 0:1]), r=[su2, tab_u], w=[su2])
            kb.op("dve", lambda e, st2=st2: e.reciprocal(out=st2[:, 8:12], in_=st2[:, 8:12]), r=[su2], w=[su2])
            for h in range(4):
                kb.op("dve", lambda e, h=h, osb=osb, st2=st2: e.tensor_scalar(
                    out=osb[:, h * 128:(h + 1) * 128], in0=osb[:, h * 128:(h + 1) * 128],
                    scalar1=st2[:, 2 * h:2 * h + 1], scalar2=st2[:, 8 + h:9 + h], op0=ALU.subtract, op1=ALU.mult),
                    r=[su2], w=[osu])
            sg, sgu = t32.next()
            kb.op("act", lambda e, sg=sg, gt=gt: e.activation(out=sg[:], in_=gt, func=AF.Silu), r=[gu], w=[sgu])
            kb.op("pool", lambda e, osb=osb: e.tensor_tensor(out=osb[:], in0=osb[:], in1=gnw, op=ALU.mult),
                  r=[gnw_u], w=[osu])
            ct, cu = catb.next()
            tgc = "c%d_%d" % (l, j)
            kb.op("pool", lambda e, ct=ct, osb=osb, sg=sg: e.tensor_tensor(out=ct[:, 0:512], in0=osb[:], in1=sg[:],
                                                                          op=ALU.mult), r=[osu, sgu], wa=[cu], tag=tgc)
            a0, a0u = accl.t[0], accl.u[0]
            kb.dma("sp", a0, acc[0][rows, :], r=[acc_u[0]], w=[a0u])
            if not samp:
                a1, a1u = accl.t[1], accl.u[1]
                kb.dma("sp", a1, acc[1][rows, :], r=[acc_u[1]], w=[a1u])
                kb.op("dve", lambda e, a0=a0, a1=a1: e.tensor_tensor(out=a0, in0=a0, in1=a1, op=ALU.add),
                      r=[a1u], w=[a0u])
                a2, a2u = a1, a1u
                kb.dma("sp", a2, acc[2][rows, :], r=[acc_u[2]], w=[a2u])
                kb.op("dve", lambda e, a0=a0, a2=a2: e.tensor_tensor(out=a0, in0=a0, in1=a2, op=ALU.add),
                      r=[a2u], w=[a0u])
            a0v = a0.rearrange("p (e hp c) -> p e hp c", e=2, hp=4)
            rd, rdu = sm.next()
            kb.op("dve", lambda e, rd=rd, a0v=a0v: e.reciprocal(
                out=rd[:, 0:8].rearrange("p (e hp) -> p e hp", e=2), in_=a0v[:, :, :, 64]), r=[a0u], w=[rdu])
            kb.op("dve", lambda e, ct=ct, a0v=a0v, rd=rd: e.tensor_tensor(
                out=ct[:, 512:1024].rearrange("p (hp e d) -> p e hp d", hp=4, e=2), in0=a0v[:, :, :, 0:64],
                in1=rd[:, 0:8].rearrange("p (e hp) -> p e hp", e=2).unsqueeze(3).broadcast_to([128, 2, 4, 64]),
                op=ALU.mult), r=[a0u, rdu], wa=[cu], tag=tgc)
            if dbg and l == 0:
                kb.dma("pool", dbg_cat[rows, :], ct[:], r=[cu])
                kb.dma("pool", dbg_acc[rows, :], a0, r=[a0u])
            cT, cTu = catT2.next()
            transpose8(ct, cu, cT, dict(w=[cTu]), eng="act")
            for hf in range(2):
                ps, psu = pab.next()

                def f(e, ps=ps, cT=cT, hf=hf):
                    ins = None
                    for k in range(8):
                        ins = e.matmul(ps, lhsT=cT[:, k, :], rhs=wo[hf][0][:, k, :], start=(k == 0), stop=(k == 7))
                    return ins
                kb.op("pe", f, r=[cTu, wo[hf][1]], w=[psu])
                kb.op("dve", lambda e, ps=ps, j=j, hf=hf: e.tensor_tensor(
                    out=x[:, j, hf * 512:(hf + 1) * 512], in0=ps, in1=x[:, j, hf * 512:(hf + 1) * 512], op=ALU.add),
                    r=[psu], w=[x_u[j]])

        if stop == 6:
            kb.finish()
            return nc
        tsets = [tb2, tb2x]
        pend = r2_a(0, tsets[0])
        for j in range(NT):
            nxt = r2_a(j + 1, tsets[(j + 1) % 2]) if j + 1 < NT else None
            r2_b(pend)
            pend = nxt
        if dbg and l == 0:
            for j in range(NT):
                kb.dma("pool", dbg_x[j * 128:(j + 1) * 128, :], x[:, j, :], r=[x_u[j]])
        kb.dma("pool", gin2.ap(), x[126:128, NPT - 1, :], r=[x_u[NPT - 1]], w=[gin2_u])
        allgather(gin2.ap(), gin2_u, gout2.ap(), gout2_u)
        alias(hT_u, [xh_u])
        kb.op("dve", lambda e: e.memset(xh, 0.0), w=[xh_u])
        kb.dma("sp", xh[0:2, :], gout2.ap()[0:2, :], r=[gout2_u], w=[xh_u])
        kb.op("dve", lambda e: e.tensor_scalar(out=xh[0:2, :], in0=xh[0:2, :], scalar1=flag[0:2, 0:1], scalar2=None,
                                               op0=ALU.mult), r=[tab_u], w=[xh_u])
        kb.dma("sp", nwb[0][:], n2b[l], w=[nwb_u[0]])
        h_t, h_u = hb.next()
        rmsnorm(xh, xh_u, nwb[1], nwb_u[1], h_t[:], dict(w=[h_u]))
        transpose8(h_t, h_u, hTh[:, :, :], dict(w=[hTh_u]), ncols=2)

        if stop == 7:
            kb.finish()
            return nc
        alias(hT_u + [xh_u], [aT_u])
        alias(arena_units, [wd_u])
        w_down_v = w_down[l].rearrange("(c p) n -> p c n", p=128)
        for c0 in range(0, 22, 2):
            kb.dma("pool", wd[:, c0:c0 + 2, :], w_down_v[:, c0:c0 + 2, :], wa=[wd_u], tag="wd%d" % l)
        w_up_v = w_up[l].rearrange("(k p) n -> p k n", p=128)
        ffn_seq = [(gi_, f2_) for gi_ in range(len(FFN_GROUPS)) for f2_ in range(11)]
        ffn_loaded = {}

        def load_wup(idx):
            if idx >= len(ffn_seq):
                return
            gi_, f2_ = ffn_seq[idx]
            wt_, wu_ = wbuf.next()
            tgw_ = "u%d_%d_%d" % (l, gi_, f2_)
            c_ = f2_ * 256
            for k2 in range(2):
                kb.dma("pool", wt_[:, 4 * k2:4 * k2 + 4, 0:256], w_up_v[:, 4 * k2:4 * k2 + 4, c_:c_ + 256],
                       wa=[wu_], tag=tgw_)
                kb.dma("pool", wt_[:, 4 * k2:4 * k2 + 4, 256:512], w_up_v[:, 4 * k2:4 * k2 + 4, DFF + c_:DFF + c_ + 256],
                       wa=[wu_], tag=tgw_)
            ffn_loaded[idx] = (wt_, wu_)
        load_wup(0)
        for gi, (t0, t1) in enumerate(FFN_GROUPS):
            ntok = (t1 - t0) * 128
            last = (gi == len(FFN_GROUPS) - 1)
            tiles = list(range(t0, t1)) + ([NPT] if last else [])
            for j in tiles:
                h_t, h_u = hb.next()
                rmsnorm(x[:, j, :], x_u[j], nwb[1], nwb_u[1], h_t[:], dict(w=[h_u]))
                if j < NPT:
                    transpose8(h_t, h_u, h2T[:, :, (j - t0) * 128:(j - t0 + 1) * 128],
                               dict(wa=[h2T_u], tag="h2T%d_%d" % (l, gi)), eng="dve")
                else:
                    transpose8(h_t, h_u, h2Ts[:, :, :], dict(w=[h2Ts_u]), ncols=24, eng="dve")
                    kb.op("dve", lambda e: e.memset(h2Ts[:, :, :].rearrange("p k (b s) -> p k b s", b=4)[:, :, :, 0:2],
                                                    0.0), w=[h2Ts_u])
            wins = []
            c = 0
            while c < ntok:
                n_ = min(512, ntok - c)
                wins.append((c, n_))
                c += n_
            for fc in range(22):
                if fc % 2 == 0:
                    wt2, wu = ffn_loaded.pop(gi * 11 + fc // 2)
                    load_wup(gi * 11 + fc // 2 + 1)
                o_ = (fc % 2) * 128
                wt = wt2[:, :, :].rearrange("p k (s c) -> p k s c", s=2)[:, :, :, o_:o_ + 128]
                cs = (fc, 22 + fc)
                if gi == 0:
                    def f(e, wt=wt):
                        ins = None
                        for s_ in range(2):
                            for k in range(8):
                                ins = e.matmul(PC[:, 1, 2 * s_:2 * s_ + 2], lhsT=wt[:, k, s_, :],
                                               rhs=hTh[:, k, :], start=(k == 0), stop=(k == 7), skip_group_check=True)
                        return ins
                    kb.op("pe", f, r=[wu, hTh_u], w=[PC_u[1]])
                    for s_ in range(2):
                        kb.op("act", lambda e, s_=s_, cs=cs: e.copy(out=carry[:, cs[s_], :],
                                                                    in_=PC[:, 1, 2 * s_:2 * s_ + 2]),
                              r=[PC_u[1]], w=[carry_u])
                for (c0, n_) in wins:
                    ug, ugu = pab.next()
                    uv, uvu = pab.next()
                    for s_, (pp, ppu) in enumerate(((ug, ugu), (uv, uvu))):
                        def f(e, pp=pp, s_=s_, wt=wt, c0=c0, n_=n_):
                            ins = None
                            for k in range(8):
                                ins = e.matmul(pp[:, 0:n_], lhsT=wt[:, k, s_, :],
                                               rhs=h2T[:, k, c0:c0 + n_], start=(k == 0), stop=(k == 7))
                            return ins
                        kb.op("pe", f, r=[wu, h2T_u], w=[ppu])
                    cb = []
                    for s_, (pp, ppu) in enumerate(((ug, ugu), (uv, uvu))):
                        ch = cs[s_]
                        A, Au = t32.next()
                        kb.op("act", lambda e, A=A, pp=pp, ch=ch, n_=n_: e.activation(
                            out=A[:, 0:n_], in_=pp[:, 0:n_], func=AF.Identity, scale=cwt[:, ch, 2:3],
                            bias=cwt[:, ch, 3:4]), r=[ppu, cwt_u], w=[Au])
                        cb.append((A, Au))
                    for s_, (pp, ppu) in enumerate(((ug, ugu), (uv, uvu))):
                        ch = cs[s_]
                        A, Au = cb[s_]
                        kb.op("dve", lambda e, A=A, pp=pp, ch=ch, n_=n_: e.scalar_tensor_tensor(
                            out=A[:, 1:n_], in0=pp[:, 0:n_ - 1], scalar=cwt[:, ch, 1:2], in1=A[:, 1:n_],
                            op0=ALU.mult, op1=ALU.add), r=[ppu, cwt_u], w=[Au])
                        kb.op("dve", lambda e, A=A, ch=ch: e.scalar_tensor_tensor(
                            out=A[:, 0:1], in0=carry[:, ch, 1:2], scalar=cwt[:, ch, 1:2], in1=A[:, 0:1],
                            op0=ALU.mult, op1=ALU.add), r=[carry_u, cwt_u], w=[Au])
                        kb.op("dve", lambda e, A=A, pp=pp, ch=ch, n_=n_: e.scalar_tensor_tensor(
                            out=A[:, 2:n_], in0=pp[:, 0:n_ - 2], scalar=cwt[:, ch, 0:1], in1=A[:, 2:n_],
                            op0=ALU.mult, op1=ALU.add), r=[ppu, cwt_u], w=[Au])
                        kb.op("dve", lambda e, A=A, ch=ch: e.scalar_tensor_tensor(
                            out=A[:, 0:2], in0=carry[:, ch, 0:2], scalar=cwt[:, ch, 0:1], in1=A[:, 0:2],
                            op0=ALU.mult, op1=ALU.add), r=[carry_u, cwt_u], w=[Au])
                        kb.op("dve", lambda e, pp=pp, ch=ch, n_=n_: e.tensor_copy(out=carry[:, ch, :], in_=pp[:, n_ - 2:n_]),
                              r=[ppu], w=[carry_u])
                    kb.op("act", lambda e, A=cb[0][0], n_=n_: e.activation(out=A[:, 0:n_], in_=A[:, 0:n_],
                                                                          func=AF.Silu), w=[cb[0][1]])
                    kb.op("pool", lambda e, Ag=cb[0][0], A=cb[1][0], fc=fc, c0=c0, n_=n_: e.tensor_tensor(
                        out=aT[:, fc, c0:c0 + n_], in0=Ag[:, 0:n_], in1=A[:, 0:n_], op=ALU.mult),
                        r=[cb[0][1], cb[1][1]], wa=[aT_u], tag="aT%d_%d" % (l, gi))
                if last:
                    def f(e, wt=wt):
                        ins = None
                        for s_ in range(2):
                            for k in range(8):
                                ins = e.matmul(PC[:, 1, 32 * s_:32 * s_ + 24], lhsT=wt[:, k, s_, :],
                                               rhs=h2Ts[:, k, :], start=(k == 0), stop=(k == 7), skip_group_check=True)
                        return ins
                    kb.op("pe", f, r=[wu, h2Ts_u], w=[PC_u[1]])
                    cb = []
                    for s_ in range(2):
                        ch = cs[s_]
                        us, usu = sm.next()
                        kb.op("dve", lambda e, us=us, s_=s_: e.tensor_copy(out=us[:, 0:24],
                                                                           in_=PC[:, 1, 32 * s_:32 * s_ + 24]),
                              r=[PC_u[1]], w=[usu])
                        kb.op("dve", lambda e, us=us, ch=ch: e.tensor_copy(
                            out=us[:, 0:24].rearrange("p (b s) -> p b s", b=4)[:, :, 0:2], in_=ctx8[:, ch, :, :]),
                            r=[ctx8_u], w=[usu])
                        A, Au = sm.next()
                        kb.op("act", lambda e, A=A, us=us, ch=ch: e.activation(
                            out=A[:, 0:22], in_=us[:, 2:24], func=AF.Identity, scale=cwt[:, ch, 2:3],
                            bias=cwt[:, ch, 3:4]), r=[usu, cwt_u], w=[Au])
                        kb.op("dve", lambda e, A=A, us=us, ch=ch: e.scalar_tensor_tensor(
                            out=A[:, 0:22], in0=us[:, 1:23], scalar=cwt[:, ch, 1:2], in1=A[:, 0:22],
                            op0=ALU.mult, op1=ALU.add), r=[usu, cwt_u], w=[Au])
                        kb.op("dve", lambda e, A=A, us=us, ch=ch: e.scalar_tensor_tensor(
                            out=A[:, 0:22], in0=us[:, 0:22], scalar=cwt[:, ch, 0:1], in1=A[:, 0:22],
                            op0=ALU.mult, op1=ALU.add), r=[usu, cwt_u], w=[Au])
                        kb.op("act", lambda e, us=us, ch=ch: e.copy(
                            out=uout_s[:, ch, :, :], in_=us[:, 0:24].rearrange("p (b s) -> p b s", b=4)[:, :, 4:6]),
                            r=[usu], wa=[uouts_u], tag="uo%d" % l)
                        cb.append((A, Au))
                    kb.op("act", lambda e, A=cb[0][0]: e.activation(out=A[:, 0:22], in_=A[:, 0:22], func=AF.Silu),
                          w=[cb[0][1]])
                    kb.op("pool", lambda e, Ag=cb[0][0], A=cb[1][0], fc=fc: e.tensor_tensor(
                        out=aTs[:, fc, 2:24], in0=Ag[:, 0:22], in1=A[:, 0:22], op=ALU.mult),
                        r=[cb[0][1], cb[1][1]], wa=[aTs_u], tag="aTs%d" % l)
            for j in tiles:
                for hf in range(2):
                    ps, psu = pab.next()
                    if j < NPT:
                        def f(e, ps=ps, j=j, hf=hf):
                            ins = None
                            for fc in range(22):
                                ins = e.matmul(ps, lhsT=aT[:, fc, (j - t0) * 128:(j - t0 + 1) * 128],
                                               rhs=wd[:, fc, hf * 512:(hf + 1) * 512], start=(fc == 0), stop=(fc == 21))
                            return ins
                        kb.op("pe", f, r=[aT_u, wd_u], w=[psu])
                        kb.op("dve", lambda e, ps=ps, j=j, hf=hf: e.tensor_tensor(
                            out=x[:, j, hf * 512:(hf + 1) * 512], in0=ps, in1=x[:, j, hf * 512:(hf + 1) * 512],
                            op=ALU.add), r=[psu], w=[x_u[j]])
                    else:
                        def f(e, ps=ps, hf=hf):
                            ins = None
                            for fc in range(22):
                                ins = e.matmul(ps[0:24, :], lhsT=aTs[:, fc, :], rhs=wd[:, fc, hf * 512:(hf + 1) * 512],
                                               start=(fc == 0), stop=(fc == 21))
                            return ins
                        kb.op("pe", f, r=[aTs_u, wd_u], w=[psu])
                        kb.op("dve", lambda e, ps=ps, j=j, hf=hf: e.tensor_tensor(
                            out=x[0:24, j, hf * 512:(hf + 1) * 512], in0=ps[0:24, :],
                            in1=x[0:24, j, hf * 512:(hf + 1) * 512], op=ALU.add), r=[psu], w=[x_u[j]])
        if stop == 8:
            kb.finish()
            return nc
        kb.dma("pool", pconv_o[l], carry[:], r=[carry_u])
        kb.dma("pool", sconv_o[l], uout_s[:], r=[uouts_u])

    kb.dma("sp", nwb[0][:], fnb, w=[nwb_u[0]])
    for j in range(NT):
        for hf in range(1):
            pass
        yt, yu = t32.next()
        yt2, yu2 = t32.next()
        st, su = sm.next()
        kb.op("act", lambda e, j=j, st=st: e.activation(out=junk[:, :], in_=x[:, j, :], func=AF.Square,
                                                        accum_out=st[:, 0:1]), r=[x_u[j]], w=[junk_u, su])
        kb.op("act", lambda e, st=st: e.activation(out=st[:, 1:2], in_=st[:, 0:1], func=AF.Sqrt, scale=1.0 / D,
                                                   bias=epsb[:, 0:1]), r=[su, tab_u], w=[su])
        kb.op("dve", lambda e, st=st: e.reciprocal(out=st[:, 2:3], in_=st[:, 1:2]), r=[su], w=[su])
        for hf, (yy, yyu) in enumerate(((yt, yu), (yt2, yu2))):
            kb.op("dve", lambda e, j=j, st=st, yy=yy, hf=hf: e.scalar_tensor_tensor(
                out=yy[:], in0=x[:, j, hf * 512:(hf + 1) * 512], scalar=st[:, 2:3],
                in1=nwb[0][:, hf * 512:(hf + 1) * 512], op0=ALU.mult, op1=ALU.mult),
                r=[x_u[j], su, nwb_u[0]], w=[yyu])
            kb.dma("pool", y_o[j * 128:(j + 1) * 128, hf * 512:(hf + 1) * 512], yy[:], r=[yyu])
    kb.finish()
    return nc


def _tables(half):
    f32 = np.float32
    pos = np.zeros((NT, 128), np.float64)
    for j in range(NPT):
        pos[j] = half * 2048 + 128 * j + np.arange(128)
    for b in range(4):
        for t in range(4):
            pos[NPT, srow(b, t)] = PAST + t
    rope = np.zeros((NT, 128, 192), f32)
    inv_r = (10000.0 ** (-np.arange(64, dtype=np.float32) / np.float32(64))).astype(f32)
    inv_a = (10000.0 ** (-np.arange(32, dtype=np.float32) / np.float32(32))).astype(f32)
    ang_r = pos.astype(f32)[:, :, None] * inv_r[None, None, :]
    ang_a = pos.astype(f32)[:, :, None] * inv_a[None, None, :]
    rope[:, :, 0:64] = np.cos(ang_r.astype(np.float64))
    rope[:, :, 64:128] = np.sin(ang_r.astype(np.float64))
    rope[:, :, 128:160] = np.cos(ang_a.astype(np.float64))
    rope[:, :, 160:192] = np.sin(ang_a.astype(np.float64))
    g = np.array(GAM, np.float64)
    sc = 128.0 ** -0.5
    decq = np.ones((128, NT, 4))
    deck = np.ones((128, NT, 4)) * sc
    p = np.arange(128)
    for j in range(NPT):
        decq[:, j, :] = g[None, :] ** (p[:, None] + 1.0)
        deck[:, j, :] = g[None, :] ** (127.0 - p[:, None]) * sc
    for b in range(4):
        for t in range(4):
            decq[srow(b, t), NPT, :] = g ** (t + 1.0)
            deck[srow(b, t), NPT, :] = g ** (3.0 - t) * sc
    decf = np.zeros((128, NPT, 4))
    for j in range(NPT):
        decf[:, j, :] = g[None, :] ** (2047.0 - (128 * j + p[:, None])) * sc
    mretp = np.zeros((128, 4, 128))
    jj, ii = np.meshgrid(p, p, indexing="ij")
    for h in range(4):
        mretp[:, h, :] = (jj <= ii) * (g[h] ** (-(jj + 1.0))) * sc
    mrets = np.zeros((128, 4, 128))
    rowmask = np.zeros((128, 4))
    for b in range(4):
        for tj in range(4):
            rowmask[srow(b, tj), b] = 1.0
            for ti in range(tj, 4):
                for h in range(4):
                    mrets[srow(b, tj), h, srow(b, ti)] = g[h] ** (-(tj + 1.0)) * sc
    amask = np.zeros((128, 3, 128))
    amask[:, 0, :] = (jj <= ii)
    amask[:, 1, :] = (jj >= ii)
    amask[:, 2, :] = (jj >= ii) * float(half)
    smask = np.zeros((128, 9, 4))
    for t in range(4):
        smask[:, 0, t] = (p >= t)
        smask[:, 1 + t, t] = 1.0
    for b in range(4):
        for tp in range(4):
            for t in range(4):
                smask[srow(b, tp), 5 + b, t] = float(tp <= t) + 2.0 * float(tp == t)
    flag = np.full((128, 1), float(half))
    return dict(rope=rope, decq=decq.astype(f32), deck=deck.astype(f32), decf=decf.astype(f32),
                mretp=mretp.astype(f32), mrets=mrets.astype(f32), rowmask=rowmask.astype(f32),
                amask=amask.astype(f32), smask=smask.astype(f32), flag=flag.astype(f32))


_NC_CACHE = {}


def kernel(x_prompt, x_sample, cache_win_k, cache_win_v, state_ret, state_conv,
           norm1_w, w_in, ret_gn_w, w_out, norm2_w, w_up, conv_w, conv_b, w_down, final_norm_w, _nl=NL, _stop=99, _dbg=False):
    f32 = np.float32
    A = lambda a: np.ascontiguousarray(np.asarray(a, dtype=f32))
    x_prompt, x_sample = A(x_prompt), A(x_sample)
    cache_win_k, cache_win_v = np.asarray(cache_win_k, f32), np.asarray(cache_win_v, f32)
    state_ret, state_conv = np.asarray(state_ret, f32), np.asarray(state_conv, f32)
    if (_nl, _stop, _dbg) not in _NC_CACHE:
        _NC_CACHE[(_nl, _stop, _dbg)] = build(_nl, _stop, _dbg)
    nc = _NC_CACHE[(_nl, _stop, _dbg)]
    shared = dict(
        w_in=A(w_in), w_out=A(w_out), w_up=A(w_up), w_down=A(w_down),
        n1b=A(np.broadcast_to(np.asarray(norm1_w, f32)[:, None, :], (NL, 128, D))),
        n2b=A(np.broadcast_to(np.asarray(norm2_w, f32)[:, None, :], (NL, 128, D))),
        fnb=A(np.broadcast_to(np.asarray(final_norm_w, f32)[None, :], (128, D))),
        gnb=A(np.broadcast_to(np.asarray(ret_gn_w, f32)[:, None, :], (NL, 128, 512))),
    )
    cw = np.concatenate([np.asarray(conv_w, f32), np.asarray(conv_b, f32)[:, None, :]], axis=1)
    shared["convT"] = A(cw.reshape(NL, 4, 44, 128).transpose(0, 3, 2, 1))
    tabs = [_tables(0), _tables(1)]
    in_maps = []
    for c in range(8):
        s, half = c // 2, c % 2
        xin = np.zeros((TOK, D), f32)
        xin[0:2048] = x_prompt[s, half * 2048:(half + 1) * 2048]
        for b in range(4):
            xin[2048 + srow(b, 0):2048 + srow(b, 0) + 4] = x_sample[4 * c + b]
        m = dict(shared)
        m["xin"] = xin
        m["ck"] = A(cache_win_k[:, 4 * c:4 * c + 4].reshape(NL, 4, 2048, 512))
        m["cv"] = A(cache_win_v[:, 4 * c:4 * c + 4].reshape(NL, 4, 2048, 512))
        m["sret_in"] = A(state_ret[:, 4 * c:4 * c + 4])
        sc_ = state_conv[:, 4 * c:4 * c + 4]
        m["sconv_in"] = A(sc_.reshape(NL, 4, 2, 44, 128).transpose(0, 4, 3, 1, 2))
        m.update(tabs[half])
        in_maps.append(m)
    res = run_bass_kernel_spmd(nc, in_maps, core_ids=list(range(8)))
    R = res.results
    y_prompt = np.zeros((4, 4096, D), f32)
    y_sample = np.zeros((32, 4, D), f32)
    p_win_k = np.zeros((NL, 4, 2048, 8, 64), f32)
    p_win_v = np.zeros((NL, 4, 2048, 8, 64), f32)
    p_ret = np.zeros((NL, 4, 4, 128, 128), f32)
    p_conv = np.zeros((NL, 4, 2, 2 * DFF), f32)
    s_win_k = np.zeros((NL, 32, 4, 8, 64), f32)
    s_win_v = np.zeros((NL, 32, 4, 8, 64), f32)
    s_ret = np.zeros((NL, 32, 4, 128, 128), f32)
    s_conv = np.zeros((NL, 32, 2, 2 * DFF), f32)
    for c in range(8):
        s, half = c // 2, c % 2
        r = R[c]
        y_prompt[s, half * 2048:(half + 1) * 2048] = r["y"][0:2048]
        for b in range(4):
            rs = slice(2048 + srow(b, 0), 2048 + srow(b, 0) + 4)
            y_sample[4 * c + b] = r["y"][rs]
            ls = slice(srow(b, 0), srow(b, 0) + 4)
            s_win_k[:, 4 * c + b] = r["sk"][:, ls].reshape(NL, 4, 8, 64)
            s_win_v[:, 4 * c + b] = r["sv"][:, ls].reshape(NL, 4, 8, 64)
            s_ret[:, 4 * c + b] = r["sret"][:, b].transpose(0, 2, 1, 3)
        s_conv[:, 4 * c:4 * c + 4] = r["sconvT"].transpose(0, 3, 4, 2, 1).reshape(NL, 4, 2, 2 * DFF)
        if half == 1:
            p_win_k[:, s] = r["pk"].reshape(NL, 2048, 8, 64)
            p_win_v[:, s] = r["pv"].reshape(NL, 2048, 8, 64)
            p_ret[:, s] = r["pret"].transpose(0, 2, 1, 3)
            p_conv[:, s] = r["pconvT"].transpose(0, 3, 2, 1).reshape(NL, 2, 2 * DFF)
    return (y_prompt, y_sample, p_win_k, p_win_v, p_ret, p_conv, s_win_k, s_win_v, s_ret, s_conv)
```
